# Optimizing a Trainium2 kernel written in Bass

```python
import jax
import jax.numpy as jnp
from jax import lax
import numpy as np

D_MODEL = 2048
BATCH = 8
SEQ = 2048
DEPTH = 4
DEC_BATCH = 8
DEC_SEQ = 64
PAST_LEN = 1024

CHUNK = 64
PLE_DIM = 256
POOL_WINDOWS = (2, 4, 8, 16)
POOL_GROUPS = 4
POOL_WIDTH = D_MODEL // 2
POOL_GC = POOL_WIDTH // POOL_GROUPS
POOL_PAST = max(POOL_WINDOWS) - 1
RW_WIDTH = D_MODEL // 2
RW_HEAD = 64
RW_HEADS = RW_WIDTH // RW_HEAD
DECAY_LORA = 64
AAA_LORA = 64
GATE_LORA = 160
RW_PROJ = 3 * RW_WIDTH + DECAY_LORA + AAA_LORA + GATE_LORA
IN_COLS = POOL_WIDTH + RW_PROJ + 2 * D_MODEL
D_FF = ((8 * D_MODEL + 3 * 256 - 1) // (3 * 256)) * 256
NORM_EPS = 1e-6
GN_EPS = 64e-5

kernel_name = 'hybrid_pool_rwkv7_stream_step'


def rmsnorm(x, g):
    xf = x.astype(jnp.float32)
    y = xf * lax.rsqrt(jnp.mean(xf * xf, axis=-1, keepdims=True) + NORM_EPS)
    return (y * g.astype(jnp.float32)).astype(x.dtype)


def pool_mixer(u, past, pos0, pool_w, pool_scale):
    B, T, _ = u.shape
    z = jnp.concatenate([past.astype(u.dtype), u], axis=1)
    zf = z.astype(jnp.float32)
    c = jnp.concatenate([jnp.zeros((B, 1, POOL_WIDTH), jnp.float32), jnp.cumsum(zf, axis=1)], axis=1)
    pos = pos0 + jnp.arange(T, dtype=jnp.int32)
    means = []
    for gi, win in enumerate(POOL_WINDOWS):
        sl = slice(gi * POOL_GC, (gi + 1) * POOL_GC)
        hi = c[:, POOL_PAST + 1:POOL_PAST + 1 + T, sl]
        lo = c[:, POOL_PAST + 1 - win:POOL_PAST + 1 - win + T, sl]
        cnt = jnp.minimum(pos + 1, win).astype(jnp.float32)[None, :, None]
        means.append((hi - lo) / cnt)
    d = jnp.concatenate(means, axis=-1) - zf[:, POOL_PAST:]
    d = d.astype(u.dtype).reshape(B, T, POOL_GROUPS, POOL_GC)
    y = jnp.einsum('btgc,gcd->btgd', d, pool_w).reshape(B, T, POOL_WIDTH) * pool_scale
    return y, z[:, -POOL_PAST:]


def wkv7_scan(r, decay, k, v, kk, b, S0):
    def step(S, xs):
        r_t, w_t, k_t, v_t, kk_t, b_t = xs
        sa = jnp.einsum('bhvk,bhk->bhv', S, -kk_t)
        S = S * w_t[:, :, None, :] + sa[..., None] * b_t[:, :, None, :] + v_t[..., None] * k_t[:, :, None, :]
        y = jnp.einsum('bhvk,bhk->bhv', S, r_t)
        return S, y
    xs = tuple(jnp.moveaxis(a, 1, 0) for a in (r, decay, k, v, kk, b))
    S, ys = lax.scan(step, S0, xs)
    return jnp.moveaxis(ys, 0, 1), S


def rwkv7_mix(rw, shift_prev, S0, mu, w0, w_decay_up, a0, w_aaa_up, w_gate_up, k_k, k_a, r_k, gn_gain, gn_bias):
    B, T, _ = rw.shape
    prev = jnp.concatenate([shift_prev[:, None].astype(rw.dtype), rw[:, :-1]], axis=1)
    xm = rw + (prev - rw) * mu
    r, k, v, wd, ad, gd = jnp.split(
        xm, [RW_WIDTH, 2 * RW_WIDTH, 3 * RW_WIDTH, 3 * RW_WIDTH + DECAY_LORA,
             3 * RW_WIDTH + DECAY_LORA + AAA_LORA], axis=-1)
    w = -jax.nn.softplus(-(w0 + jnp.tanh(wd) @ w_decay_up)) - 0.5
    decay = jnp.exp(-jnp.exp(w.astype(jnp.float32)))
    a = jax.nn.sigmoid(a0 + ad @ w_aaa_up)
    g = jax.nn.sigmoid(gd) @ w_gate_up
    heads = lambda t: t.astype(jnp.float32).reshape(B, T, RW_HEADS, RW_HEAD)
    kk = heads(k * k_k)
    kk = kk / jnp.maximum(jnp.sqrt(jnp.sum(kk * kk, axis=-1, keepdims=True)), 1e-12)
    aH = heads(a)
    kH = heads(k * (1.0 + (a - 1.0) * k_a))
    rH = heads(r)
    vH = heads(v)
    y, S = wkv7_scan(rH, heads(decay), kH, vH, kk, kk * aH, S0.astype(jnp.float32))
    mean = jnp.mean(y, axis=-1, keepdims=True)
    var = jnp.mean(jnp.square(y - mean), axis=-1, keepdims=True)
    y = ((y - mean) * lax.rsqrt(var + GN_EPS)).reshape(B, T, RW_WIDTH)
    y = y * gn_gain.astype(jnp.float32) + gn_bias.astype(jnp.float32)
    bonus = (jnp.sum(rH * kH * r_k.astype(jnp.float32), axis=-1, keepdims=True) * vH).reshape(B, T, RW_WIDTH)
    y = (y + bonus) * g.astype(jnp.float32)
    return y.astype(rw.dtype), rw[:, -1], S.astype(S0.dtype)


def trunk_layer(x, p, shift_prev, pool_past, S0, pos0,
                norm_mix, w_in, mu_shift, pool_w, pool_scale, w0, w_decay_up, a0, w_aaa_up, w_gate_up,
                k_k, k_a, r_k, gn_gain, gn_bias, proj_pool, proj_rwkv, w_out,
                norm_ffn, w_ffn_gate, w_ffn_up, w_ffn_down, norm_ple, w_ple_gate, w_ple_proj):
    h = rmsnorm(x, norm_mix)
    proj = h @ w_in
    u_pool, u_rw, g_pool, g_rw = jnp.split(
        proj, [POOL_WIDTH, POOL_WIDTH + RW_PROJ, POOL_WIDTH + RW_PROJ + D_MODEL], axis=-1)
    y_a, new_pool = pool_mixer(u_pool, pool_past, pos0, pool_w, pool_scale)
    y_b, new_shift, S = rwkv7_mix(u_rw, shift_prev, S0, mu_shift, w0, w_decay_up, a0, w_aaa_up,
                                  w_gate_up, k_k, k_a, r_k, gn_gain, gn_bias)
    merged = jax.nn.sigmoid(g_pool) * (y_a @ proj_pool) + jax.nn.sigmoid(g_rw) * (y_b @ proj_rwkv)
    x = x + merged @ w_out
    h2 = rmsnorm(x, norm_ffn)
    x = x + (jax.nn.silu(h2 @ w_ffn_gate) * (h2 @ w_ffn_up)) @ w_ffn_down
    h3 = rmsnorm(x, norm_ple)
    x = x + jax.nn.sigmoid(h3 @ w_ple_gate) * (p @ w_ple_proj)
    return x, new_shift, new_pool, S


def setup_inputs(seed: int = 0) -> dict:
    key = jax.random.key(seed)
    ks = iter(jax.random.split(key, 40))
    nrm = lambda shape, scale: jax.random.normal(next(ks), shape, jnp.float32) * scale
    unif = lambda shape, lo, hi: jax.random.uniform(next(ks), shape, jnp.float32, lo, hi)
    L = DEPTH
    return {
        'x_prompt': nrm((BATCH, SEQ, D_MODEL), 1.0),
        'x_sample': nrm((DEC_BATCH, DEC_SEQ, D_MODEL), 1.0),
        'state_shift': nrm((L, DEC_BATCH, RW_PROJ), 1.0),
        'state_pool': nrm((L, DEC_BATCH, POOL_PAST, POOL_WIDTH), 1.0),
        'state_wkv': nrm((L, DEC_BATCH, RW_HEADS, RW_HEAD, RW_HEAD), 0.5),
        'p_prompt': nrm((L, BATCH, SEQ, PLE_DIM), 1.0),
        'p_sample': nrm((L, DEC_BATCH, DEC_SEQ, PLE_DIM), 1.0),
        'norm_mix': 1.0 + nrm((L, D_MODEL), 0.05),
        'w_in': nrm((L, D_MODEL, IN_COLS), D_MODEL ** -0.5),
        'mu_shift': unif((L, RW_PROJ), 0.0, 1.0),
        'pool_w': nrm((L, POOL_GROUPS, POOL_GC, POOL_GC), POOL_GC ** -0.5),
        'pool_scale': 1.0 + nrm((L, POOL_WIDTH), 0.1),
        'w0': unif((L, RW_WIDTH), -5.0, 0.5),
        'w_decay_up': nrm((L, DECAY_LORA, RW_WIDTH), 0.1 * DECAY_LORA ** -0.5),
        'a0': nrm((L, RW_WIDTH), 0.5),
        'w_aaa_up': nrm((L, AAA_LORA, RW_WIDTH), AAA_LORA ** -0.5),
        'w_gate_up': nrm((L, GATE_LORA, RW_WIDTH), GATE_LORA ** -0.5),
        'k_k': 0.85 + nrm((L, RW_WIDTH), 0.05),
        'k_a': 1.0 + nrm((L, RW_WIDTH), 0.05),
        'r_k': nrm((L, RW_HEADS, RW_HEAD), 0.1),
        'gn_gain': 1.0 + nrm((L, RW_WIDTH), 0.05),
        'gn_bias': nrm((L, RW_WIDTH), 0.01),
        'proj_pool': nrm((L, POOL_WIDTH, D_MODEL), POOL_WIDTH ** -0.5),
        'proj_rwkv': nrm((L, RW_WIDTH, D_MODEL), RW_WIDTH ** -0.5),
        'w_out': nrm((L, D_MODEL, D_MODEL), D_MODEL ** -0.5),
        'norm_ffn': 1.0 + nrm((L, D_MODEL), 0.05),
        'w_ffn_gate': nrm((L, D_MODEL, D_FF), D_MODEL ** -0.5),
        'w_ffn_up': nrm((L, D_MODEL, D_FF), D_MODEL ** -0.5),
        'w_ffn_down': nrm((L, D_FF, D_MODEL), D_FF ** -0.5),
        'norm_ple': 1.0 + nrm((L, D_MODEL), 0.05),
        'w_ple_gate': nrm((L, D_MODEL, D_MODEL), D_MODEL ** -0.5),
        'w_ple_proj': nrm((L, PLE_DIM, D_MODEL), PLE_DIM ** -0.5),
        'norm_final': 1.0 + nrm((D_MODEL,), 0.05),
    }


def reference(x_prompt, x_sample, state_shift, state_pool, state_wkv, p_prompt, p_sample,
              norm_mix, w_in, mu_shift, pool_w, pool_scale, w0, w_decay_up, a0, w_aaa_up, w_gate_up,
              k_k, k_a, r_k, gn_gain, gn_bias, proj_pool, proj_rwkv, w_out,
              norm_ffn, w_ffn_gate, w_ffn_up, w_ffn_down, norm_ple, w_ple_gate, w_ple_proj, norm_final):
    def layer_params(i):
        return (norm_mix[i], w_in[i], mu_shift[i], pool_w[i], pool_scale[i], w0[i], w_decay_up[i], a0[i],
                w_aaa_up[i], w_gate_up[i], k_k[i], k_a[i], r_k[i], gn_gain[i], gn_bias[i], proj_pool[i],
                proj_rwkv[i], w_out[i], norm_ffn[i], w_ffn_gate[i], w_ffn_up[i], w_ffn_down[i], norm_ple[i],
                w_ple_gate[i], w_ple_proj[i])

    bp = x_prompt.shape[0]
    x = x_prompt
    sh_p, po_p, wk_p = [], [], []
    for i in range(DEPTH):
        x, s_sh, s_po, s_wk = trunk_layer(
            x, p_prompt[i], jnp.zeros((bp, RW_PROJ), x.dtype), jnp.zeros((bp, POOL_PAST, POOL_WIDTH), x.dtype),
            jnp.zeros((bp, RW_HEADS, RW_HEAD, RW_HEAD), jnp.float32), 0, *layer_params(i))
        sh_p.append(s_sh)
        po_p.append(s_po)
        wk_p.append(s_wk)
    y_prompt = rmsnorm(x, norm_final)

    x = x_sample
    sh_s, po_s, wk_s = [], [], []
    for i in range(DEPTH):
        x, s_sh, s_po, s_wk = trunk_layer(
            x, p_sample[i], state_shift[i], state_pool[i], state_wkv[i], PAST_LEN, *layer_params(i))
        sh_s.append(s_sh)
        po_s.append(s_po)
        wk_s.append(s_wk)
    y_sample = rmsnorm(x, norm_final)

    return (y_prompt, y_sample,
            jnp.stack(sh_p), jnp.stack(po_p), jnp.stack(wk_p),
            jnp.stack(sh_s), jnp.stack(po_s), jnp.stack(wk_s))
```

```python
import contextlib
import numpy as np
import concourse.bass as bass
import concourse.mybir as mybir
from concourse.bass_utils import run_bass_kernel_spmd

F32 = mybir.dt.float32
BF16 = mybir.dt.bfloat16
AF = mybir.ActivationFunctionType
ALU = mybir.AluOpType
AX = mybir.AxisListType

D = 2048
KC = 16
POOLW = 1024
RWW = 1024
NHEAD = 16
DFF = 5632
FC = 44
FG = 4
FGC = 11
PLE = 256
NQ = 28
NWIN = 68
NPRM = 124
C_NM, C_NF, C_NP, C_MU, C_PSC, C_W0, C_A0, C_KK, C_KA, C_RK = 0, 16, 32, 48, 76, 84, 92, 100, 108, 116
K_ID, K_PERM, K_ONES, K_BLK2, K_BO2, K_MG, K_MX, K_RST, K_RC = 0, 128, 256, 384, 512, 514, 770, 834, 1346
NCST = 1410
NORM_EPS = 1e-6
GN_EPS = 64e-5
C0 = float(np.exp(-0.5))
NSLOT = 6
SCR_KB = 72


class Sched:
    QUEUES = ['pe', 'act', 'dve', 'pool', 'sp']

    def __init__(self):
        self.ops = []

    def op(self, q, fn, reads=(), writes=(), stream=None):
        self.ops.append(dict(q=q, fn=fn, reads=reads, writes=writes, stream=stream))

    def analyze(self):
        ops = self.ops
        last_w = {}
        readers = {}
        last_on_stream = {}
        for i, o in enumerate(ops):
            deps = set()
            for r in o['reads']:
                j = last_w.get(r)
                if j is not None:
                    deps.add(j)
            for w in o['writes']:
                j = last_w.get(w)
                if j is not None:
                    deps.add(j)
                rd = readers.get(w)
                if rd:
                    deps.update(rd.values())
            if o['stream'] is not None:
                j = last_on_stream.get(o['stream'])
                if j is not None:
                    deps.add(j)
                last_on_stream[o['stream']] = i
            deps.discard(i)
            ck = ('s', o['stream']) if o['stream'] is not None else ('q', o['q'])
            o['ck'] = ck
            for r in o['reads']:
                readers.setdefault(r, {})[ck] = i
            for w in o['writes']:
                last_w[w] = i
                readers[w] = {}
            o['deps'] = deps
            o['sig'] = o['stream'] is not None
        for o in ops:
            nd = set()
            for d in o['deps']:
                p = ops[d]
                if p['ck'] == ('q', 'pe') and o['ck'] == ('q', 'pe'):
                    continue
                nd.add(d)
                p['sig'] = True
            o['deps'] = nd
        cnt = {}
        for o in ops:
            if o['sig']:
                k = o['ck']
                cnt[k] = cnt.get(k, 0) + (16 if o['stream'] is not None else 1)
                o['semval'] = cnt[k]
        self.final_counts = cnt
        known = {q: {} for q in self.QUEUES}
        for o in ops:
            need = {}
            for d in o['deps']:
                p = ops[d]
                k = p['ck']
                v = p['semval']
                if v > need.get(k, 0):
                    need[k] = v
            waits = []
            kq = known[o['q']]
            for k, v in need.items():
                if kq.get(k, 0) >= v:
                    continue
                kq[k] = v
                waits.append((k, v))
            o['waits'] = waits
        return cnt

    def emit(self, sems, block):
        ops = self.ops
        byq = {q: [o for o in ops if o['q'] == q] for q in self.QUEUES}
        dec = {'pe': block.tensor, 'act': block.scalar, 'dve': block.vector, 'pool': block.gpsimd, 'sp': block.sync}
        final = list(self.final_counts.items())
        for q in self.QUEUES:
            lst = byq[q]

            def body(eng, lst=lst, q=q):
                for o in lst:
                    for (k, v) in o['waits']:
                        eng.wait_ge(sems[k], v)
                    ins = o['fn'](eng)
                    if o['sig']:
                        ins.then_inc(sems[o['ck']], 16 if o['stream'] is not None else 1)
                if q == 'sp':
                    for k, v in final:
                        eng.wait_ge(sems[k], v)
            dec[q](body)


class Buf:
    def __init__(self, ap, keys):
        self.ap = ap
        self.keys = tuple(keys)


def _keys(items):
    out = []
    for it in items:
        if isinstance(it, Buf):
            out.extend(it.keys)
        elif isinstance(it, list):
            out.extend(_keys(it))
        else:
            out.append(it)
    return tuple(out)


def build_program(L, TP, TS, TILE):
    nc = bass.Bass("TRN2", target_bir_lowering=False)
    S = Sched()

    def OP(q, fn, r=(), w=(), stream=None):
        S.op(q, fn, _keys(r), _keys(w), stream)

    def din(name, shape):
        return nc.dram_tensor(name, list(shape), F32, kind="ExternalInput").ap()

    def dout(name, shape):
        return nc.dram_tensor(name, list(shape), F32, kind="ExternalOutput").ap()

    xp_d = din("xp", [KC, 128, TP]); xs_d = din("xs", [KC, 128, TS])
    pp_d = din("pp", [L, 2, 128, TP]); psm_d = din("psm", [L, 2, 128, TS])
    sts_d = din("st_shift", [128, L * NQ]); stp_d = din("st_pool", [128, L * 120]); stw_d = din("st_wkv", [L, 64, 1024])
    prm_d = din("prm", [L, 128, NPRM]); nfin_d = din("nfin", [128, KC]); gn_d = din("gn", [L, 2048]); cst_d = din("cst", [128, NCST])
    win_d = din("w_in_b", [L, NWIN, 128, 2048]); pw_d = din("pw_b", [L, 128, 2048]); lora_d = din("lora_b", [L, 2, 128, 2048])
    ppj_d = din("ppool_b", [L, 8, 128, 2048]); prw_d = din("prwkv_b", [L, 8, 128, 2048]); wo_d = din("wo_b", [L, 16, 128, 2048])
    wg_d = din("wg_b", [L, FC, 128, 2048]); wu_d = din("wu_b", [L, FC, 128, 2048]); wd_d = din("wd_b", [L, FG, 16, 128, FGC * 128])
    pg_d = din("pg_b", [L, 16, 128, 2048]); pj_d = din("pj_b", [L, 2, 128, 2048])
    yp_d = dout("yp", [KC, 128, TP]); ys_d = dout("ys", [KC, 128, TS])
    shp_d = dout("sh_p", [128, L * NQ]); shs_d = dout("sh_s", [128, L * NQ])
    pop_d = dout("po_p", [128, L * 120]); pos_d = dout("po_s", [128, L * 120])
    wkp_d = dout("wk_p", [L, 64, 1024]); wks_d = dout("wk_s", [L, 64, 1024])

    es = contextlib.ExitStack()
    with es:
        def sbt(name, shape, dt):
            return es.enter_context(nc.sbuf_tensor(name, list(shape), dt))

        TM = TILE
        X = sbt("X", [128, KC, TM], F32)
        H = sbt("H", [128, KC, TM], BF16)
        YA = sbt("YA", [128, 8, TM], BF16)
        YB = sbt("YB", [128, 8, TM], BF16)
        ST = sbt("ST", [64, NHEAD, 64], F32)
        POOLST = sbt("POOLST", [128, L, 8, 15], F32)
        SHIFT = sbt("SHIFT", [128, L, NQ], F32)
        WR = sbt("WR", [128, NSLOT, 2048], BF16)
        LA = sbt("LA", [128, 2048], BF16)
        LB = sbt("LB", [128, 2048], BF16)
        CST = sbt("CST", [128, NCST], F32)
        CB = sbt("CB", [128, 514], BF16)
        PRM = sbt("PRM", [128, NPRM], F32)
        NFIN = sbt("NFIN", [128, KC], F32)
        SCR = sbt("SCR", [128, SCR_KB * 256], F32)
        SCRf = SCR[:]
        SCRb = SCR[:].bitcast(BF16)
        PQ = [es.enter_context(nc.psum_tensor("PQ%d" % i, [128, 1024], F32)) for i in range(4)]
        PQb = [p[:].bitcast(BF16) for p in PQ]

        IDb = CB[:, 0:128]; PERMb = CB[:, 128:256]; ONESb = CB[:, 256:384]; BLK2b = CB[:, 384:512]; BO2b = CB[:, 512:514]
        PERMf = CST[:, K_PERM:K_PERM + 128]

        scr_top = [0]

        def salloc(free_shape, dt, parts=128):
            n = int(np.prod(free_shape))
            nbytes = n * (4 if dt == F32 else 2)
            nslab = (nbytes + 1023) // 1024
            off = scr_top[0]
            scr_top[0] += nslab
            assert scr_top[0] <= SCR_KB, "scratch overflow %d" % scr_top[0]
            if dt == F32:
                ap = SCRf[0:parts, off * 256: off * 256 + n]
            else:
                ap = SCRb[0:parts, off * 512: off * 512 + n]
            if len(free_shape) == 2:
                ap = ap.rearrange("p (a b) -> p a b", b=free_shape[1])
            elif len(free_shape) == 3:
                ap = ap.rearrange("p (a b c) -> p a b c", b=free_shape[1], c=free_shape[2])
            elif len(free_shape) == 4:
                ap = ap.rearrange("p (a b c d) -> p a b c d", b=free_shape[1], c=free_shape[2], d=free_shape[3])
            return Buf(ap, [('scr', off + i) for i in range(nslab)])

        def smark():
            return scr_top[0]

        def srelease(m):
            scr_top[0] = m

        bank_rr = [0]

        def bank():
            b = bank_rr[0] % 8
            bank_rr[0] += 1
            return b

        def bank_ap(b):
            return PQ[b // 2][:, (b % 2) * 512:(b % 2) * 512 + 512]

        def bank_apb(b):
            return PQb[b // 2][:, (b % 2) * 1024:(b % 2) * 1024 + 1024]

        pair_rr = [0]

        def bpair():
            p = pair_rr[0] % 4
            pair_rr[0] += 1
            return p

        wr_ctr = [0]

        wr_busy = [False] * NSLOT

        def wfree(s):
            wr_busy[s] = False

        def wload(src_ap):
            n = src_ap.shape[-1]
            s = wr_ctr[0] % NSLOT
            wr_ctr[0] += 1
            assert not wr_busy[s], "weight ring slot still pinned"
            wr_busy[s] = True
            OP('pool', lambda e, s=s, src_ap=src_ap, n=n: e.dma_start(out=WR[:, s, 0:n], in_=src_ap), w=[('w', s)], stream='w%d' % s)
            return s

        sp_ctr = [0]

        def spdma(out, in_, r=(), w=()):
            st = 'sp%d' % (sp_ctr[0] % 4)
            sp_ctr[0] += 1
            OP('sp', lambda e, out=out, in_=in_: e.dma_start(out=out, in_=in_), r=r, w=w, stream=st)

        def pooldma(out, in_, r=(), w=()):
            OP('pool', lambda e, out=out, in_=in_: e.dma_start(out=out, in_=in_), r=r, w=w, stream='pm')

        def MM(out, lhsT, rhs, start, stop, r, w):
            OP('pe', lambda e, out=out, lhsT=lhsT, rhs=rhs, start=start, stop=stop: e.matmul(out, lhsT=lhsT, rhs=rhs, start=start, stop=stop), r=r, w=w)

        def TR(out, in_, ident, r, w):
            OP('pe', lambda e, out=out, in_=in_, ident=ident: e.transpose(out, in_, ident), r=r, w=w)

        def ACT(out, in_, func, r, w, bias=0.0, scale=1.0):
            OP('act', lambda e, out=out, in_=in_, func=func, bias=bias, scale=scale: e.activation(out=out, in_=in_, func=func, bias=bias, scale=scale), r=r, w=w)

        def TT(out, in0, in1, op, r, w, q='dve'):
            OP(q, lambda e, out=out, in0=in0, in1=in1, op=op: e.tensor_tensor(out=out, in0=in0, in1=in1, op=op), r=r, w=w)

        def TS_(out, in0, s1, s2, op0, op1, r, w):
            OP('dve', lambda e, out=out, in0=in0, s1=s1, s2=s2, op0=op0, op1=op1: e.tensor_scalar(out=out, in0=in0, scalar1=s1, scalar2=s2, op0=op0, op1=op1), r=r, w=w)

        def STT(out, in0, sc, in1, op0, op1, r, w):
            OP('dve', lambda e, out=out, in0=in0, sc=sc, in1=in1, op0=op0, op1=op1: e.scalar_tensor_tensor(out=out, in0=in0, scalar=sc, in1=in1, op0=op0, op1=op1), r=r, w=w)

        def CP(out, in_, r, w, q='dve'):
            if q == 'act':
                ACT(out, in_, AF.Copy, r, w)
            else:
                OP('dve', lambda e, out=out, in_=in_: e.tensor_copy(out=out, in_=in_), r=r, w=w)

        def MSET(ap, val, w):
            OP('dve', lambda e, ap=ap, val=val: e.memset(ap, val), w=w)

        spdma(CST[:], cst_d, w=['cst'])
        spdma(NFIN[:], nfin_d, w=['nfin'])
        CP(CB[:, 0:514], CST[:, 0:514], r=['cst'], w=['cb'])
        MASKG = CST[0:64, K_MG:K_MG + 256]
        MASKX = CST[0:64, K_MX:K_MX + 64]
        I64f = CST[0:64, K_ID:K_ID + 64]
        RC = CST[:, K_RC:K_RC + 64].rearrange("p (g t) -> p g t", t=16)

        def rmsnorm(T, gsrc, gkey, dst_fn, dst_key_fn):
            m = smark()
            sqb = [salloc([T], BF16) for _ in range(4)]
            rt = salloc([T], F32)
            rstd = salloc([T], F32)
            b = bank()
            pk = ('ps', b)
            for kc in range(KC):
                sq = sqb[kc % 4]
                ACT(sq.ap, X[:, kc, 0:T], AF.Square, r=[('x', kc)], w=[sq])
                MM(bank_ap(b)[:, 0:T], ONESb, sq.ap, kc == 0, kc == KC - 1, r=[sq, 'cb'], w=[pk])
            ACT(rt.ap, bank_ap(b)[:, 0:T], AF.Sqrt, r=[pk], w=[rt], bias=NORM_EPS, scale=1.0 / D)
            OP('dve', lambda e: e.reciprocal(out=rstd.ap, in_=rt.ap), r=_keys([rt]), w=_keys([rstd]))
            for kc in range(KC):
                STT(dst_fn(kc), X[:, kc, 0:T], gsrc[:, kc:kc + 1], rstd.ap, ALU.mult, ALU.mult,
                    r=[('x', kc), gkey, rstd], w=list(dst_key_fn(kc)))
            srelease(m)

        def proj_block(T, blk_ap, n_k, rhs_fn, rhs_keys_fn, lhs_off=0, m=128):
            s = wload(blk_ap)
            b = bank()
            for kc in range(n_k):
                MM(bank_ap(b)[0:m, 0:T], WR[:, s, lhs_off + kc * 128: lhs_off + kc * 128 + m], rhs_fn(kc), kc == 0, kc == n_k - 1,
                   r=[('w', s)] + list(rhs_keys_fn(kc)), w=[('ps', b)])
            wfree(s)
            return b

        def layer_tile(l, T, first, is_prompt_first, wk_src, wk_dst, p_src):
            nch = T // 64
            W = 15 + T
            spdma(PRM[:], prm_d[l], w=['prm'])
            pooldma(LA[:], lora_d[l, 0], w=['la'])
            pooldma(LB[:], lora_d[l, 1], w=['lb'])
            if is_prompt_first:
                MSET(ST[:], 0.0, w=['st'])
            else:
                spdma(ST[:].rearrange("p h v -> p (h v)"), wk_src[l], r=[('wk', l)] if wk_src is wkp_d else [], w=['st'])

            def pcol(c):
                return PRM[:, c:c + 1]

            rmsnorm(T, PRM[:, C_NM:C_NM + KC], 'prm', lambda kc: H[:, kc, 0:T], lambda kc: [('h', kc)])
            hr = lambda kc: H[:, kc, 0:T]
            hk = lambda kc: [('h', kc)]

            mB = smark()
            tw = salloc([T], BF16); ad = salloc([T], BF16); sg1 = salloc([T], BF16); sg2 = salloc([T], BF16)
            gnt = salloc([2, 256], F32, parts=64)
            rwb_all = salloc([3, T + 1], F32)
            rwb = [Buf(rwb_all.ap[:, i, :], rwb_all.keys) for i in range(3)]
            xm = [salloc([T], F32) for _ in range(3)]
            tmpA = salloc([T], F32)

            def rw_chunk(q, slot):
                b = proj_block(T, win_d[l, 8 + q], KC, hr, hk)
                rb = rwb[slot]
                ACT(rb.ap[:, 1:T + 1], bank_ap(b)[:, 0:T], AF.Copy, r=[('ps', b)], w=[rb])
                CP(rb.ap[:, 0:1], SHIFT[:, l, q:q + 1], r=['shift'], w=[rb])
                TT(tmpA.ap, rb.ap[:, 0:T], rb.ap[:, 1:T + 1], ALU.subtract, r=[rb], w=[tmpA])
                STT(xm[slot].ap, tmpA.ap, pcol(C_MU + q), rb.ap[:, 1:T + 1], ALU.mult, ALU.add, r=[tmpA, rb, 'prm'], w=[xm[slot]])
                CP(SHIFT[:, l, q:q + 1], rb.ap[:, T:T + 1], r=[rb], w=['shift'])

            rw_chunk(0, 0)
            ACT(tw.ap[0:64, :], xm[0].ap[0:64, :], AF.Tanh, r=[xm[0]], w=[tw])
            rw_chunk(1, 1)
            CP(ad.ap[0:64, :], xm[1].ap[0:64, :], r=[xm[1]], w=[ad], q='act')
            rw_chunk(2, 2)
            ACT(sg1.ap, xm[2].ap, AF.Sigmoid, r=[xm[2]], w=[sg1])
            rw_chunk(3, 0)
            ACT(sg2.ap, xm[0].ap, AF.Sigmoid, r=[xm[0]], w=[sg2])

            for qd in range(4):
                mQ = smark()
                spdma(gnt.ap[:, 0, :], gn_d[l:l + 1, qd * 256:(qd + 1) * 256].partition_broadcast(64), w=[gnt])
                spdma(gnt.ap[:, 1, :], gn_d[l:l + 1, 1024 + qd * 256:1024 + (qd + 1) * 256].partition_broadcast(64), w=[gnt])
                ARs = salloc([2, nch, 128], BF16); ARo = salloc([2, nch, 128], BF16)
                BKs = salloc([2, nch, 128], BF16); BKo = salloc([2, nch, 128], BF16)
                BKc = salloc([2, nch, 128], BF16)
                VB = salloc([2, T], BF16); RK = salloc([2, T], BF16)
                PCs = salloc([2, 8], F32); PCH = salloc([nch, 4], F32, parts=64)
                MSET(PCs.ap, 0.0, w=[PCs])
                mP = smark()
                t = [salloc([T], F32) for _ in range(11)]
                sqk = salloc([T], BF16)
                for jj in range(2):
                    j = qd * 2 + jj
                    for i3 in range(3):
                        rw_chunk(4 + 3 * j + i3, i3)
                    XR, XK, XV = xm[0], xm[1], xm[2]
                    sigz, cum, exc, pinc, pinv, pexc, av, kkr, kk, kh, tb = t
                    b = bank()
                    MM(bank_ap(b)[:, 0:T], LA[0:64, j * 128:(j + 1) * 128], tw.ap[0:64, :], True, True, r=['la', tw], w=[('ps', b)])
                    ACT(sigz.ap, bank_ap(b)[:, 0:T], AF.Sigmoid, r=[('ps', b), 'prm'], w=[sigz], bias=pcol(C_W0 + j))
                    OP('dve', lambda e, cum=cum, sigz=sigz: e.tensor_tensor_scan(out=cum.ap, data0=CST[:, K_RST:K_RST + T], data1=sigz.ap,
                                                                                 initial=0.0, op0=ALU.mult, op1=ALU.add),
                       r=_keys([sigz, 'cst']), w=_keys([cum]))
                    TT(exc.ap, cum.ap, sigz.ap, ALU.subtract, r=[cum, sigz], w=[exc])
                    ACT(pinc.ap, cum.ap, AF.Exp, r=[cum], w=[pinc], scale=-C0)
                    ACT(pinv.ap, cum.ap, AF.Exp, r=[cum], w=[pinv], scale=C0)
                    ACT(pexc.ap, exc.ap, AF.Exp, r=[exc], w=[pexc], scale=-C0)
                    CP(PCs.ap[:, jj, 0:nch], pinc.ap.rearrange("p (c t) -> p c t", t=64)[:, :, 63], r=[pinc], w=[PCs])
                    b = bank()
                    MM(bank_ap(b)[:, 0:T], LA[0:64, 1024 + j * 128:1024 + (j + 1) * 128], ad.ap[0:64, :], True, True, r=['la', ad], w=[('ps', b)])
                    ACT(av.ap, bank_ap(b)[:, 0:T], AF.Sigmoid, r=[('ps', b), 'prm'], w=[av], bias=pcol(C_A0 + j))
                    TS_(kkr.ap, XK.ap, pcol(C_KK + j), None, ALU.mult, ALU.bypass, r=[XK, 'prm'], w=[kkr])
                    TT(sqk.ap, kkr.ap, kkr.ap, ALU.mult, r=[kkr], w=[sqk])
                    b = bank()
                    MM(bank_ap(b)[:, 0:T], BLK2b, sqk.ap, True, True, r=['cb', sqk], w=[('ps', b)])
                    ACT(exc.ap, bank_ap(b)[:, 0:T], AF.Sqrt, r=[('ps', b)], w=[exc])
                    TS_(exc.ap, exc.ap, 1e-12, None, ALU.max, ALU.bypass, r=[exc], w=[exc])
                    OP('dve', lambda e, exc=exc: e.reciprocal(out=exc.ap, in_=exc.ap), r=_keys([exc]), w=_keys([exc]))
                    TT(kk.ap, kkr.ap, exc.ap, ALU.mult, r=[kkr, exc], w=[kk])
                    TS_(tb.ap, av.ap, 1.0, pcol(C_KA + j), ALU.subtract, ALU.mult, r=[av, 'prm'], w=[tb])
                    STT(kh.ap, tb.ap, 1.0, XK.ap, ALU.add, ALU.mult, r=[tb, XK], w=[kh])
                    c3 = lambda ap: ap.rearrange("p (c t) -> p c t", t=64)
                    STT(ARs.ap[:, jj, :, 0:64], c3(kk.ap), -1.0, c3(pexc.ap), ALU.mult, ALU.mult, r=[kk, pexc], w=[ARs])
                    TT(ARs.ap[:, jj, :, 64:128], c3(XR.ap), c3(pinc.ap), ALU.mult, r=[XR, pinc], w=[ARs])
                    STT(RK.ap[:, jj, :], XR.ap, pcol(C_RK + j), kh.ap, ALU.mult, ALU.mult, r=[XR, kh, 'prm'], w=[RK])
                    CP(VB.ap[:, jj, :], XV.ap, r=[XV], w=[VB], q='act')
                    TT(tb.ap, kk.ap, av.ap, ALU.mult, r=[kk, av], w=[tb])
                    TT(tb.ap, tb.ap, pinv.ap, ALU.mult, r=[tb, pinv], w=[tb])
                    CP(BKs.ap[:, jj, :, 0:64], c3(tb.ap), r=[tb], w=[BKs], q='act')
                    pcb = PCs.ap[:, jj, 0:nch].unsqueeze(2).to_broadcast([128, nch, 64])
                    TT(BKc.ap[:, jj, :, 0:64], c3(tb.ap), pcb, ALU.mult, r=[tb, PCs], w=[BKc])
                    TT(kh.ap, kh.ap, pinv.ap, ALU.mult, r=[kh, pinv], w=[kh])
                    CP(BKs.ap[:, jj, :, 64:128], c3(kh.ap), r=[kh], w=[BKs], q='act')
                    TT(BKc.ap[:, jj, :, 64:128], c3(kh.ap), pcb, ALU.mult, r=[kh, PCs], w=[BKc])
                    for (src, dst) in ((ARs, ARo), (BKs, BKo)):
                        p2 = bpair()
                        n = nch * 128
                        for h0 in range(0, n, 512):
                            hn = min(512, n - h0)
                            MM(PQ[p2][:, h0:h0 + hn], PERMb, src.ap[:, jj, :, :].rearrange("p c t -> p (c t)")[:, h0:h0 + hn], True, True,
                               r=['cb', src], w=[('ps', 2 * p2), ('ps', 2 * p2 + 1)])
                        CP(dst.ap[0:64, jj, :, :].rearrange("p c t -> p (c t)"), PQ[p2][0:64, 0:n], r=[('ps', 2 * p2), ('ps', 2 * p2 + 1)], w=[dst], q='act')
                    b = bank()
                    MM(bank_ap(b)[:, 0:8], PERMf, PCs.ap[:, jj, :], True, True, r=['cst', PCs], w=[('ps', b)])
                    CP(PCH.ap[:, :, 2 * jj], PCs.ap[0:64, jj, 0:nch], r=[PCs], w=[PCH])
                    CP(PCH.ap[:, :, 2 * jj + 1], bank_ap(b)[0:64, 0:nch], r=[('ps', b)], w=[PCH])
                srelease(mP)
                TOKc = salloc([2, 3, 128], BF16, parts=64)
                GT = salloc([4, 4, 64], BF16, parts=64)
                Xp = [salloc([4, 64], BF16, parts=64) for _ in range(2)]
                MTp = [salloc([4, 2, 64], BF16, parts=64) for _ in range(2)]
                Wb = salloc([4, 64], BF16, parts=64); Ub = salloc([4, 64], BF16, parts=64); STb = salloc([4, 64], BF16, parts=64)
                ysq = salloc([4, 64], F32, parts=64); yn = salloc([4, 64], F32, parts=64); bon = salloc([4, 64], F32, parts=64)
                stm = salloc([4, 64], F32, parts=64)
                yout = salloc([256], BF16, parts=64)
                sm = salloc([8, 4], F32, parts=64)
                STq = ST[:, qd * 4:(qd + 1) * 4, :]
                CP(STb.ap[0:64], STq, r=['st'], w=[STb])

                def ARh(hq, c):
                    jj, hp = hq // 2, hq % 2
                    return (ARs if hp == 0 else ARo).ap[0:64, jj, c, :], (ARs if hp == 0 else ARo)

                def BKh(hq, c):
                    jj, hp = hq // 2, hq % 2
                    return (BKs if hp == 0 else BKo).ap[0:64, jj, c, :], (BKs if hp == 0 else BKo)

                for c in range(nch):
                    cs = slice(c * 64, (c + 1) * 64)
                    b = bank()
                    pk = ('ps', b)
                    tv = bank_apb(b)[0:64, 0:768].rearrange("p (j k c) -> p j k c", k=3, c=128)
                    for jj in range(2):
                        TR(tv[:, jj, 0, :], VB.ap[:, jj, cs], IDb, r=[VB, 'cb'], w=[pk])
                        TR(tv[:, jj, 1, :], BKc.ap[:, jj, c, 0:64], IDb, r=[BKc, 'cb'], w=[pk])
                        TR(tv[:, jj, 2, :], BKc.ap[:, jj, c, 64:128], IDb, r=[BKc, 'cb'], w=[pk])
                    CP(TOKc.ap[0:64], tv, r=[pk], w=[TOKc])
                    vtok = lambda hq: TOKc.ap[0:64, hq // 2, 0, (hq % 2) * 64:(hq % 2) * 64 + 64]
                    bctok = lambda hq: TOKc.ap[0:64, hq // 2, 1, (hq % 2) * 64:(hq % 2) * 64 + 64]
                    kctok = lambda hq: TOKc.ap[0:64, hq // 2, 2, (hq % 2) * 64:(hq % 2) * 64 + 64]
                    p2 = bpair()
                    gk = [('ps', 2 * p2), ('ps', 2 * p2 + 1)]
                    gv = PQ[p2][0:64, :].rearrange("p (h x) -> p h x", x=256)
                    for hq in range(4):
                        ar, arb = ARh(hq, c)
                        bk, bkb = BKh(hq, c)
                        MM(gv[:, hq, 0:128], bk[:, 0:64], ar, True, True, r=[arb, bkb], w=gk)
                        MM(gv[:, hq, 128:256], bk[:, 64:128], ar, True, True, r=[arb, bkb], w=gk)
                    TT(GT.ap[0:64].rearrange("p h b t -> p h (b t)"), gv, MASKG.unsqueeze(1).to_broadcast([64, 4, 256]), ALU.mult, r=gk + ['cst'], w=[GT])
                    b = bank()
                    pk = ('ps', b)
                    xv = bank_ap(b)[0:64, 0:256].rearrange("p (h s) -> p h s", s=64)
                    for hq in range(4):
                        ar, arb = ARh(hq, c)
                        bk, bkb = BKh(hq, c)
                        MM(xv[:, hq, :], ar[:, 0:64], bk[:, 0:64], True, True, r=[arb, bkb], w=[pk])
                    Xc, MTc = Xp[0], MTp[0]
                    TT(Xc.ap[0:64], xv, MASKX.unsqueeze(1).to_broadcast([64, 4, 64]), ALU.mult, r=[pk, 'cst'], w=[Xc])
                    CP(MTc.ap[0:64, :, 0, :], GT.ap[0:64, :, 0, :], r=[GT], w=[MTc], q='act')
                    CP(MTc.ap[0:64, :, 1, :], I64f.unsqueeze(1).to_broadcast([64, 4, 64]), r=['cst'], w=[MTc])
                    for lev in range(6):
                        Xn, MTn = Xp[(lev + 1) % 2], MTp[(lev + 1) % 2]
                        b1 = bank(); b2 = bank()
                        o1 = bank_ap(b1)[0:64, 0:512].rearrange("p (h x) -> p h x", x=128)
                        o2 = bank_ap(b2)[0:64, 0:256].rearrange("p (h s) -> p h s", s=64)
                        last = lev == 5
                        for hq in range(4):
                            if last:
                                MM(o1[:, hq, 64:128], Xc.ap[0:64, hq, :], MTc.ap[0:64, hq, 1, :], True, True, r=[Xc, MTc], w=[('ps', b1)])
                            else:
                                MM(o1[:, hq, :], Xc.ap[0:64, hq, :], MTc.ap[0:64, hq, :, :].rearrange("p a t -> p (a t)"), True, True, r=[Xc, MTc], w=[('ps', b1)])
                                MM(o2[:, hq, :], MTc.ap[0:64, hq, 0, :], Xc.ap[0:64, hq, :], True, True, r=[Xc, MTc], w=[('ps', b2)])
                        TT(MTn.ap[0:64, :, 1, :], o1[:, :, 64:128], MTc.ap[0:64, :, 1, :], ALU.add, r=[('ps', b1), MTc], w=[MTn])
                        if not last:
                            CP(MTn.ap[0:64, :, 0, :], o1[:, :, 0:64], r=[('ps', b1)], w=[MTn], q='act')
                            CP(Xn.ap[0:64], o2, r=[('ps', b2)], w=[Xn], q='act')
                        Xc, MTc = Xn, MTn
                    TTf = MTc
                    bw = bank(); wk_ = ('ps', bw)
                    wv = bank_ap(bw)[0:64, 0:256].rearrange("p (h v) -> p h v", v=64)
                    for hq in range(4):
                        ar, arb = ARh(hq, c)
                        MM(wv[:, hq, :], ar[:, 0:64], STb.ap[0:64, hq, :], True, False, r=[arb, STb], w=[wk_])
                        MM(wv[:, hq, :], GT.ap[0:64, hq, 2, :], vtok(hq), False, True, r=[GT, TOKc], w=[wk_])
                    CP(Wb.ap[0:64], wv, r=[wk_], w=[Wb], q='act')
                    bu = bank(); uk = ('ps', bu)
                    uv = bank_ap(bu)[0:64, 0:256].rearrange("p (h v) -> p h v", v=64)
                    for hq in range(4):
                        MM(uv[:, hq, :], TTf.ap[0:64, hq, 1, :], Wb.ap[0:64, hq, :], True, True, r=[TTf, Wb], w=[uk])
                    CP(Ub.ap[0:64], uv, r=[uk], w=[Ub])
                    by = bank(); yk = ('ps', by)
                    yv = bank_ap(by)[0:64, 0:256].rearrange("p (h v) -> p h v", v=64)
                    for hq in range(4):
                        ar, arb = ARh(hq, c)
                        MM(yv[:, hq, :], ar[:, 64:128], STb.ap[0:64, hq, :], True, False, r=[arb, STb], w=[yk])
                        MM(yv[:, hq, :], GT.ap[0:64, hq, 1, :], Ub.ap[0:64, hq, :], False, False, r=[GT, Ub], w=[yk])
                        MM(yv[:, hq, :], GT.ap[0:64, hq, 3, :], vtok(hq), False, True, r=[GT, TOKc], w=[yk])
                    bs_ = bank(); sk = ('ps', bs_)
                    sv = bank_ap(bs_)[0:64, 0:256].rearrange("p (h v) -> p h v", v=64)
                    for hq in range(4):
                        MM(sv[:, hq, :], bctok(hq), Ub.ap[0:64, hq, :], True, False, r=[TOKc, Ub], w=[sk])
                        MM(sv[:, hq, :], kctok(hq), vtok(hq), False, True, r=[TOKc], w=[sk])
                    TT(stm.ap[0:64], STq, PCH.ap[0:64, c, :].unsqueeze(2).to_broadcast([64, 4, 64]), ALU.mult, r=['st', PCH], w=[stm])
                    TT(STq, stm.ap[0:64], sv, ALU.add, r=[stm, sk], w=['st'])
                    CP(STb.ap[0:64], STq, r=['st'], w=[STb], q='act')
                    bb = bank(); bk_ = ('ps', bb)
                    for jj in range(2):
                        MM(bank_ap(bb)[0:64, 2 * jj:2 * jj + 2], RK.ap[:, jj, cs], BO2b, True, True, r=[RK, 'cb'], w=[bk_])
                    bg = bank(); gk_ = ('ps', bg)
                    MM(bank_ap(bg)[0:64, 0:256], sg1.ap[:, cs], LB[:, qd * 256:(qd + 1) * 256], True, False, r=[sg1, 'lb'], w=[gk_])
                    MM(bank_ap(bg)[0:64, 0:256], sg2.ap[:, cs], LB[:, 1024 + qd * 256:1024 + (qd + 1) * 256], False, True, r=[sg2, 'lb'], w=[gk_])
                    s1 = sm.ap[0:64, 0, :]; s2 = sm.ap[0:64, 1, :]; mean = sm.ap[0:64, 2, :]; msq = sm.ap[0:64, 3, :]
                    var = sm.ap[0:64, 4, :]; rs = sm.ap[0:64, 5, :]; bsc = sm.ap[0:64, 6, :]
                    OP('dve', lambda e, s1=s1, yv=yv: e.tensor_reduce(out=s1, in_=yv, axis=AX.X, op=ALU.add), r=_keys([yk]), w=_keys([sm]))
                    ACT(ysq.ap[0:64], yv, AF.Square, r=[yk], w=[ysq])
                    OP('dve', lambda e, s2=s2, ysq=ysq: e.tensor_reduce(out=s2, in_=ysq.ap[0:64], axis=AX.X, op=ALU.add), r=_keys([ysq]), w=_keys([sm]))
                    TS_(mean, s1, 1.0 / 64, None, ALU.mult, ALU.bypass, r=[sm], w=[sm])
                    TT(msq, mean, mean, ALU.mult, r=[sm], w=[sm])
                    STT(var, s2, 1.0 / 64, msq, ALU.mult, ALU.subtract, r=[sm], w=[sm])
                    ACT(var, var, AF.Sqrt, r=[sm], w=[sm], bias=GN_EPS)
                    OP('dve', lambda e, rs=rs, var=var: e.reciprocal(out=rs, in_=var), r=_keys([sm]), w=_keys([sm]))
                    CP(bsc, bank_ap(bb)[0:64, 0:4], r=[bk_], w=[sm])
                    b3 = lambda ap: ap.unsqueeze(2).to_broadcast([64, 4, 64])
                    TT(yn.ap[0:64], yv, b3(mean), ALU.subtract, r=[yk, sm], w=[yn])
                    TT(yn.ap[0:64], yn.ap[0:64], b3(rs), ALU.mult, r=[yn, sm], w=[yn])
                    gq = gnt.ap[0:64, 0, :].rearrange("p (h v) -> p h v", v=64)
                    bq = gnt.ap[0:64, 1, :].rearrange("p (h v) -> p h v", v=64)
                    TT(yn.ap[0:64], yn.ap[0:64], gq, ALU.mult, r=[yn, gnt], w=[yn])
                    TT(yn.ap[0:64], yn.ap[0:64], bq, ALU.add, r=[yn, gnt], w=[yn])
                    vt4 = TOKc.ap[0:64, :, 0, :].rearrange("p j (h v) -> p j h v", v=64)
                    bsc4 = bsc.rearrange("p (j h) -> p j h", h=2).unsqueeze(3).to_broadcast([64, 2, 2, 64])
                    TT(bon.ap[0:64].rearrange("p (j h) v -> p j h v", h=2), vt4, bsc4, ALU.mult, r=[TOKc, sm], w=[bon])
                    TT(yn.ap[0:64], yn.ap[0:64], bon.ap[0:64], ALU.add, r=[yn, bon], w=[yn])
                    TT(yout.ap[0:64], yn.ap[0:64].rearrange("p h v -> p (h v)"), bank_ap(bg)[0:64, 0:256], ALU.mult, r=[yn, gk_], w=[yout])
                    bt = bank(); tk = ('ps', bt)
                    for jj in range(2):
                        TR(bank_apb(bt)[:, jj * 64:(jj + 1) * 64], yout.ap[0:64, jj * 128:(jj + 1) * 128], IDb[0:64, 0:64], r=[yout, 'cb'], w=[tk])
                    CP(YB[:, qd * 2:qd * 2 + 2, cs], bank_apb(bt)[:, 0:128].rearrange("p (j t) -> p j t", t=64), r=[tk], w=[('yb', qd)], q='act')
                srelease(mQ)
            srelease(mB)
            spdma(wk_dst[l], ST[:].rearrange("p h v -> p (h v)"), r=['st'], w=[('wk', l)])

            mA = smark()
            Z = salloc([8, W], F32); DP = salloc([8, T], BF16)
            SA = salloc([W], F32); SB_ = salloc([W], F32); PW = salloc([2048], BF16); tfix = salloc([16], F32)
            pooldma(PW.ap, pw_d[l], w=[PW])
            for ch in range(8):
                b = proj_block(T, win_d[l, ch], KC, hr, hk)
                ACT(Z.ap[:, ch, 15:W], bank_ap(b)[:, 0:T], AF.Copy, r=[('ps', b)], w=[Z])
                CP(Z.ap[:, ch, 0:15], POOLST[:, l, ch, :], r=['poolst'], w=[Z])
                gi = ch // 2
                win = 2 << gi
                zc = Z.ap[:, ch, :]
                TT(SA.ap[:, 1:W], zc[:, 1:W], zc[:, 0:W - 1], ALU.add, r=[Z], w=[SA])
                fin = SA
                if win >= 4:
                    TT(SB_.ap[:, 3:W], SA.ap[:, 3:W], SA.ap[:, 1:W - 2], ALU.add, r=[SA], w=[SB_]); fin = SB_
                if win >= 8:
                    TT(SA.ap[:, 7:W], SB_.ap[:, 7:W], SB_.ap[:, 3:W - 4], ALU.add, r=[SB_], w=[SA]); fin = SA
                if win >= 16:
                    TT(SB_.ap[:, 15:W], SA.ap[:, 15:W], SA.ap[:, 7:W - 8], ALU.add, r=[SA], w=[SB_]); fin = SB_
                STT(DP.ap[:, ch, :], fin.ap[:, 15:W], 1.0 / win, zc[:, 15:W], ALU.mult, ALU.subtract, r=[fin, Z], w=[DP])
                if is_prompt_first:
                    TT(tfix.ap, fin.ap[:, 15:31], RC[:, gi, :], ALU.mult, r=[fin, 'cst'], w=[tfix])
                    TT(DP.ap[:, ch, 0:16], tfix.ap, zc[:, 15:31], ALU.subtract, r=[tfix, Z], w=[DP])
                CP(POOLST[:, l, ch, :], zc[:, T:T + 15], r=[Z], w=['poolst'])
            pwv = PW.ap.rearrange("p (g k o) -> p g k o", k=2, o=256)
            for g in range(4):
                for o2 in range(2):
                    b = bank()
                    for k2 in range(2):
                        MM(bank_ap(b)[:, 0:T], pwv[:, g, k2, o2 * 128:(o2 + 1) * 128], DP.ap[:, 2 * g + k2, :], k2 == 0, k2 == 1, r=[PW, DP], w=[('ps', b)])
                    ACT(YA[:, 2 * g + o2, 0:T], bank_ap(b)[:, 0:T], AF.Copy, r=[('ps', b), 'prm'], w=[('ya', 2 * g + o2)], scale=pcol(C_PSC + 2 * g + o2))
            srelease(mA)

            mC = smark()
            MG = salloc([KC, T], BF16)
            s1b = salloc([T], F32); s2b = salloc([T], F32); m1 = salloc([T], F32); m2 = salloc([T], F32)
            for oc in range(KC):
                half, sub = oc // 2, oc % 2
                if sub == 0:
                    spool = wload(ppj_d[l, half])
                    srw = wload(prw_d[l, half])
                bp1 = bank()
                for kc in range(8):
                    MM(bank_ap(bp1)[:, 0:T], WR[:, spool, sub * 1024 + kc * 128: sub * 1024 + (kc + 1) * 128], YA[:, kc, 0:T], kc == 0, kc == 7,
                       r=[('w', spool), ('ya', kc)], w=[('ps', bp1)])
                bp2 = bank()
                for kc in range(8):
                    MM(bank_ap(bp2)[:, 0:T], WR[:, srw, sub * 1024 + kc * 128: sub * 1024 + (kc + 1) * 128], YB[:, kc, 0:T], kc == 0, kc == 7,
                       r=[('w', srw), ('yb', kc // 2)], w=[('ps', bp2)])
                bg1 = proj_block(T, win_d[l, 36 + oc], KC, hr, hk)
                bg2 = proj_block(T, win_d[l, 52 + oc], KC, hr, hk)
                if sub == 1:
                    wfree(spool); wfree(srw)
                ACT(s1b.ap, bank_ap(bg1)[:, 0:T], AF.Sigmoid, r=[('ps', bg1)], w=[s1b])
                ACT(s2b.ap, bank_ap(bg2)[:, 0:T], AF.Sigmoid, r=[('ps', bg2)], w=[s2b])
                TT(m1.ap, s1b.ap, bank_ap(bp1)[:, 0:T], ALU.mult, r=[s1b, ('ps', bp1)], w=[m1])
                TT(m2.ap, s2b.ap, bank_ap(bp2)[:, 0:T], ALU.mult, r=[s2b, ('ps', bp2)], w=[m2])
                TT(MG.ap[:, oc, :], m1.ap, m2.ap, ALU.add, r=[m1, m2], w=[MG])
            for oc in range(KC):
                b = proj_block(T, wo_d[l, oc], KC, lambda kc: MG.ap[:, kc, :], lambda kc: [MG])
                TT(X[:, oc, 0:T], X[:, oc, 0:T], bank_ap(b)[:, 0:T], ALU.add, r=[('x', oc), ('ps', b)], w=[('x', oc)])
            srelease(mC)

            rmsnorm(T, PRM[:, C_NF:C_NF + KC], 'prm', lambda kc: H[:, kc, 0:T], lambda kc: [('h', kc)])
            mD = smark()
            ACTB = [salloc([FGC, T], BF16) for _ in range(2)]
            sgb = [salloc([T], F32) for _ in range(2)]
            for fg in range(FG):
                ab = ACTB[fg % 2]
                for fi in range(FGC):
                    fc = fg * FGC + fi
                    bg_ = proj_block(T, wg_d[l, fc], KC, hr, hk)
                    bu_ = proj_block(T, wu_d[l, fc], KC, hr, hk)
                    sg = sgb[fi % 2]
                    ACT(sg.ap, bank_ap(bg_)[:, 0:T], AF.Silu, r=[('ps', bg_)], w=[sg])
                    TT(ab.ap[:, fi, :], sg.ap, bank_ap(bu_)[:, 0:T], ALU.mult, r=[sg, ('ps', bu_)], w=[ab])
                for oc in range(KC):
                    b = proj_block(T, wd_d[l, fg, oc], FGC, lambda kc: ab.ap[:, kc, :], lambda kc: [ab])
                    TT(X[:, oc, 0:T], X[:, oc, 0:T], bank_ap(b)[:, 0:T], ALU.add, r=[('x', oc), ('ps', b)], w=[('x', oc)])
            srelease(mD)

            rmsnorm(T, PRM[:, C_NP:C_NP + KC], 'prm', lambda kc: H[:, kc, 0:T], lambda kc: [('h', kc)])
            mE = smark()
            PT = salloc([2, T], BF16); sp_ = salloc([T], F32); tp_ = salloc([T], F32)
            PJ = [salloc([2048], BF16) for _ in range(2)]
            pooldma(PT.ap, p_src(l), w=[PT])
            pooldma(PJ[0].ap, pj_d[l, 0], w=[PJ[0]])
            pooldma(PJ[1].ap, pj_d[l, 1], w=[PJ[1]])
            for oc in range(KC):
                pjb = PJ[oc // 8]
                bgp = proj_block(T, pg_d[l, oc], KC, hr, hk)
                bpp = bank()
                for k2 in range(2):
                    MM(bank_ap(bpp)[:, 0:T], pjb.ap[:, (oc % 8) * 256 + k2 * 128:(oc % 8) * 256 + (k2 + 1) * 128], PT.ap[:, k2, :], k2 == 0, k2 == 1,
                       r=[pjb, PT], w=[('ps', bpp)])
                ACT(sp_.ap, bank_ap(bgp)[:, 0:T], AF.Sigmoid, r=[('ps', bgp)], w=[sp_])
                TT(tp_.ap, sp_.ap, bank_ap(bpp)[:, 0:T], ALU.mult, r=[sp_, ('ps', bpp)], w=[tp_])
                TT(X[:, oc, 0:T], X[:, oc, 0:T], tp_.ap, ALU.add, r=[('x', oc), tp_], w=[('x', oc)])
            srelease(mE)

        def run_segment(kind, t0, T, first, last):
            xsrc = xp_d if kind == 'p' else xs_d
            ydst = yp_d if kind == 'p' else ys_d
            spdma(X[:, :, 0:T], xsrc[:, :, t0:t0 + T].rearrange("c p t -> p c t"), w=[('x', kc) for kc in range(KC)])
            if first:
                if kind == 'p':
                    MSET(POOLST[:], 0.0, w=['poolst'])
                    MSET(SHIFT[:], 0.0, w=['shift'])
                else:
                    spdma(POOLST[:].rearrange("p l c t -> p (l c t)"), stp_d, w=['poolst'])
                    spdma(SHIFT[:].rearrange("p l q -> p (l q)"), sts_d, w=['shift'])
            for l in range(L):
                if kind == 'p':
                    psrc = lambda l, t0=t0, T=T: pp_d[l, :, :, t0:t0 + T].rearrange("k p t -> p k t")
                    layer_tile(l, T, first, first, wkp_d, wkp_d, psrc)
                else:
                    psrc = lambda l: psm_d[l].rearrange("k p t -> p k t")
                    layer_tile(l, T, first, False, stw_d, wks_d, psrc)
            m = smark()
            YO = salloc([KC, T], F32)
            rmsnorm(T, NFIN[:], 'nfin', lambda kc: YO.ap[:, kc, :], lambda kc: list(YO.keys))
            spdma(ydst[:, :, t0:t0 + T].rearrange("c p t -> p c t"), YO.ap, r=[YO])
            srelease(m)
            if last:
                spdma(shp_d if kind == 'p' else shs_d, SHIFT[:].rearrange("p l q -> p (l q)"), r=['shift'])
                spdma(pop_d if kind == 'p' else pos_d, POOLST[:].rearrange("p l c t -> p (l c t)"), r=['poolst'])

        ntile = TP // TILE
        for ti in range(ntile):
            run_segment('p', ti * TILE, TILE, ti == 0, ti == ntile - 1)
        run_segment('s', 0, TS, True, True)

        cnt = S.analyze()
        sems = {k: es.enter_context(nc.semaphore("sem_%s_%s" % k)) for k in cnt}
        block = es.enter_context(nc.Block())
        S.emit(sems, block)
    return nc, len(S.ops)


def rw_cols():
    cols = []
    cols.append(list(range(3072, 3136)) + [-1] * 64)
    cols.append(list(range(3136, 3200)) + [-1] * 64)
    cols.append(list(range(3200, 3328)))
    cols.append(list(range(3328, 3360)) + [-1] * 96)
    for j in range(8):
        for base in (0, 1024, 2048):
            cols.append(list(range(base + 128 * j, base + 128 * (j + 1))))
    return np.array(cols, dtype=np.int64)


def make_consts():
    c = np.zeros((128, NCST), np.float32)
    c[:, K_ID:K_ID + 128] = np.eye(128)
    for m in range(128):
        c[(m + 64) % 128, K_PERM + m] = 1.0
    c[:, K_ONES:K_ONES + 128] = 1.0
    p = np.arange(128)
    c[:, K_BLK2:K_BLK2 + 128] = (p[:, None] // 64 == p[None, :] // 64)
    c[:, K_BO2:K_BO2 + 2] = (p[:, None] // 64 == np.arange(2)[None, :])
    s = np.arange(64)[:, None]; t = np.arange(64)[None, :]
    strict = (t > s).astype(np.float32); incl = (t >= s).astype(np.float32)
    c[0:64, K_MG:K_MG + 256] = np.concatenate([strict, incl, strict, incl], axis=1)
    c[0:64, K_MX:K_MX + 64] = (s > t)
    rst = np.ones(512, np.float32); rst[::64] = 0.0
    c[:, K_RST:K_RST + 512] = rst[None, :]
    for gi, win in enumerate((2, 4, 8, 16)):
        c[:, K_RC + gi * 16:K_RC + (gi + 1) * 16] = (1.0 / np.minimum(np.arange(16) + 1, win))[None, :]
    return c


def kblocks(w, kc_n):
    K, N = w.shape
    return np.ascontiguousarray(w.reshape(kc_n, 128, N // 128, 128).transpose(2, 1, 0, 3).reshape(N // 128, 128, kc_n * 128))


def layout_weights(inp, L):
    f = lambda k: np.asarray(inp[k], dtype=np.float32)
    cols = rw_cols()
    out = {}
    w_in = f('w_in')
    win_b = np.zeros((L, NWIN, 128, 2048), np.float32)
    lora_b = np.zeros((L, 2, 128, 2048), np.float32)
    prm = np.zeros((L, 128, NPRM), np.float32)
    mu = f('mu_shift')
    for l in range(L):
        wl = w_in[l]
        win_b[l, 0:8] = kblocks(wl[:, 0:1024], KC)
        rwm = np.zeros((D, NQ * 128), np.float32)
        cc = cols.reshape(-1)
        ok = cc >= 0
        rwm[:, ok] = wl[:, 1024 + cc[ok]]
        win_b[l, 8:36] = kblocks(rwm, KC)
        win_b[l, 36:52] = kblocks(wl[:, 1024 + 3360:1024 + 3360 + 2048], KC)
        win_b[l, 52:68] = kblocks(wl[:, 1024 + 3360 + 2048:], KC)
        lora_b[l, 0, 0:64, 0:1024] = f('w_decay_up')[l]
        lora_b[l, 0, 0:64, 1024:2048] = f('w_aaa_up')[l]
        lora_b[l, 1, :, 0:1024] = f('w_gate_up')[l][0:128]
        lora_b[l, 1, 0:32, 1024:2048] = f('w_gate_up')[l][128:160]
        pc = lambda v: v.reshape(-1, 128).T
        prm[l, :, C_NM:C_NM + 16] = pc(f('norm_mix')[l]); prm[l, :, C_NF:C_NF + 16] = pc(f('norm_ffn')[l]); prm[l, :, C_NP:C_NP + 16] = pc(f('norm_ple')[l])
        mum = np.zeros(NQ * 128, np.float32); mum[ok] = mu[l][cc[ok]]
        prm[l, :, C_MU:C_MU + NQ] = mum.reshape(NQ, 128).T
        prm[l, :, C_PSC:C_PSC + 8] = pc(f('pool_scale')[l]); prm[l, :, C_W0:C_W0 + 8] = pc(f('w0')[l]); prm[l, :, C_A0:C_A0 + 8] = pc(f('a0')[l])
        prm[l, :, C_KK:C_KK + 8] = pc(f('k_k')[l]); prm[l, :, C_KA:C_KA + 8] = pc(f('k_a')[l]); prm[l, :, C_RK:C_RK + 8] = pc(f('r_k')[l].reshape(-1))
    out['w_in_b'] = win_b; out['lora_b'] = lora_b; out['prm'] = prm
    pw = f('pool_w').reshape(L, 4, 2, 128, 256).transpose(0, 3, 1, 2, 4).reshape(L, 128, 2048)
    out['pw_b'] = np.ascontiguousarray(pw)

    def two_oc(w):
        r = np.stack([kblocks(w[l], 8) for l in range(L)])
        return np.ascontiguousarray(r.reshape(L, 8, 2, 128, 1024).transpose(0, 1, 3, 2, 4).reshape(L, 8, 128, 2048))
    out['ppool_b'] = two_oc(f('proj_pool')); out['prwkv_b'] = two_oc(f('proj_rwkv'))
    out['wo_b'] = np.stack([kblocks(f('w_out')[l], KC) for l in range(L)])
    out['wg_b'] = np.stack([kblocks(f('w_ffn_gate')[l], KC) for l in range(L)])
    out['wu_b'] = np.stack([kblocks(f('w_ffn_up')[l], KC) for l in range(L)])
    wd = f('w_ffn_down')
    out['wd_b'] = np.stack([np.stack([kblocks(wd[l][fg * FGC * 128:(fg + 1) * FGC * 128], FGC) for fg in range(FG)]) for l in range(L)])
    out['pg_b'] = np.stack([kblocks(f('w_ple_gate')[l], KC) for l in range(L)])
    pj = np.stack([kblocks(f('w_ple_proj')[l], 2) for l in range(L)])
    out['pj_b'] = np.ascontiguousarray(pj.reshape(L, 2, 8, 128, 256).transpose(0, 1, 3, 2, 4).reshape(L, 2, 128, 2048))
    out['nfin'] = np.ascontiguousarray(f('norm_final').reshape(16, 128).T)
    out['gn'] = np.ascontiguousarray(np.concatenate([f('gn_gain'), f('gn_bias')], axis=1))
    out['cst'] = make_consts()
    return out, cols


_CACHE = {}


def run(inputs, L, TP, TS, TILE, ncores):
    key = (L, TP, TS, TILE)
    if key not in _CACHE:
        _CACHE[key] = build_program(L, TP, TS, TILE)[0]
    nc = _CACHE[key]
    f = lambda k: np.asarray(inputs[k], dtype=np.float32)
    shared, cols = layout_weights(inputs, L)
    cc = cols.reshape(-1); ok = cc >= 0
    in_maps = []
    for b in range(ncores):
        m = dict(shared)
        m['xp'] = np.ascontiguousarray(f('x_prompt')[b].T.reshape(KC, 128, TP))
        m['xs'] = np.ascontiguousarray(f('x_sample')[b].T.reshape(KC, 128, TS))
        m['pp'] = np.ascontiguousarray(f('p_prompt')[:, b].transpose(0, 2, 1).reshape(L, 2, 128, TP))
        m['psm'] = np.ascontiguousarray(f('p_sample')[:, b].transpose(0, 2, 1).reshape(L, 2, 128, TS))
        ss = np.zeros((L, NQ * 128), np.float32); ss[:, ok] = f('state_shift')[:, b][:, cc[ok]]
        m['st_shift'] = np.ascontiguousarray(ss.reshape(L, NQ, 128).transpose(2, 0, 1).reshape(128, L * NQ))
        sp = f('state_pool')[:, b]
        m['st_pool'] = np.ascontiguousarray(sp.reshape(L, 15, 8, 128).transpose(3, 0, 2, 1).reshape(128, L * 120))
        sw = f('state_wkv')[:, b]
        m['st_wkv'] = np.ascontiguousarray(sw.transpose(0, 3, 1, 2).reshape(L, 64, 1024))
        in_maps.append(m)
    res = run_bass_kernel_spmd(nc, in_maps, core_ids=list(range(ncores)))
    R = res.results
    inv = np.full(3360, -1, np.int64); inv[cc[ok]] = np.nonzero(ok)[0]

    def unshift(a):
        v = a.reshape(128, L, NQ).transpose(1, 2, 0).reshape(L, NQ * 128)
        return v[:, inv]
    y_p = np.stack([R[b]['yp'].reshape(D, TP).T for b in range(ncores)])
    y_s = np.stack([R[b]['ys'].reshape(D, TS).T for b in range(ncores)])
    outs = [y_p, y_s]
    for sfx in ('p', 's'):
        sh = np.stack([unshift(R[b]['sh_' + sfx]) for b in range(ncores)], axis=1)
        po = np.stack([R[b]['po_' + sfx].reshape(128, L, 8, 15).transpose(1, 3, 2, 0).reshape(L, 15, 1024) for b in range(ncores)], axis=1)
        wk = np.stack([R[b]['wk_' + sfx].reshape(L, 64, 16, 64).transpose(0, 2, 3, 1) for b in range(ncores)], axis=1)
        outs += [sh, po, wk]
    return tuple(np.ascontiguousarray(o.astype(np.float32)) for o in outs)


def kernel(**inputs):
    return run(inputs, 4, 2048, 64, 512, 8)
```

```python
import contextlib
import numpy as np
import concourse.bass as bass
import concourse.mybir as mybir
from concourse.bass_utils import run_bass_kernel_spmd

F32 = mybir.dt.float32
BF16 = mybir.dt.bfloat16
AF = mybir.ActivationFunctionType
ALU = mybir.AluOpType
AX = mybir.AxisListType

D = 2048
KC = 16
POOLW = 1024
RWW = 1024
NHEAD = 16
DFF = 5632
FC = 44
FG = 4
FGC = 11
PLE = 256
NQ = 28
NWIN = 68
NPRM = 124
C_NM, C_NF, C_NP, C_MU, C_PSC, C_W0, C_A0, C_KK, C_KA, C_RK = 0, 16, 32, 48, 76, 84, 92, 100, 108, 116
K_ID, K_PERM, K_ONES, K_BLK2, K_BO2, K_MG, K_MX, K_RST, K_RC = 0, 128, 256, 384, 512, 514, 770, 834, 1346
NCST = 1410
NORM_EPS = 1e-6
GN_EPS = 64e-5
C0 = float(np.exp(-0.5))
NSLOT = 6
SCR_KB = 72


class Sched:
    QUEUES = ['pe', 'act', 'dve', 'pool', 'sp']

    def __init__(self):
        self.ops = []

    def op(self, q, fn, reads=(), writes=(), stream=None):
        self.ops.append(dict(q=q, fn=fn, reads=reads, writes=writes, stream=stream))

    def analyze(self):
        ops = self.ops
        last_w = {}
        readers = {}
        last_on_stream = {}
        for i, o in enumerate(ops):
            deps = set()
            for r in o['reads']:
                j = last_w.get(r)
                if j is not None:
                    deps.add(j)
            for w in o['writes']:
                j = last_w.get(w)
                if j is not None:
                    deps.add(j)
                rd = readers.get(w)
                if rd:
                    deps.update(rd.values())
            if o['stream'] is not None:
                j = last_on_stream.get(o['stream'])
                if j is not None:
                    deps.add(j)
                last_on_stream[o['stream']] = i
            deps.discard(i)
            ck = ('s', o['stream']) if o['stream'] is not None else ('q', o['q'])
            o['ck'] = ck
            for r in o['reads']:
                readers.setdefault(r, {})[ck] = i
            for w in o['writes']:
                last_w[w] = i
                readers[w] = {}
            o['deps'] = deps
            o['sig'] = o['stream'] is not None
        for o in ops:
            nd = set()
            for d in o['deps']:
                p = ops[d]
                if p['ck'] == ('q', 'pe') and o['ck'] == ('q', 'pe'):
                    continue
                nd.add(d)
                p['sig'] = True
            o['deps'] = nd
        cnt = {}
        for o in ops:
            if o['sig']:
                k = o['ck']
                cnt[k] = cnt.get(k, 0) + (16 if o['stream'] is not None else 1)
                o['semval'] = cnt[k]
        self.final_counts = cnt
        known = {q: {} for q in self.QUEUES}
        for o in ops:
            need = {}
            for d in o['deps']:
                p = ops[d]
                k = p['ck']
                v = p['semval']
                if v > need.get(k, 0):
                    need[k] = v
            waits = []
            kq = known[o['q']]
            for k, v in need.items():
                if kq.get(k, 0) >= v:
                    continue
                kq[k] = v
                waits.append((k, v))
            o['waits'] = waits
        return cnt

    def emit(self, sems, block):
        ops = self.ops
        byq = {q: [o for o in ops if o['q'] == q] for q in self.QUEUES}
        dec = {'pe': block.tensor, 'act': block.scalar, 'dve': block.vector, 'pool': block.gpsimd, 'sp': block.sync}
        final = list(self.final_counts.items())
        for q in self.QUEUES:
            lst = byq[q]

            def body(eng, lst=lst, q=q):
                for o in lst:
                    for (k, v) in o['waits']:
                        eng.wait_ge(sems[k], v)
                    ins = o['fn'](eng)
                    if o['sig']:
                        ins.then_inc(sems[o['ck']], 16 if o['stream'] is not None else 1)
                if q == 'sp':
                    for k, v in final:
                        eng.wait_ge(sems[k], v)
            dec[q](body)


class Buf:
    def __init__(self, ap, keys):
        self.ap = ap
        self.keys = tuple(keys)


def _keys(items):
    out = []
    for it in items:
        if isinstance(it, Buf):
            out.extend(it.keys)
        elif isinstance(it, list):
            out.extend(_keys(it))
        else:
            out.append(it)
    return tuple(out)


def build_program(L, TP, TS, TILE):
    nc = bass.Bass("TRN2", target_bir_lowering=False)
    S = Sched()

    def OP(q, fn, r=(), w=(), stream=None):
        S.op(q, fn, _keys(r), _keys(w), stream)

    def din(name, shape):
        return nc.dram_tensor(name, list(shape), F32, kind="ExternalInput").ap()

    def dout(name, shape):
        return nc.dram_tensor(name, list(shape), F32, kind="ExternalOutput").ap()

    xp_d = din("xp", [KC, 128, TP]); xs_d = din("xs", [KC, 128, TS])
    pp_d = din("pp", [L, 2, 128, TP]); psm_d = din("psm", [L, 2, 128, TS])
    sts_d = din("st_shift", [128, L * NQ]); stp_d = din("st_pool", [128, L * 120]); stw_d = din("st_wkv", [L, 64, 1024])
    prm_d = din("prm", [L, 128, NPRM]); nfin_d = din("nfin", [128, KC]); gn_d = din("gn", [L, 2048]); cst_d = din("cst", [128, NCST])
    win_d = din("w_in_b", [L, NWIN, 128, 2048]); pw_d = din("pw_b", [L, 128, 2048]); lora_d = din("lora_b", [L, 2, 128, 2048])
    ppj_d = din("ppool_b", [L, 8, 128, 2048]); prw_d = din("prwkv_b", [L, 8, 128, 2048]); wo_d = din("wo_b", [L, 16, 128, 2048])
    wg_d = din("wg_b", [L, FC, 128, 2048]); wu_d = din("wu_b", [L, FC, 128, 2048]); wd_d = din("wd_b", [L, FG, 16, 128, FGC * 128])
    pg_d = din("pg_b", [L, 16, 128, 2048]); pj_d = din("pj_b", [L, 2, 128, 2048])
    yp_d = dout("yp", [KC, 128, TP]); ys_d = dout("ys", [KC, 128, TS])
    shp_d = dout("sh_p", [128, L * NQ]); shs_d = dout("sh_s", [128, L * NQ])
    pop_d = dout("po_p", [128, L * 120]); pos_d = dout("po_s", [128, L * 120])
    wkp_d = dout("wk_p", [L, 64, 1024]); wks_d = dout("wk_s", [L, 64, 1024])

    es = contextlib.ExitStack()
    with es:
        def sbt(name, shape, dt):
            return es.enter_context(nc.sbuf_tensor(name, list(shape), dt))

        TM = TILE
        X = sbt("X", [128, KC, TM], F32)
        H = sbt("H", [128, KC, TM], BF16)
        YA = sbt("YA", [128, 8, TM], BF16)
        YB = sbt("YB", [128, 8, TM], BF16)
        ST = sbt("ST", [64, NHEAD, 64], F32)
        POOLST = sbt("POOLST", [128, L, 8, 15], F32)
        SHIFT = sbt("SHIFT", [128, L, NQ], F32)
        WR = sbt("WR", [128, NSLOT, 2048], BF16)
        LA = sbt("LA", [128, 2048], BF16)
        LB = sbt("LB", [128, 2048], BF16)
        CST = sbt("CST", [128, NCST], F32)
        CB = sbt("CB", [128, 514], BF16)
        PRM = sbt("PRM", [128, NPRM], F32)
        NFIN = sbt("NFIN", [128, KC], F32)
        SCR = sbt("SCR", [128, SCR_KB * 256], F32)
        SCRf = SCR[:]
        SCRb = SCR[:].bitcast(BF16)
        PQ = [es.enter_context(nc.psum_tensor("PQ%d" % i, [128, 1024], F32)) for i in range(4)]
        PQb = [p[:].bitcast(BF16) for p in PQ]

        IDb = CB[:, 0:128]; PERMb = CB[:, 128:256]; ONESb = CB[:, 256:384]; BLK2b = CB[:, 384:512]; BO2b = CB[:, 512:514]
        PERMf = CST[:, K_PERM:K_PERM + 128]

        scr_top = [0]

        def salloc(free_shape, dt, parts=128):
            n = int(np.prod(free_shape))
            nbytes = n * (4 if dt == F32 else 2)
            nslab = (nbytes + 1023) // 1024
            off = scr_top[0]
            scr_top[0] += nslab
            assert scr_top[0] <= SCR_KB, "scratch overflow %d" % scr_top[0]
            if dt == F32:
                ap = SCRf[0:parts, off * 256: off * 256 + n]
            else:
                ap = SCRb[0:parts, off * 512: off * 512 + n]
            if len(free_shape) == 2:
                ap = ap.rearrange("p (a b) -> p a b", b=free_shape[1])
            elif len(free_shape) == 3:
                ap = ap.rearrange("p (a b c) -> p a b c", b=free_shape[1], c=free_shape[2])
            elif len(free_shape) == 4:
                ap = ap.rearrange("p (a b c d) -> p a b c d", b=free_shape[1], c=free_shape[2], d=free_shape[3])
            return Buf(ap, [('scr', off + i) for i in range(nslab)])

        def smark():
            return scr_top[0]

        def srelease(m):
            scr_top[0] = m

        bank_rr = [0]

        def bank():
            b = bank_rr[0] % 8
            bank_rr[0] += 1
            return b

        def bank_ap(b):
            return PQ[b // 2][:, (b % 2) * 512:(b % 2) * 512 + 512]

        def bank_apb(b):
            return PQb[b // 2][:, (b % 2) * 1024:(b % 2) * 1024 + 1024]

        pair_rr = [0]

        def bpair():
            p = pair_rr[0] % 4
            pair_rr[0] += 1
            return p

        wr_ctr = [0]

        wr_busy = [False] * NSLOT

        def wfree(s):
            wr_busy[s] = False

        def wload(src_ap):
            n = src_ap.shape[-1]
            s = wr_ctr[0] % NSLOT
            wr_ctr[0] += 1
            assert not wr_busy[s], "weight ring slot still pinned"
            wr_busy[s] = True
            OP('pool', lambda e, s=s, src_ap=src_ap, n=n: e.dma_start(out=WR[:, s, 0:n], in_=src_ap), w=[('w', s)], stream='w%d' % s)
            return s

        sp_ctr = [0]

        def spdma(out, in_, r=(), w=()):
            st = 'sp%d' % (sp_ctr[0] % 4)
            sp_ctr[0] += 1
            OP('sp', lambda e, out=out, in_=in_: e.dma_start(out=out, in_=in_), r=r, w=w, stream=st)

        def pooldma(out, in_, r=(), w=()):
            OP('pool', lambda e, out=out, in_=in_: e.dma_start(out=out, in_=in_), r=r, w=w, stream='pm')

        def MM(out, lhsT, rhs, start, stop, r, w):
            OP('pe', lambda e, out=out, lhsT=lhsT, rhs=rhs, start=start, stop=stop: e.matmul(out, lhsT=lhsT, rhs=rhs, start=start, stop=stop), r=r, w=w)

        def TR(out, in_, ident, r, w):
            OP('pe', lambda e, out=out, in_=in_, ident=ident: e.transpose(out, in_, ident), r=r, w=w)

        def ACT(out, in_, func, r, w, bias=0.0, scale=1.0):
            OP('act', lambda e, out=out, in_=in_, func=func, bias=bias, scale=scale: e.activation(out=out, in_=in_, func=func, bias=bias, scale=scale), r=r, w=w)

        def TT(out, in0, in1, op, r, w, q='dve'):
            OP(q, lambda e, out=out, in0=in0, in1=in1, op=op: e.tensor_tensor(out=out, in0=in0, in1=in1, op=op), r=r, w=w)

        def TS_(out, in0, s1, s2, op0, op1, r, w):
            OP('dve', lambda e, out=out, in0=in0, s1=s1, s2=s2, op0=op0, op1=op1: e.tensor_scalar(out=out, in0=in0, scalar1=s1, scalar2=s2, op0=op0, op1=op1), r=r, w=w)

        def STT(out, in0, sc, in1, op0, op1, r, w):
            OP('dve', lambda e, out=out, in0=in0, sc=sc, in1=in1, op0=op0, op1=op1: e.scalar_tensor_tensor(out=out, in0=in0, scalar=sc, in1=in1, op0=op0, op1=op1), r=r, w=w)

        def CP(out, in_, r, w, q='dve'):
            if q == 'act':
                ACT(out, in_, AF.Copy, r, w)
            else:
                OP('dve', lambda e, out=out, in_=in_: e.tensor_copy(out=out, in_=in_), r=r, w=w)

        def MSET(ap, val, w):
            OP('dve', lambda e, ap=ap, val=val: e.memset(ap, val), w=w)

        spdma(CST[:], cst_d, w=['cst'])
        spdma(NFIN[:], nfin_d, w=['nfin'])
        CP(CB[:, 0:514], CST[:, 0:514], r=['cst'], w=['cb'])
        MASKG = CST[0:64, K_MG:K_MG + 256]
        MASKX = CST[0:64, K_MX:K_MX + 64]
        I64f = CST[0:64, K_ID:K_ID + 64]
        RC = CST[:, K_RC:K_RC + 64].rearrange("p (g t) -> p g t", t=16)

        def rmsnorm(T, gsrc, gkey, dst_fn, dst_key_fn):
            m = smark()
            sqb = [salloc([T], BF16) for _ in range(4)]
            rt = salloc([T], F32)
            rstd = salloc([T], F32)
            b = bank()
            pk = ('ps', b)
            for kc in range(KC):
                sq = sqb[kc % 4]
                ACT(sq.ap, X[:, kc, 0:T], AF.Square, r=[('x', kc)], w=[sq])
                MM(bank_ap(b)[:, 0:T], ONESb, sq.ap, kc == 0, kc == KC - 1, r=[sq, 'cb'], w=[pk])
            ACT(rt.ap, bank_ap(b)[:, 0:T], AF.Sqrt, r=[pk], w=[rt], bias=NORM_EPS, scale=1.0 / D)
            OP('dve', lambda e: e.reciprocal(out=rstd.ap, in_=rt.ap), r=_keys([rt]), w=_keys([rstd]))
            for kc in range(KC):
                STT(dst_fn(kc), X[:, kc, 0:T], gsrc[:, kc:kc + 1], rstd.ap, ALU.mult, ALU.mult,
                    r=[('x', kc), gkey, rstd], w=list(dst_key_fn(kc)))
            srelease(m)

        def proj_block(T, blk_ap, n_k, rhs_fn, rhs_keys_fn, lhs_off=0, m=128):
            s = wload(blk_ap)
            b = bank()
            for kc in range(n_k):
                MM(bank_ap(b)[0:m, 0:T], WR[:, s, lhs_off + kc * 128: lhs_off + kc * 128 + m], rhs_fn(kc), kc == 0, kc == n_k - 1,
                   r=[('w', s)] + list(rhs_keys_fn(kc)), w=[('ps', b)])
            wfree(s)
            return b

        def layer_tile(l, T, first, is_prompt_first, wk_src, wk_dst, p_src):
            nch = T // 64
            W = 15 + T
            spdma(PRM[:], prm_d[l], w=['prm'])
            pooldma(LA[:], lora_d[l, 0], w=['la'])
            pooldma(LB[:], lora_d[l, 1], w=['lb'])
            if is_prompt_first:
                MSET(ST[:], 0.0, w=['st'])
            else:
                spdma(ST[:].rearrange("p h v -> p (h v)"), wk_src[l], r=[('wk', l)] if wk_src is wkp_d else [], w=['st'])

            def pcol(c):
                return PRM[:, c:c + 1]

            rmsnorm(T, PRM[:, C_NM:C_NM + KC], 'prm', lambda kc: H[:, kc, 0:T], lambda kc: [('h', kc)])
            hr = lambda kc: H[:, kc, 0:T]
            hk = lambda kc: [('h', kc)]

            mB = smark()
            tw = salloc([T], BF16); ad = salloc([T], BF16); sg1 = salloc([T], BF16); sg2 = salloc([T], BF16)
            gnt = salloc([2, 256], F32, parts=64)
            rwb_all = salloc([3, T + 1], F32)
            rwb = [Buf(rwb_all.ap[:, i, :], rwb_all.keys) for i in range(3)]
            xm = [salloc([T], F32) for _ in range(3)]
            tmpA = salloc([T], F32)

            def rw_chunk(q, slot):
                b = proj_block(T, win_d[l, 8 + q], KC, hr, hk)
                rb = rwb[slot]
                ACT(rb.ap[:, 1:T + 1], bank_ap(b)[:, 0:T], AF.Copy, r=[('ps', b)], w=[rb])
                CP(rb.ap[:, 0:1], SHIFT[:, l, q:q + 1], r=['shift'], w=[rb])
                TT(tmpA.ap, rb.ap[:, 0:T], rb.ap[:, 1:T + 1], ALU.subtract, r=[rb], w=[tmpA])
                STT(xm[slot].ap, tmpA.ap, pcol(C_MU + q), rb.ap[:, 1:T + 1], ALU.mult, ALU.add, r=[tmpA, rb, 'prm'], w=[xm[slot]])
                CP(SHIFT[:, l, q:q + 1], rb.ap[:, T:T + 1], r=[rb], w=['shift'])

            rw_chunk(0, 0)
            ACT(tw.ap[0:64, :], xm[0].ap[0:64, :], AF.Tanh, r=[xm[0]], w=[tw])
            rw_chunk(1, 1)
            CP(ad.ap[0:64, :], xm[1].ap[0:64, :], r=[xm[1]], w=[ad], q='act')
            rw_chunk(2, 2)
            ACT(sg1.ap, xm[2].ap, AF.Sigmoid, r=[xm[2]], w=[sg1])
            rw_chunk(3, 0)
            ACT(sg2.ap, xm[0].ap, AF.Sigmoid, r=[xm[0]], w=[sg2])

            for qd in range(4):
                mQ = smark()
                spdma(gnt.ap[:, 0, :], gn_d[l:l + 1, qd * 256:(qd + 1) * 256].partition_broadcast(64), w=[gnt])
                spdma(gnt.ap[:, 1, :], gn_d[l:l + 1, 1024 + qd * 256:1024 + (qd + 1) * 256].partition_broadcast(64), w=[gnt])
                ARs = salloc([2, nch, 128], BF16); ARo = salloc([2, nch, 128], BF16)
                BKs = salloc([2, nch, 128], BF16); BKo = salloc([2, nch, 128], BF16)
                BKc = salloc([2, nch, 128], BF16)
                VB = salloc([2, T], BF16); RK = salloc([2, T], BF16)
                PCs = salloc([2, 8], F32); PCH = salloc([nch, 4], F32, parts=64)
                MSET(PCs.ap, 0.0, w=[PCs])
                mP = smark()
                t = [salloc([T], F32) for _ in range(11)]
                sqk = salloc([T], BF16)
                for jj in range(2):
                    j = qd * 2 + jj
                    for i3 in range(3):
                        rw_chunk(4 + 3 * j + i3, i3)
                    XR, XK, XV = xm[0], xm[1], xm[2]
                    sigz, cum, exc, pinc, pinv, pexc, av, kkr, kk, kh, tb = t
                    b = bank()
                    MM(bank_ap(b)[:, 0:T], LA[0:64, j * 128:(j + 1) * 128], tw.ap[0:64, :], True, True, r=['la', tw], w=[('ps', b)])
                    ACT(sigz.ap, bank_ap(b)[:, 0:T], AF.Sigmoid, r=[('ps', b), 'prm'], w=[sigz], bias=pcol(C_W0 + j))
                    OP('dve', lambda e, cum=cum, sigz=sigz: e.tensor_tensor_scan(out=cum.ap, data0=CST[:, K_RST:K_RST + T], data1=sigz.ap,
                                                                                 initial=0.0, op0=ALU.mult, op1=ALU.add),
                       r=_keys([sigz, 'cst']), w=_keys([cum]))
                    TT(exc.ap, cum.ap, sigz.ap, ALU.subtract, r=[cum, sigz], w=[exc])
                    ACT(pinc.ap, cum.ap, AF.Exp, r=[cum], w=[pinc], scale=-C0)
                    ACT(pinv.ap, cum.ap, AF.Exp, r=[cum], w=[pinv], scale=C0)
                    ACT(pexc.ap, exc.ap, AF.Exp, r=[exc], w=[pexc], scale=-C0)
                    CP(PCs.ap[:, jj, 0:nch], pinc.ap.rearrange("p (c t) -> p c t", t=64)[:, :, 63], r=[pinc], w=[PCs])
                    b = bank()
                    MM(bank_ap(b)[:, 0:T], LA[0:64, 1024 + j * 128:1024 + (j + 1) * 128], ad.ap[0:64, :], True, True, r=['la', ad], w=[('ps', b)])
                    ACT(av.ap, bank_ap(b)[:, 0:T], AF.Sigmoid, r=[('ps', b), 'prm'], w=[av], bias=pcol(C_A0 + j))
                    TS_(kkr.ap, XK.ap, pcol(C_KK + j), None, ALU.mult, ALU.bypass, r=[XK, 'prm'], w=[kkr])
                    TT(sqk.ap, kkr.ap, kkr.ap, ALU.mult, r=[kkr], w=[sqk])
                    b = bank()
                    MM(bank_ap(b)[:, 0:T], BLK2b, sqk.ap, True, True, r=['cb', sqk], w=[('ps', b)])
                    ACT(exc.ap, bank_ap(b)[:, 0:T], AF.Sqrt, r=[('ps', b)], w=[exc])
                    TS_(exc.ap, exc.ap, 1e-12, None, ALU.max, ALU.bypass, r=[exc], w=[exc])
                    OP('dve', lambda e, exc=exc: e.reciprocal(out=exc.ap, in_=exc.ap), r=_keys([exc]), w=_keys([exc]))
                    TT(kk.ap, kkr.ap, exc.ap, ALU.mult, r=[kkr, exc], w=[kk])
                    TS_(tb.ap, av.ap, 1.0, pcol(C_KA + j), ALU.subtract, ALU.mult, r=[av, 'prm'], w=[tb])
                    STT(kh.ap, tb.ap, 1.0, XK.ap, ALU.add, ALU.mult, r=[tb, XK], w=[kh])
                    c3 = lambda ap: ap.rearrange("p (c t) -> p c t", t=64)
                    STT(ARs.ap[:, jj, :, 0:64], c3(kk.ap), -1.0, c3(pexc.ap), ALU.mult, ALU.mult, r=[kk, pexc], w=[ARs])
                    TT(ARs.ap[:, jj, :, 64:128], c3(XR.ap), c3(pinc.ap), ALU.mult, r=[XR, pinc], w=[ARs])
                    STT(RK.ap[:, jj, :], XR.ap, pcol(C_RK + j), kh.ap, ALU.mult, ALU.mult, r=[XR, kh, 'prm'], w=[RK])
                    CP(VB.ap[:, jj, :], XV.ap, r=[XV], w=[VB], q='act')
                    TT(tb.ap, kk.ap, av.ap, ALU.mult, r=[kk, av], w=[tb])
                    TT(tb.ap, tb.ap, pinv.ap, ALU.mult, r=[tb, pinv], w=[tb])
                    CP(BKs.ap[:, jj, :, 0:64], c3(tb.ap), r=[tb], w=[BKs], q='act')
                    pcb = PCs.ap[:, jj, 0:nch].unsqueeze(2).to_broadcast([128, nch, 64])
                    TT(BKc.ap[:, jj, :, 0:64], c3(tb.ap), pcb, ALU.mult, r=[tb, PCs], w=[BKc])
                    TT(kh.ap, kh.ap, pinv.ap, ALU.mult, r=[kh, pinv], w=[kh])
                    CP(BKs.ap[:, jj, :, 64:128], c3(kh.ap), r=[kh], w=[BKs], q='act')
                    TT(BKc.ap[:, jj, :, 64:128], c3(kh.ap), pcb, ALU.mult, r=[kh, PCs], w=[BKc])
                    for (src, dst) in ((ARs, ARo), (BKs, BKo)):
                        p2 = bpair()
                        n = nch * 128
                        for h0 in range(0, n, 512):
                            hn = min(512, n - h0)
                            MM(PQ[p2][:, h0:h0 + hn], PERMb, src.ap[:, jj, :, :].rearrange("p c t -> p (c t)")[:, h0:h0 + hn], True, True,
                               r=['cb', src], w=[('ps', 2 * p2), ('ps', 2 * p2 + 1)])
                        CP(dst.ap[0:64, jj, :, :].rearrange("p c t -> p (c t)"), PQ[p2][0:64, 0:n], r=[('ps', 2 * p2), ('ps', 2 * p2 + 1)], w=[dst], q='act')
                    b = bank()
                    MM(bank_ap(b)[:, 0:8], PERMf, PCs.ap[:, jj, :], True, True, r=['cst', PCs], w=[('ps', b)])
                    CP(PCH.ap[:, :, 2 * jj], PCs.ap[0:64, jj, 0:nch], r=[PCs], w=[PCH])
                    CP(PCH.ap[:, :, 2 * jj + 1], bank_ap(b)[0:64, 0:nch], r=[('ps', b)], w=[PCH])
                srelease(mP)
                TOKp = [salloc([2, 3, 128], BF16, parts=64) for _ in range(2)]
                GTp = [salloc([4, 4, 64], BF16, parts=64) for _ in range(2)]
                Xpp = [[salloc([4, 64], BF16, parts=64) for _ in range(2)] for _ in range(2)]
                MTpp = [[salloc([4, 2, 64], BF16, parts=64) for _ in range(2)] for _ in range(2)]
                Wb = salloc([4, 64], BF16, parts=64); Ub = salloc([4, 64], BF16, parts=64); STb = salloc([4, 64], BF16, parts=64)
                ysq = salloc([4, 64], F32, parts=64); yn = salloc([4, 64], F32, parts=64); bon = salloc([4, 64], F32, parts=64)
                stm = salloc([4, 64], F32, parts=64)
                yout = salloc([256], BF16, parts=64)
                sm = salloc([8, 4], F32, parts=64)
                STq = ST[:, qd * 4:(qd + 1) * 4, :]
                CP(STb.ap[0:64], STq, r=['st'], w=[STb])

                def ARh(hq, c):
                    jj, hp = hq // 2, hq % 2
                    return (ARs if hp == 0 else ARo).ap[0:64, jj, c, :], (ARs if hp == 0 else ARo)

                def BKh(hq, c):
                    jj, hp = hq // 2, hq % 2
                    return (BKs if hp == 0 else BKo).ap[0:64, jj, c, :], (BKs if hp == 0 else BKo)

                def partA(c):
                    par = c % 2
                    TOKc, GT, Xp, MTp = TOKp[par], GTp[par], Xpp[par], MTpp[par]
                    cs = slice(c * 64, (c + 1) * 64)
                    b = 0
                    pk = ('ps', b)
                    tv = bank_apb(b)[0:64, 0:768].rearrange("p (j k c) -> p j k c", k=3, c=128)
                    for jj in range(2):
                        TR(tv[:, jj, 0, :], VB.ap[:, jj, cs], IDb, r=[VB, 'cb'], w=[pk])
                        TR(tv[:, jj, 1, :], BKc.ap[:, jj, c, 0:64], IDb, r=[BKc, 'cb'], w=[pk])
                        TR(tv[:, jj, 2, :], BKc.ap[:, jj, c, 64:128], IDb, r=[BKc, 'cb'], w=[pk])
                    p2 = 1
                    gk = [('ps', 2 * p2), ('ps', 2 * p2 + 1)]
                    gv = PQ[p2][0:64, :].rearrange("p (h x) -> p h x", x=256)
                    for hq in range(4):
                        ar, arb = ARh(hq, c)
                        bk, bkb = BKh(hq, c)
                        MM(gv[:, hq, 0:128], bk[:, 0:64], ar, True, True, r=[arb, bkb], w=gk)
                        MM(gv[:, hq, 128:256], bk[:, 64:128], ar, True, True, r=[arb, bkb], w=gk)
                    bx = 1
                    xk = ('ps', bx)
                    xv = bank_ap(bx)[0:64, 0:256].rearrange("p (h s) -> p h s", s=64)
                    for hq in range(4):
                        ar, arb = ARh(hq, c)
                        bk, bkb = BKh(hq, c)
                        MM(xv[:, hq, :], ar[:, 0:64], bk[:, 0:64], True, True, r=[arb, bkb], w=[xk])
                    yield
                    CP(TOKc.ap[0:64], tv, r=[pk], w=[TOKc], q='act')
                    TT(GT.ap[0:64].rearrange("p h b t -> p h (b t)"), gv, MASKG.unsqueeze(1).to_broadcast([64, 4, 256]), ALU.mult, r=gk + ['cst'], w=[GT])
                    Xc, MTc = Xp[0], MTp[0]
                    TT(Xc.ap[0:64], xv, MASKX.unsqueeze(1).to_broadcast([64, 4, 64]), ALU.mult, r=[xk, 'cst'], w=[Xc])
                    CP(MTc.ap[0:64, :, 0, :], GT.ap[0:64, :, 0, :], r=[GT], w=[MTc], q='act')
                    CP(MTc.ap[0:64, :, 1, :], I64f.unsqueeze(1).to_broadcast([64, 4, 64]), r=['cst'], w=[MTc])
                    yield
                    for lev in range(6):
                        Xn, MTn = Xp[(lev + 1) % 2], MTp[(lev + 1) % 2]
                        b1, b2 = (0, 1) if lev % 2 == 0 else (2, 3)
                        o1 = bank_ap(b1)[0:64, 0:512].rearrange("p (h x) -> p h x", x=128)
                        o2 = bank_ap(b2)[0:64, 0:256].rearrange("p (h s) -> p h s", s=64)
                        last = lev == 5
                        for hq in range(4):
                            if last:
                                MM(o1[:, hq, 64:128], Xc.ap[0:64, hq, :], MTc.ap[0:64, hq, 1, :], True, True, r=[Xc, MTc], w=[('ps', b1)])
                            else:
                                MM(o1[:, hq, :], Xc.ap[0:64, hq, :], MTc.ap[0:64, hq, :, :].rearrange("p a t -> p (a t)"), True, True, r=[Xc, MTc], w=[('ps', b1)])
                                MM(o2[:, hq, :], MTc.ap[0:64, hq, 0, :], Xc.ap[0:64, hq, :], True, True, r=[Xc, MTc], w=[('ps', b2)])
                        yield
                        TT(MTn.ap[0:64, :, 1, :], o1[:, :, 64:128], MTc.ap[0:64, :, 1, :], ALU.add, r=[('ps', b1), MTc], w=[MTn])
                        if not last:
                            CP(MTn.ap[0:64, :, 0, :], o1[:, :, 0:64], r=[('ps', b1)], w=[MTn], q='act')
                            CP(Xn.ap[0:64], o2, r=[('ps', b2)], w=[Xn], q='act')
                        Xc, MTc = Xn, MTn
                        yield
                    assert MTc is MTp[0]

                def partB(c):
                    par = c % 2
                    TOKc, GT, TTf = TOKp[par], GTp[par], MTpp[par][0]
                    cs = slice(c * 64, (c + 1) * 64)
                    vtok = lambda hq: TOKc.ap[0:64, hq // 2, 0, (hq % 2) * 64:(hq % 2) * 64 + 64]
                    bctok = lambda hq: TOKc.ap[0:64, hq // 2, 1, (hq % 2) * 64:(hq % 2) * 64 + 64]
                    kctok = lambda hq: TOKc.ap[0:64, hq // 2, 2, (hq % 2) * 64:(hq % 2) * 64 + 64]
                    bw = 4; wk_ = ('ps', bw)
                    wv = bank_ap(bw)[0:64, 0:256].rearrange("p (h v) -> p h v", v=64)
                    for hq in range(4):
                        ar, arb = ARh(hq, c)
                        MM(wv[:, hq, :], ar[:, 0:64], STb.ap[0:64, hq, :], True, False, r=[arb, STb], w=[wk_])
                        MM(wv[:, hq, :], GT.ap[0:64, hq, 2, :], vtok(hq), False, True, r=[GT, TOKc], w=[wk_])
                    bb = 5; bk_ = ('ps', bb)
                    for jj in range(2):
                        MM(bank_ap(bb)[0:64, 256 + 2 * jj:256 + 2 * jj + 2], RK.ap[:, jj, cs], BO2b, True, True, r=[RK, 'cb'], w=[bk_])
                    bg = 5; gk_ = ('ps', bg)
                    MM(bank_ap(bg)[0:64, 0:256], sg1.ap[:, cs], LB[:, qd * 256:(qd + 1) * 256], True, False, r=[sg1, 'lb'], w=[gk_])
                    MM(bank_ap(bg)[0:64, 0:256], sg2.ap[:, cs], LB[:, 1024 + qd * 256:1024 + (qd + 1) * 256], False, True, r=[sg2, 'lb'], w=[gk_])
                    yield
                    CP(Wb.ap[0:64], wv, r=[wk_], w=[Wb], q='act')
                    yield
                    bu = 6; uk = ('ps', bu)
                    uv = bank_ap(bu)[0:64, 0:256].rearrange("p (h v) -> p h v", v=64)
                    for hq in range(4):
                        MM(uv[:, hq, :], TTf.ap[0:64, hq, 1, :], Wb.ap[0:64, hq, :], True, True, r=[TTf, Wb], w=[uk])
                    yield
                    CP(Ub.ap[0:64], uv, r=[uk], w=[Ub])
                    yield
                    bs_ = 7; sk = ('ps', bs_)
                    sv = bank_ap(bs_)[0:64, 0:256].rearrange("p (h v) -> p h v", v=64)
                    for hq in range(4):
                        MM(sv[:, hq, :], bctok(hq), Ub.ap[0:64, hq, :], True, False, r=[TOKc, Ub], w=[sk])
                        MM(sv[:, hq, :], kctok(hq), vtok(hq), False, True, r=[TOKc], w=[sk])
                    by = 4; yk = ('ps', by)
                    yv = bank_ap(by)[0:64, 0:256].rearrange("p (h v) -> p h v", v=64)
                    for hq in range(4):
                        ar, arb = ARh(hq, c)
                        MM(yv[:, hq, :], ar[:, 64:128], STb.ap[0:64, hq, :], True, False, r=[arb, STb], w=[yk])
                        MM(yv[:, hq, :], GT.ap[0:64, hq, 1, :], Ub.ap[0:64, hq, :], False, False, r=[GT, Ub], w=[yk])
                        MM(yv[:, hq, :], GT.ap[0:64, hq, 3, :], vtok(hq), False, True, r=[GT, TOKc], w=[yk])
                    TT(stm.ap[0:64], STq, PCH.ap[0:64, c, :].unsqueeze(2).to_broadcast([64, 4, 64]), ALU.mult, r=['st', PCH], w=[stm])
                    yield
                    TT(STq, stm.ap[0:64], sv, ALU.add, r=[stm, sk], w=['st'])
                    CP(STb.ap[0:64], STq, r=['st'], w=[STb], q='act')
                    yield
                    s1 = sm.ap[0:64, 0, :]; s2 = sm.ap[0:64, 1, :]; mean = sm.ap[0:64, 2, :]; msq = sm.ap[0:64, 3, :]
                    var = sm.ap[0:64, 4, :]; rs = sm.ap[0:64, 5, :]; bsc = sm.ap[0:64, 6, :]
                    OP('dve', lambda e, s1=s1, yv=yv: e.tensor_reduce(out=s1, in_=yv, axis=AX.X, op=ALU.add), r=_keys([yk]), w=_keys([sm]))
                    ACT(ysq.ap[0:64], yv, AF.Square, r=[yk], w=[ysq])
                    yield
                    OP('dve', lambda e, s2=s2, ysq=ysq: e.tensor_reduce(out=s2, in_=ysq.ap[0:64], axis=AX.X, op=ALU.add), r=_keys([ysq]), w=_keys([sm]))
                    TS_(mean, s1, 1.0 / 64, None, ALU.mult, ALU.bypass, r=[sm], w=[sm])
                    TT(msq, mean, mean, ALU.mult, r=[sm], w=[sm])
                    yield
                    STT(var, s2, 1.0 / 64, msq, ALU.mult, ALU.subtract, r=[sm], w=[sm])
                    ACT(var, var, AF.Sqrt, r=[sm], w=[sm], bias=GN_EPS)
                    yield
                    OP('dve', lambda e, rs=rs, var=var: e.reciprocal(out=rs, in_=var), r=_keys([sm]), w=_keys([sm]))
                    CP(bsc, bank_ap(bb)[0:64, 256:260], r=[bk_], w=[sm], q='act')
                    b3 = lambda ap: ap.unsqueeze(2).to_broadcast([64, 4, 64])
                    TT(yn.ap[0:64], yv, b3(mean), ALU.subtract, r=[yk, sm], w=[yn])
                    yield
                    TT(yn.ap[0:64], yn.ap[0:64], b3(rs), ALU.mult, r=[yn, sm], w=[yn])
                    gq = gnt.ap[0:64, 0, :].rearrange("p (h v) -> p h v", v=64)
                    bq = gnt.ap[0:64, 1, :].rearrange("p (h v) -> p h v", v=64)
                    yield
                    TT(yn.ap[0:64], yn.ap[0:64], gq, ALU.mult, r=[yn, gnt], w=[yn])
                    vt4 = TOKc.ap[0:64, :, 0, :].rearrange("p j (h v) -> p j h v", v=64)
                    bsc4 = bsc.rearrange("p (j h) -> p j h", h=2).unsqueeze(3).to_broadcast([64, 2, 2, 64])
                    TT(bon.ap[0:64].rearrange("p (j h) v -> p j h v", h=2), vt4, bsc4, ALU.mult, r=[TOKc, sm], w=[bon])
                    yield
                    TT(yn.ap[0:64], yn.ap[0:64], bq, ALU.add, r=[yn, gnt], w=[yn])
                    yield
                    TT(yn.ap[0:64], yn.ap[0:64], bon.ap[0:64], ALU.add, r=[yn, bon], w=[yn])
                    yield
                    TT(yout.ap[0:64], yn.ap[0:64].rearrange("p h v -> p (h v)"), bank_ap(bg)[0:64, 0:256], ALU.mult, r=[yn, gk_], w=[yout])
                    yield
                    bt = 6; tk = ('ps', bt)
                    for jj in range(2):
                        TR(bank_apb(bt)[:, jj * 64:(jj + 1) * 64], yout.ap[0:64, jj * 128:(jj + 1) * 128], IDb[0:64, 0:64], r=[yout, 'cb'], w=[tk])
                    yield
                    CP(YB[:, qd * 2:qd * 2 + 2, cs], bank_apb(bt)[:, 0:128].rearrange("p (j t) -> p j t", t=64), r=[tk], w=[('yb', qd)], q='act')

                for step in range(nch + 1):
                    gens = []
                    if step < nch:
                        gens.append(partA(step))
                    if step >= 1:
                        gens.append(partB(step - 1))
                    while gens:
                        for g in list(gens):
                            try:
                                next(g)
                            except StopIteration:
                                gens.remove(g)
                srelease(mQ)
            srelease(mB)
            spdma(wk_dst[l], ST[:].rearrange("p h v -> p (h v)"), r=['st'], w=[('wk', l)])

            mA = smark()
            Z = salloc([8, W], F32); DP = salloc([8, T], BF16)
            SA = salloc([W], F32); SB_ = salloc([W], F32); PW = salloc([2048], BF16); tfix = salloc([16], F32)
            pooldma(PW.ap, pw_d[l], w=[PW])
            for ch in range(8):
                b = proj_block(T, win_d[l, ch], KC, hr, hk)
                ACT(Z.ap[:, ch, 15:W], bank_ap(b)[:, 0:T], AF.Copy, r=[('ps', b)], w=[Z])
                CP(Z.ap[:, ch, 0:15], POOLST[:, l, ch, :], r=['poolst'], w=[Z])
                gi = ch // 2
                win = 2 << gi
                zc = Z.ap[:, ch, :]
                TT(SA.ap[:, 1:W], zc[:, 1:W], zc[:, 0:W - 1], ALU.add, r=[Z], w=[SA])
                fin = SA
                if win >= 4:
                    TT(SB_.ap[:, 3:W], SA.ap[:, 3:W], SA.ap[:, 1:W - 2], ALU.add, r=[SA], w=[SB_]); fin = SB_
                if win >= 8:
                    TT(SA.ap[:, 7:W], SB_.ap[:, 7:W], SB_.ap[:, 3:W - 4], ALU.add, r=[SB_], w=[SA]); fin = SA
                if win >= 16:
                    TT(SB_.ap[:, 15:W], SA.ap[:, 15:W], SA.ap[:, 7:W - 8], ALU.add, r=[SA], w=[SB_]); fin = SB_
                STT(DP.ap[:, ch, :], fin.ap[:, 15:W], 1.0 / win, zc[:, 15:W], ALU.mult, ALU.subtract, r=[fin, Z], w=[DP])
                if is_prompt_first:
                    TT(tfix.ap, fin.ap[:, 15:31], RC[:, gi, :], ALU.mult, r=[fin, 'cst'], w=[tfix])
                    TT(DP.ap[:, ch, 0:16], tfix.ap, zc[:, 15:31], ALU.subtract, r=[tfix, Z], w=[DP])
                CP(POOLST[:, l, ch, :], zc[:, T:T + 15], r=[Z], w=['poolst'])
            pwv = PW.ap.rearrange("p (g k o) -> p g k o", k=2, o=256)
            for g in range(4):
                for o2 in range(2):
                    b = bank()
                    for k2 in range(2):
                        MM(bank_ap(b)[:, 0:T], pwv[:, g, k2, o2 * 128:(o2 + 1) * 128], DP.ap[:, 2 * g + k2, :], k2 == 0, k2 == 1, r=[PW, DP], w=[('ps', b)])
                    ACT(YA[:, 2 * g + o2, 0:T], bank_ap(b)[:, 0:T], AF.Copy, r=[('ps', b), 'prm'], w=[('ya', 2 * g + o2)], scale=pcol(C_PSC + 2 * g + o2))
            srelease(mA)

            mC = smark()
            MG = salloc([KC, T], BF16)
            s1b = salloc([T], F32); s2b = salloc([T], F32); m1 = salloc([T], F32); m2 = salloc([T], F32)
            for oc in range(KC):
                half, sub = oc // 2, oc % 2
                if sub == 0:
                    spool = wload(ppj_d[l, half])
                    srw = wload(prw_d[l, half])
                bp1 = bank()
                for kc in range(8):
                    MM(bank_ap(bp1)[:, 0:T], WR[:, spool, sub * 1024 + kc * 128: sub * 1024 + (kc + 1) * 128], YA[:, kc, 0:T], kc == 0, kc == 7,
                       r=[('w', spool), ('ya', kc)], w=[('ps', bp1)])
                bp2 = bank()
                for kc in range(8):
                    MM(bank_ap(bp2)[:, 0:T], WR[:, srw, sub * 1024 + kc * 128: sub * 1024 + (kc + 1) * 128], YB[:, kc, 0:T], kc == 0, kc == 7,
                       r=[('w', srw), ('yb', kc // 2)], w=[('ps', bp2)])
                bg1 = proj_block(T, win_d[l, 36 + oc], KC, hr, hk)
                bg2 = proj_block(T, win_d[l, 52 + oc], KC, hr, hk)
                if sub == 1:
                    wfree(spool); wfree(srw)
                ACT(s1b.ap, bank_ap(bg1)[:, 0:T], AF.Sigmoid, r=[('ps', bg1)], w=[s1b])
                ACT(s2b.ap, bank_ap(bg2)[:, 0:T], AF.Sigmoid, r=[('ps', bg2)], w=[s2b])
                TT(m1.ap, s1b.ap, bank_ap(bp1)[:, 0:T], ALU.mult, r=[s1b, ('ps', bp1)], w=[m1])
                TT(m2.ap, s2b.ap, bank_ap(bp2)[:, 0:T], ALU.mult, r=[s2b, ('ps', bp2)], w=[m2])
                TT(MG.ap[:, oc, :], m1.ap, m2.ap, ALU.add, r=[m1, m2], w=[MG])
            for oc in range(KC):
                b = proj_block(T, wo_d[l, oc], KC, lambda kc: MG.ap[:, kc, :], lambda kc: [MG])
                TT(X[:, oc, 0:T], X[:, oc, 0:T], bank_ap(b)[:, 0:T], ALU.add, r=[('x', oc), ('ps', b)], w=[('x', oc)])
            srelease(mC)

            rmsnorm(T, PRM[:, C_NF:C_NF + KC], 'prm', lambda kc: H[:, kc, 0:T], lambda kc: [('h', kc)])
            mD = smark()
            ACTB = [salloc([FGC, T], BF16) for _ in range(2)]
            sgb = [salloc([T], F32) for _ in range(2)]
            for fg in range(FG):
                ab = ACTB[fg % 2]
                for fi in range(FGC):
                    fc = fg * FGC + fi
                    bg_ = proj_block(T, wg_d[l, fc], KC, hr, hk)
                    bu_ = proj_block(T, wu_d[l, fc], KC, hr, hk)
                    sg = sgb[fi % 2]
                    ACT(sg.ap, bank_ap(bg_)[:, 0:T], AF.Silu, r=[('ps', bg_)], w=[sg])
                    TT(ab.ap[:, fi, :], sg.ap, bank_ap(bu_)[:, 0:T], ALU.mult, r=[sg, ('ps', bu_)], w=[ab])
                for oc in range(KC):
                    b = proj_block(T, wd_d[l, fg, oc], FGC, lambda kc: ab.ap[:, kc, :], lambda kc: [ab])
                    TT(X[:, oc, 0:T], X[:, oc, 0:T], bank_ap(b)[:, 0:T], ALU.add, r=[('x', oc), ('ps', b)], w=[('x', oc)])
            srelease(mD)

            rmsnorm(T, PRM[:, C_NP:C_NP + KC], 'prm', lambda kc: H[:, kc, 0:T], lambda kc: [('h', kc)])
            mE = smark()
            PT = salloc([2, T], BF16); sp_ = salloc([T], F32); tp_ = salloc([T], F32)
            PJ = [salloc([2048], BF16) for _ in range(2)]
            pooldma(PT.ap, p_src(l), w=[PT])
            pooldma(PJ[0].ap, pj_d[l, 0], w=[PJ[0]])
            pooldma(PJ[1].ap, pj_d[l, 1], w=[PJ[1]])
            for oc in range(KC):
                pjb = PJ[oc // 8]
                bgp = proj_block(T, pg_d[l, oc], KC, hr, hk)
                bpp = bank()
                for k2 in range(2):
                    MM(bank_ap(bpp)[:, 0:T], pjb.ap[:, (oc % 8) * 256 + k2 * 128:(oc % 8) * 256 + (k2 + 1) * 128], PT.ap[:, k2, :], k2 == 0, k2 == 1,
                       r=[pjb, PT], w=[('ps', bpp)])
                ACT(sp_.ap, bank_ap(bgp)[:, 0:T], AF.Sigmoid, r=[('ps', bgp)], w=[sp_])
                TT(tp_.ap, sp_.ap, bank_ap(bpp)[:, 0:T], ALU.mult, r=[sp_, ('ps', bpp)], w=[tp_])
                TT(X[:, oc, 0:T], X[:, oc, 0:T], tp_.ap, ALU.add, r=[('x', oc), tp_], w=[('x', oc)])
            srelease(mE)

        def run_segment(kind, t0, T, first, last):
            xsrc = xp_d if kind == 'p' else xs_d
            ydst = yp_d if kind == 'p' else ys_d
            spdma(X[:, :, 0:T], xsrc[:, :, t0:t0 + T].rearrange("c p t -> p c t"), w=[('x', kc) for kc in range(KC)])
            if first:
                if kind == 'p':
                    MSET(POOLST[:], 0.0, w=['poolst'])
                    MSET(SHIFT[:], 0.0, w=['shift'])
                else:
                    spdma(POOLST[:].rearrange("p l c t -> p (l c t)"), stp_d, w=['poolst'])
                    spdma(SHIFT[:].rearrange("p l q -> p (l q)"), sts_d, w=['shift'])
            for l in range(L):
                if kind == 'p':
                    psrc = lambda l, t0=t0, T=T: pp_d[l, :, :, t0:t0 + T].rearrange("k p t -> p k t")
                    layer_tile(l, T, first, first, wkp_d, wkp_d, psrc)
                else:
                    psrc = lambda l: psm_d[l].rearrange("k p t -> p k t")
                    layer_tile(l, T, first, False, stw_d, wks_d, psrc)
            m = smark()
            YO = salloc([KC, T], F32)
            rmsnorm(T, NFIN[:], 'nfin', lambda kc: YO.ap[:, kc, :], lambda kc: list(YO.keys))
            spdma(ydst[:, :, t0:t0 + T].rearrange("c p t -> p c t"), YO.ap, r=[YO])
            srelease(m)
            if last:
                spdma(shp_d if kind == 'p' else shs_d, SHIFT[:].rearrange("p l q -> p (l q)"), r=['shift'])
                spdma(pop_d if kind == 'p' else pos_d, POOLST[:].rearrange("p l c t -> p (l c t)"), r=['poolst'])

        ntile = TP // TILE
        for ti in range(ntile):
            run_segment('p', ti * TILE, TILE, ti == 0, ti == ntile - 1)
        run_segment('s', 0, TS, True, True)

        cnt = S.analyze()
        sems = {k: es.enter_context(nc.semaphore("sem_%s_%s" % k)) for k in cnt}
        block = es.enter_context(nc.Block())
        S.emit(sems, block)
    return nc, len(S.ops)


def rw_cols():
    cols = []
    cols.append(list(range(3072, 3136)) + [-1] * 64)
    cols.append(list(range(3136, 3200)) + [-1] * 64)
    cols.append(list(range(3200, 3328)))
    cols.append(list(range(3328, 3360)) + [-1] * 96)
    for j in range(8):
        for base in (0, 1024, 2048):
            cols.append(list(range(base + 128 * j, base + 128 * (j + 1))))
    return np.array(cols, dtype=np.int64)


def make_consts():
    c = np.zeros((128, NCST), np.float32)
    c[:, K_ID:K_ID + 128] = np.eye(128)
    for m in range(128):
        c[(m + 64) % 128, K_PERM + m] = 1.0
    c[:, K_ONES:K_ONES + 128] = 1.0
    p = np.arange(128)
    c[:, K_BLK2:K_BLK2 + 128] = (p[:, None] // 64 == p[None, :] // 64)
    c[:, K_BO2:K_BO2 + 2] = (p[:, None] // 64 == np.arange(2)[None, :])
    s = np.arange(64)[:, None]; t = np.arange(64)[None, :]
    strict = (t > s).astype(np.float32); incl = (t >= s).astype(np.float32)
    c[0:64, K_MG:K_MG + 256] = np.concatenate([strict, incl, strict, incl], axis=1)
    c[0:64, K_MX:K_MX + 64] = (s > t)
    rst = np.ones(512, np.float32); rst[::64] = 0.0
    c[:, K_RST:K_RST + 512] = rst[None, :]
    for gi, win in enumerate((2, 4, 8, 16)):
        c[:, K_RC + gi * 16:K_RC + (gi + 1) * 16] = (1.0 / np.minimum(np.arange(16) + 1, win))[None, :]
    return c


def kblocks(w, kc_n):
    K, N = w.shape
    return np.ascontiguousarray(w.reshape(kc_n, 128, N // 128, 128).transpose(2, 1, 0, 3).reshape(N // 128, 128, kc_n * 128))


def layout_weights(inp, L):
    f = lambda k: np.asarray(inp[k], dtype=np.float32)
    cols = rw_cols()
    out = {}
    w_in = f('w_in')
    win_b = np.zeros((L, NWIN, 128, 2048), np.float32)
    lora_b = np.zeros((L, 2, 128, 2048), np.float32)
    prm = np.zeros((L, 128, NPRM), np.float32)
    mu = f('mu_shift')
    for l in range(L):
        wl = w_in[l]
        win_b[l, 0:8] = kblocks(wl[:, 0:1024], KC)
        rwm = np.zeros((D, NQ * 128), np.float32)
        cc = cols.reshape(-1)
        ok = cc >= 0
        rwm[:, ok] = wl[:, 1024 + cc[ok]]
        win_b[l, 8:36] = kblocks(rwm, KC)
        win_b[l, 36:52] = kblocks(wl[:, 1024 + 3360:1024 + 3360 + 2048], KC)
        win_b[l, 52:68] = kblocks(wl[:, 1024 + 3360 + 2048:], KC)
        lora_b[l, 0, 0:64, 0:1024] = f('w_decay_up')[l]
        lora_b[l, 0, 0:64, 1024:2048] = f('w_aaa_up')[l]
        lora_b[l, 1, :, 0:1024] = f('w_gate_up')[l][0:128]
        lora_b[l, 1, 0:32, 1024:2048] = f('w_gate_up')[l][128:160]
        pc = lambda v: v.reshape(-1, 128).T
        prm[l, :, C_NM:C_NM + 16] = pc(f('norm_mix')[l]); prm[l, :, C_NF:C_NF + 16] = pc(f('norm_ffn')[l]); prm[l, :, C_NP:C_NP + 16] = pc(f('norm_ple')[l])
        mum = np.zeros(NQ * 128, np.float32); mum[ok] = mu[l][cc[ok]]
        prm[l, :, C_MU:C_MU + NQ] = mum.reshape(NQ, 128).T
        prm[l, :, C_PSC:C_PSC + 8] = pc(f('pool_scale')[l]); prm[l, :, C_W0:C_W0 + 8] = pc(f('w0')[l]); prm[l, :, C_A0:C_A0 + 8] = pc(f('a0')[l])
        prm[l, :, C_KK:C_KK + 8] = pc(f('k_k')[l]); prm[l, :, C_KA:C_KA + 8] = pc(f('k_a')[l]); prm[l, :, C_RK:C_RK + 8] = pc(f('r_k')[l].reshape(-1))
    out['w_in_b'] = win_b; out['lora_b'] = lora_b; out['prm'] = prm
    pw = f('pool_w').reshape(L, 4, 2, 128, 256).transpose(0, 3, 1, 2, 4).reshape(L, 128, 2048)
    out['pw_b'] = np.ascontiguousarray(pw)

    def two_oc(w):
        r = np.stack([kblocks(w[l], 8) for l in range(L)])
        return np.ascontiguousarray(r.reshape(L, 8, 2, 128, 1024).transpose(0, 1, 3, 2, 4).reshape(L, 8, 128, 2048))
    out['ppool_b'] = two_oc(f('proj_pool')); out['prwkv_b'] = two_oc(f('proj_rwkv'))
    out['wo_b'] = np.stack([kblocks(f('w_out')[l], KC) for l in range(L)])
    out['wg_b'] = np.stack([kblocks(f('w_ffn_gate')[l], KC) for l in range(L)])
    out['wu_b'] = np.stack([kblocks(f('w_ffn_up')[l], KC) for l in range(L)])
    wd = f('w_ffn_down')
    out['wd_b'] = np.stack([np.stack([kblocks(wd[l][fg * FGC * 128:(fg + 1) * FGC * 128], FGC) for fg in range(FG)]) for l in range(L)])
    out['pg_b'] = np.stack([kblocks(f('w_ple_gate')[l], KC) for l in range(L)])
    pj = np.stack([kblocks(f('w_ple_proj')[l], 2) for l in range(L)])
    out['pj_b'] = np.ascontiguousarray(pj.reshape(L, 2, 8, 128, 256).transpose(0, 1, 3, 2, 4).reshape(L, 2, 128, 2048))
    out['nfin'] = np.ascontiguousarray(f('norm_final').reshape(16, 128).T)
    out['gn'] = np.ascontiguousarray(np.concatenate([f('gn_gain'), f('gn_bias')], axis=1))
    out['cst'] = make_consts()
    return out, cols


_CACHE = {}


def run(inputs, L, TP, TS, TILE, ncores):
    key = (L, TP, TS, TILE)
    if key not in _CACHE:
        _CACHE[key] = build_program(L, TP, TS, TILE)[0]
    nc = _CACHE[key]
    f = lambda k: np.asarray(inputs[k], dtype=np.float32)
    shared, cols = layout_weights(inputs, L)
    cc = cols.reshape(-1); ok = cc >= 0
    in_maps = []
    for b in range(ncores):
        m = dict(shared)
        m['xp'] = np.ascontiguousarray(f('x_prompt')[b].T.reshape(KC, 128, TP))
        m['xs'] = np.ascontiguousarray(f('x_sample')[b].T.reshape(KC, 128, TS))
        m['pp'] = np.ascontiguousarray(f('p_prompt')[:, b].transpose(0, 2, 1).reshape(L, 2, 128, TP))
        m['psm'] = np.ascontiguousarray(f('p_sample')[:, b].transpose(0, 2, 1).reshape(L, 2, 128, TS))
        ss = np.zeros((L, NQ * 128), np.float32); ss[:, ok] = f('state_shift')[:, b][:, cc[ok]]
        m['st_shift'] = np.ascontiguousarray(ss.reshape(L, NQ, 128).transpose(2, 0, 1).reshape(128, L * NQ))
        sp = f('state_pool')[:, b]
        m['st_pool'] = np.ascontiguousarray(sp.reshape(L, 15, 8, 128).transpose(3, 0, 2, 1).reshape(128, L * 120))
        sw = f('state_wkv')[:, b]
        m['st_wkv'] = np.ascontiguousarray(sw.transpose(0, 3, 1, 2).reshape(L, 64, 1024))
        in_maps.append(m)
    res = run_bass_kernel_spmd(nc, in_maps, core_ids=list(range(ncores)))
    R = res.results
    inv = np.full(3360, -1, np.int64); inv[cc[ok]] = np.nonzero(ok)[0]

    def unshift(a):
        v = a.reshape(128, L, NQ).transpose(1, 2, 0).reshape(L, NQ * 128)
        return v[:, inv]
    y_p = np.stack([R[b]['yp'].reshape(D, TP).T for b in range(ncores)])
    y_s = np.stack([R[b]['ys'].reshape(D, TS).T for b in range(ncores)])
    outs = [y_p, y_s]
    for sfx in ('p', 's'):
        sh = np.stack([unshift(R[b]['sh_' + sfx]) for b in range(ncores)], axis=1)
        po = np.stack([R[b]['po_' + sfx].reshape(128, L, 8, 15).transpose(1, 3, 2, 0).reshape(L, 15, 1024) for b in range(ncores)], axis=1)
        wk = np.stack([R[b]['wk_' + sfx].reshape(L, 64, 16, 64).transpose(0, 2, 3, 1) for b in range(ncores)], axis=1)
        outs += [sh, po, wk]
    return tuple(np.ascontiguousarray(o.astype(np.float32)) for o in outs)


def kernel(**inputs):
    return run(inputs, 4, 2048, 64, 512, 8)
```

```python
import contextlib
import os
F_THREADS = int(os.environ.get('F_THREADS', '2'))
F_YV1 = int(os.environ.get('F_YV1', '0'))
F_NOB2 = int(os.environ.get('F_NOB2', '0'))
F_NOB1 = int(os.environ.get('F_NOB1', '0'))
F_NOA = int(os.environ.get('F_NOA', '0'))
F_TVB = int(os.environ.get('F_TVB', '0'))
import numpy as np
import concourse.bass as bass
import concourse.mybir as mybir
from concourse.bass_utils import run_bass_kernel_spmd

F32 = mybir.dt.float32
BF16 = mybir.dt.bfloat16
AF = mybir.ActivationFunctionType
ALU = mybir.AluOpType
AX = mybir.AxisListType

D = 2048
KC = 16
POOLW = 1024
RWW = 1024
NHEAD = 16
DFF = 5632
FC = 44
FG = 4
FGC = 11
PLE = 256
NQ = 28
NWIN = 68
NPRM = 124
C_NM, C_NF, C_NP, C_MU, C_PSC, C_W0, C_A0, C_KK, C_KA, C_RK = 0, 16, 32, 48, 76, 84, 92, 100, 108, 116
K_ID, K_PERM, K_ONES, K_BLK2, K_BO2, K_MG, K_MX, K_RST, K_RC = 0, 128, 256, 384, 512, 514, 770, 834, 1346
NCST = 1410
NORM_EPS = 1e-6
GN_EPS = 64e-5
C0 = float(np.exp(-0.5))
NSLOT = 6
SCR_KB = 72


class Sched:
    QUEUES = ['pe', 'act', 'dve', 'pool', 'sp']

    def __init__(self):
        self.ops = []

    def op(self, q, fn, reads=(), writes=(), stream=None):
        self.ops.append(dict(q=q, fn=fn, reads=reads, writes=writes, stream=stream))

    def analyze(self):
        ops = self.ops
        last_w = {}
        readers = {}
        last_on_stream = {}
        for i, o in enumerate(ops):
            deps = set()
            for r in o['reads']:
                j = last_w.get(r)
                if j is not None:
                    deps.add(j)
            for w in o['writes']:
                j = last_w.get(w)
                if j is not None:
                    deps.add(j)
                rd = readers.get(w)
                if rd:
                    deps.update(rd.values())
            if o['stream'] is not None:
                j = last_on_stream.get(o['stream'])
                if j is not None:
                    deps.add(j)
                last_on_stream[o['stream']] = i
            deps.discard(i)
            ck = ('s', o['stream']) if o['stream'] is not None else ('q', o['q'])
            o['ck'] = ck
            for r in o['reads']:
                readers.setdefault(r, {})[ck] = i
            for w in o['writes']:
                last_w[w] = i
                readers[w] = {}
            o['deps'] = deps
            o['sig'] = o['stream'] is not None
        for o in ops:
            nd = set()
            for d in o['deps']:
                p = ops[d]
                if p['ck'] == ('q', 'pe') and o['ck'] == ('q', 'pe'):
                    continue
                nd.add(d)
                p['sig'] = True
            o['deps'] = nd
        cnt = {}
        for o in ops:
            if o['sig']:
                k = o['ck']
                cnt[k] = cnt.get(k, 0) + (16 if o['stream'] is not None else 1)
                o['semval'] = cnt[k]
        self.final_counts = cnt
        known = {q: {} for q in self.QUEUES}
        for o in ops:
            need = {}
            for d in o['deps']:
                p = ops[d]
                k = p['ck']
                v = p['semval']
                if v > need.get(k, 0):
                    need[k] = v
            waits = []
            kq = known[o['q']]
            for k, v in need.items():
                if kq.get(k, 0) >= v:
                    continue
                kq[k] = v
                waits.append((k, v))
            o['waits'] = waits
        return cnt

    def emit(self, sems, block):
        ops = self.ops
        byq = {q: [o for o in ops if o['q'] == q] for q in self.QUEUES}
        dec = {'pe': block.tensor, 'act': block.scalar, 'dve': block.vector, 'pool': block.gpsimd, 'sp': block.sync}
        final = list(self.final_counts.items())
        for q in self.QUEUES:
            lst = byq[q]

            def body(eng, lst=lst, q=q):
                for o in lst:
                    for (k, v) in o['waits']:
                        eng.wait_ge(sems[k], v)
                    ins = o['fn'](eng)
                    if o['sig']:
                        ins.then_inc(sems[o['ck']], 16 if o['stream'] is not None else 1)
                if q == 'sp':
                    for k, v in final:
                        eng.wait_ge(sems[k], v)
            dec[q](body)


class Buf:
    def __init__(self, ap, keys):
        self.ap = ap
        self.keys = tuple(keys)


def _keys(items):
    out = []
    for it in items:
        if isinstance(it, Buf):
            out.extend(it.keys)
        elif isinstance(it, list):
            out.extend(_keys(it))
        elif isinstance(it, tuple) and len(it) == 2 and it[0] == 'ps':
            out.append(('psh', it[1], 0))
            out.append(('psh', it[1], 256))
        else:
            out.append(it)
    return tuple(out)


def build_program(L, TP, TS, TILE):
    nc = bass.Bass("TRN2", target_bir_lowering=False)
    S = Sched()

    def OP(q, fn, r=(), w=(), stream=None):
        S.op(q, fn, _keys(r), _keys(w), stream)

    def din(name, shape):
        return nc.dram_tensor(name, list(shape), F32, kind="ExternalInput").ap()

    def dout(name, shape):
        return nc.dram_tensor(name, list(shape), F32, kind="ExternalOutput").ap()

    xp_d = din("xp", [KC, 128, TP]); xs_d = din("xs", [KC, 128, TS])
    pp_d = din("pp", [L, 2, 128, TP]); psm_d = din("psm", [L, 2, 128, TS])
    sts_d = din("st_shift", [128, L * NQ]); stp_d = din("st_pool", [128, L * 120]); stw_d = din("st_wkv", [L, 64, 1024])
    prm_d = din("prm", [L, 128, NPRM]); nfin_d = din("nfin", [128, KC]); gn_d = din("gn", [L, 2048]); cst_d = din("cst", [128, NCST])
    win_d = din("w_in_b", [L, NWIN, 128, 2048]); pw_d = din("pw_b", [L, 128, 2048]); lora_d = din("lora_b", [L, 2, 128, 2048])
    ppj_d = din("ppool_b", [L, 8, 128, 2048]); prw_d = din("prwkv_b", [L, 8, 128, 2048]); wo_d = din("wo_b", [L, 16, 128, 2048])
    wg_d = din("wg_b", [L, FC, 128, 2048]); wu_d = din("wu_b", [L, FC, 128, 2048]); wd_d = din("wd_b", [L, FG, 16, 128, FGC * 128])
    pg_d = din("pg_b", [L, 16, 128, 2048]); pj_d = din("pj_b", [L, 2, 128, 2048])
    yp_d = dout("yp", [KC, 128, TP]); ys_d = dout("ys", [KC, 128, TS])
    shp_d = dout("sh_p", [128, L * NQ]); shs_d = dout("sh_s", [128, L * NQ])
    pop_d = dout("po_p", [128, L * 120]); pos_d = dout("po_s", [128, L * 120])
    wkp_d = dout("wk_p", [L, 64, 1024]); wks_d = dout("wk_s", [L, 64, 1024])

    es = contextlib.ExitStack()
    with es:
        def sbt(name, shape, dt):
            return es.enter_context(nc.sbuf_tensor(name, list(shape), dt))

        TM = TILE
        X = sbt("X", [128, KC, TM], F32)
        H = sbt("H", [128, KC, TM], BF16)
        YA = sbt("YA", [128, 8, TM], BF16)
        YB = sbt("YB", [128, 8, TM], BF16)
        ST = sbt("ST", [64, NHEAD, 64], F32)
        POOLST = sbt("POOLST", [128, L, 8, 15], F32)
        SHIFT = sbt("SHIFT", [128, L, NQ], F32)
        WR = sbt("WR", [128, NSLOT, 2048], BF16)
        LA = sbt("LA", [128, 2048], BF16)
        LB = sbt("LB", [128, 2048], BF16)
        CST = sbt("CST", [128, NCST], F32)
        CB = sbt("CB", [128, 514], BF16)
        PRM = sbt("PRM", [128, NPRM], F32)
        NFIN = sbt("NFIN", [128, KC], F32)
        SCR = sbt("SCR", [128, SCR_KB * 256], F32)
        SCRf = SCR[:]
        SCRb = SCR[:].bitcast(BF16)
        PQ = [es.enter_context(nc.psum_tensor("PQ%d" % i, [128, 1024], F32)) for i in range(4)]
        PQb = [p[:].bitcast(BF16) for p in PQ]

        IDb = CB[:, 0:128]; PERMb = CB[:, 128:256]; ONESb = CB[:, 256:384]; BLK2b = CB[:, 384:512]; BO2b = CB[:, 512:514]
        PERMf = CST[:, K_PERM:K_PERM + 128]

        scr_top = [0]

        def salloc(free_shape, dt, parts=128):
            n = int(np.prod(free_shape))
            nbytes = n * (4 if dt == F32 else 2)
            nslab = (nbytes + 1023) // 1024
            off = scr_top[0]
            scr_top[0] += nslab
            assert scr_top[0] <= SCR_KB, "scratch overflow %d" % scr_top[0]
            if dt == F32:
                ap = SCRf[0:parts, off * 256: off * 256 + n]
            else:
                ap = SCRb[0:parts, off * 512: off * 512 + n]
            if len(free_shape) == 2:
                ap = ap.rearrange("p (a b) -> p a b", b=free_shape[1])
            elif len(free_shape) == 3:
                ap = ap.rearrange("p (a b c) -> p a b c", b=free_shape[1], c=free_shape[2])
            elif len(free_shape) == 4:
                ap = ap.rearrange("p (a b c d) -> p a b c d", b=free_shape[1], c=free_shape[2], d=free_shape[3])
            return Buf(ap, [('scr', off + i) for i in range(nslab)])

        def smark():
            return scr_top[0]

        def srelease(m):
            scr_top[0] = m

        bank_rr = [0]

        def bank():
            b = bank_rr[0] % 8
            bank_rr[0] += 1
            return b

        def bank_ap(b):
            return PQ[b // 2][:, (b % 2) * 512:(b % 2) * 512 + 512]

        def bank_apb(b):
            return PQb[b // 2][:, (b % 2) * 1024:(b % 2) * 1024 + 1024]

        pair_rr = [0]

        def bpair():
            p = pair_rr[0] % 4
            pair_rr[0] += 1
            return p

        wr_ctr = [0]

        wr_busy = [False] * NSLOT

        def wfree(s):
            wr_busy[s] = False

        def wload(src_ap):
            n = src_ap.shape[-1]
            s = wr_ctr[0] % NSLOT
            wr_ctr[0] += 1
            assert not wr_busy[s], "weight ring slot still pinned"
            wr_busy[s] = True
            OP('pool', lambda e, s=s, src_ap=src_ap, n=n: e.dma_start(out=WR[:, s, 0:n], in_=src_ap), w=[('w', s)], stream='w%d' % s)
            return s

        sp_ctr = [0]

        def spdma(out, in_, r=(), w=()):
            st = 'sp%d' % (sp_ctr[0] % 4)
            sp_ctr[0] += 1
            OP('sp', lambda e, out=out, in_=in_: e.dma_start(out=out, in_=in_), r=r, w=w, stream=st)

        def pooldma(out, in_, r=(), w=()):
            OP('pool', lambda e, out=out, in_=in_: e.dma_start(out=out, in_=in_), r=r, w=w, stream='pm')

        def MM(out, lhsT, rhs, start, stop, r, w):
            OP('pe', lambda e, out=out, lhsT=lhsT, rhs=rhs, start=start, stop=stop: e.matmul(out, lhsT=lhsT, rhs=rhs, start=start, stop=stop), r=r, w=w)

        def TR(out, in_, ident, r, w):
            OP('pe', lambda e, out=out, in_=in_, ident=ident: e.transpose(out, in_, ident), r=r, w=w)

        def ACT(out, in_, func, r, w, bias=0.0, scale=1.0):
            OP('act', lambda e, out=out, in_=in_, func=func, bias=bias, scale=scale: e.activation(out=out, in_=in_, func=func, bias=bias, scale=scale), r=r, w=w)

        def TT(out, in0, in1, op, r, w, q='dve'):
            OP(q, lambda e, out=out, in0=in0, in1=in1, op=op: e.tensor_tensor(out=out, in0=in0, in1=in1, op=op), r=r, w=w)

        def TS_(out, in0, s1, s2, op0, op1, r, w):
            OP('dve', lambda e, out=out, in0=in0, s1=s1, s2=s2, op0=op0, op1=op1: e.tensor_scalar(out=out, in0=in0, scalar1=s1, scalar2=s2, op0=op0, op1=op1), r=r, w=w)

        def STT(out, in0, sc, in1, op0, op1, r, w):
            OP('dve', lambda e, out=out, in0=in0, sc=sc, in1=in1, op0=op0, op1=op1: e.scalar_tensor_tensor(out=out, in0=in0, scalar=sc, in1=in1, op0=op0, op1=op1), r=r, w=w)

        def CP(out, in_, r, w, q='dve'):
            if q == 'act':
                ACT(out, in_, AF.Copy, r, w)
            else:
                OP('dve', lambda e, out=out, in_=in_: e.tensor_copy(out=out, in_=in_), r=r, w=w)

        def MSET(ap, val, w):
            OP('dve', lambda e, ap=ap, val=val: e.memset(ap, val), w=w)

        spdma(CST[:], cst_d, w=['cst'])
        spdma(NFIN[:], nfin_d, w=['nfin'])
        CP(CB[:, 0:514], CST[:, 0:514], r=['cst'], w=['cb'])
        MASKG = CST[0:64, K_MG:K_MG + 256]
        MASKX = CST[0:64, K_MX:K_MX + 64]
        I64f = CST[0:64, K_ID:K_ID + 64]
        RC = CST[:, K_RC:K_RC + 64].rearrange("p (g t) -> p g t", t=16)

        def rmsnorm(T, gsrc, gkey, dst_fn, dst_key_fn):
            m = smark()
            sqb = [salloc([T], BF16) for _ in range(4)]
            rt = salloc([T], F32)
            rstd = salloc([T], F32)
            b = bank()
            pk = ('ps', b)
            for kc in range(KC):
                sq = sqb[kc % 4]
                ACT(sq.ap, X[:, kc, 0:T], AF.Square, r=[('x', kc)], w=[sq])
                MM(bank_ap(b)[:, 0:T], ONESb, sq.ap, kc == 0, kc == KC - 1, r=[sq, 'cb'], w=[pk])
            ACT(rt.ap, bank_ap(b)[:, 0:T], AF.Sqrt, r=[pk], w=[rt], bias=NORM_EPS, scale=1.0 / D)
            OP('dve', lambda e: e.reciprocal(out=rstd.ap, in_=rt.ap), r=_keys([rt]), w=_keys([rstd]))
            for kc in range(KC):
                STT(dst_fn(kc), X[:, kc, 0:T], gsrc[:, kc:kc + 1], rstd.ap, ALU.mult, ALU.mult,
                    r=[('x', kc), gkey, rstd], w=list(dst_key_fn(kc)))
            srelease(m)

        def proj_block(T, blk_ap, n_k, rhs_fn, rhs_keys_fn, lhs_off=0, m=128):
            s = wload(blk_ap)
            b = bank()
            for kc in range(n_k):
                MM(bank_ap(b)[0:m, 0:T], WR[:, s, lhs_off + kc * 128: lhs_off + kc * 128 + m], rhs_fn(kc), kc == 0, kc == n_k - 1,
                   r=[('w', s)] + list(rhs_keys_fn(kc)), w=[('ps', b)])
            wfree(s)
            return b

        def layer_tile(l, T, first, is_prompt_first, wk_src, wk_dst, p_src):
            nch = T // 64
            W = 15 + T
            spdma(PRM[:], prm_d[l], w=['prm'])
            pooldma(LA[:], lora_d[l, 0], w=['la'])
            pooldma(LB[:], lora_d[l, 1], w=['lb'])
            if is_prompt_first:
                MSET(ST[:], 0.0, w=['st'])
            else:
                spdma(ST[:].rearrange("p h v -> p (h v)"), wk_src[l], r=[('wk', l)] if wk_src is wkp_d else [], w=['st'])

            def pcol(c):
                return PRM[:, c:c + 1]

            rmsnorm(T, PRM[:, C_NM:C_NM + KC], 'prm', lambda kc: H[:, kc, 0:T], lambda kc: [('h', kc)])
            hr = lambda kc: H[:, kc, 0:T]
            hk = lambda kc: [('h', kc)]

            mB = smark()
            tw = salloc([T], BF16); ad = salloc([T], BF16); sg1 = salloc([T], BF16); sg2 = salloc([T], BF16)
            gnt = salloc([2, 256], F32, parts=64)
            RWT = {}

            def rw_alloc():
                rwb_all = salloc([3, T + 1], F32)
                RWT['rwb'] = [Buf(rwb_all.ap[:, i, :], rwb_all.keys) for i in range(3)]
                RWT['xm'] = [salloc([T], F32) for _ in range(3)]
                RWT['tmpA'] = salloc([T], F32)

            def rw_chunk(q, slot):
                rwb, xm, tmpA = RWT['rwb'], RWT['xm'], RWT['tmpA']
                b = proj_block(T, win_d[l, 8 + q], KC, hr, hk)
                rb = rwb[slot]
                ACT(rb.ap[:, 1:T + 1], bank_ap(b)[:, 0:T], AF.Copy, r=[('ps', b)], w=[rb])
                CP(rb.ap[:, 0:1], SHIFT[:, l, q:q + 1], r=['shift'], w=[rb])
                TT(tmpA.ap, rb.ap[:, 0:T], rb.ap[:, 1:T + 1], ALU.subtract, r=[rb], w=[tmpA])
                STT(xm[slot].ap, tmpA.ap, pcol(C_MU + q), rb.ap[:, 1:T + 1], ALU.mult, ALU.add, r=[tmpA, rb, 'prm'], w=[xm[slot]])
                CP(SHIFT[:, l, q:q + 1], rb.ap[:, T:T + 1], r=[rb], w=['shift'])

            mL = smark()
            rw_alloc()
            xm = RWT['xm']
            rw_chunk(0, 0)
            ACT(tw.ap[0:64, :], xm[0].ap[0:64, :], AF.Tanh, r=[xm[0]], w=[tw])
            rw_chunk(1, 1)
            CP(ad.ap[0:64, :], xm[1].ap[0:64, :], r=[xm[1]], w=[ad], q='act')
            rw_chunk(2, 2)
            ACT(sg1.ap, xm[2].ap, AF.Sigmoid, r=[xm[2]], w=[sg1])
            rw_chunk(3, 0)
            ACT(sg2.ap, xm[0].ap, AF.Sigmoid, r=[xm[0]], w=[sg2])
            srelease(mL)

            for qd in range(4):
                mQ = smark()
                spdma(gnt.ap[:, 0, :], gn_d[l:l + 1, qd * 256:(qd + 1) * 256].partition_broadcast(64), w=[gnt])
                spdma(gnt.ap[:, 1, :], gn_d[l:l + 1, 1024 + qd * 256:1024 + (qd + 1) * 256].partition_broadcast(64), w=[gnt])
                ARs = salloc([2, nch, 128], BF16); ARo = salloc([2, nch, 128], BF16)
                BKs = salloc([2, nch, 128], BF16); BKo = salloc([2, nch, 128], BF16)
                BKc = salloc([2, nch, 128], BF16)
                VB = salloc([2, T], BF16); RK = salloc([2, T], BF16)
                PCs = salloc([2, 8], F32); PCH = salloc([nch, 4], F32, parts=64)
                MSET(PCs.ap, 0.0, w=[PCs])
                mP = smark()
                rw_alloc()
                xm = RWT['xm']
                t = [salloc([T], F32) for _ in range(11)]
                sqk = salloc([T], BF16)
                for jj in range(2):
                    j = qd * 2 + jj
                    for i3 in range(3):
                        rw_chunk(4 + 3 * j + i3, i3)
                    XR, XK, XV = xm[0], xm[1], xm[2]
                    sigz, cum, exc, pinc, pinv, pexc, av, kkr, kk, kh, tb = t
                    b = bank()
                    MM(bank_ap(b)[:, 0:T], LA[0:64, j * 128:(j + 1) * 128], tw.ap[0:64, :], True, True, r=['la', tw], w=[('ps', b)])
                    ACT(sigz.ap, bank_ap(b)[:, 0:T], AF.Sigmoid, r=[('ps', b), 'prm'], w=[sigz], bias=pcol(C_W0 + j))
                    OP('dve', lambda e, cum=cum, sigz=sigz: e.tensor_tensor_scan(out=cum.ap, data0=CST[:, K_RST:K_RST + T], data1=sigz.ap,
                                                                                 initial=0.0, op0=ALU.mult, op1=ALU.add),
                       r=_keys([sigz, 'cst']), w=_keys([cum]))
                    TT(exc.ap, cum.ap, sigz.ap, ALU.subtract, r=[cum, sigz], w=[exc])
                    ACT(pinc.ap, cum.ap, AF.Exp, r=[cum], w=[pinc], scale=-C0)
                    ACT(pinv.ap, cum.ap, AF.Exp, r=[cum], w=[pinv], scale=C0)
                    ACT(pexc.ap, exc.ap, AF.Exp, r=[exc], w=[pexc], scale=-C0)
                    CP(PCs.ap[:, jj, 0:nch], pinc.ap.rearrange("p (c t) -> p c t", t=64)[:, :, 63], r=[pinc], w=[PCs])
                    b = bank()
                    MM(bank_ap(b)[:, 0:T], LA[0:64, 1024 + j * 128:1024 + (j + 1) * 128], ad.ap[0:64, :], True, True, r=['la', ad], w=[('ps', b)])
                    ACT(av.ap, bank_ap(b)[:, 0:T], AF.Sigmoid, r=[('ps', b), 'prm'], w=[av], bias=pcol(C_A0 + j))
                    TS_(kkr.ap, XK.ap, pcol(C_KK + j), None, ALU.mult, ALU.bypass, r=[XK, 'prm'], w=[kkr])
                    TT(sqk.ap, kkr.ap, kkr.ap, ALU.mult, r=[kkr], w=[sqk])
                    b = bank()
                    MM(bank_ap(b)[:, 0:T], BLK2b, sqk.ap, True, True, r=['cb', sqk], w=[('ps', b)])
                    ACT(exc.ap, bank_ap(b)[:, 0:T], AF.Sqrt, r=[('ps', b)], w=[exc])
                    TS_(exc.ap, exc.ap, 1e-12, None, ALU.max, ALU.bypass, r=[exc], w=[exc])
                    OP('dve', lambda e, exc=exc: e.reciprocal(out=exc.ap, in_=exc.ap), r=_keys([exc]), w=_keys([exc]))
                    TT(kk.ap, kkr.ap, exc.ap, ALU.mult, r=[kkr, exc], w=[kk])
                    TS_(tb.ap, av.ap, 1.0, pcol(C_KA + j), ALU.subtract, ALU.mult, r=[av, 'prm'], w=[tb])
                    STT(kh.ap, tb.ap, 1.0, XK.ap, ALU.add, ALU.mult, r=[tb, XK], w=[kh])
                    c3 = lambda ap: ap.rearrange("p (c t) -> p c t", t=64)
                    STT(ARs.ap[:, jj, :, 0:64], c3(kk.ap), -1.0, c3(pexc.ap), ALU.mult, ALU.mult, r=[kk, pexc], w=[ARs])
                    TT(ARs.ap[:, jj, :, 64:128], c3(XR.ap), c3(pinc.ap), ALU.mult, r=[XR, pinc], w=[ARs])
                    STT(RK.ap[:, jj, :], XR.ap, pcol(C_RK + j), kh.ap, ALU.mult, ALU.mult, r=[XR, kh, 'prm'], w=[RK])
                    CP(VB.ap[:, jj, :], XV.ap, r=[XV], w=[VB], q='act')
                    TT(tb.ap, kk.ap, av.ap, ALU.mult, r=[kk, av], w=[tb])
                    TT(tb.ap, tb.ap, pinv.ap, ALU.mult, r=[tb, pinv], w=[tb])
                    CP(BKs.ap[:, jj, :, 0:64], c3(tb.ap), r=[tb], w=[BKs], q='act')
                    pcb = PCs.ap[:, jj, 0:nch].unsqueeze(2).to_broadcast([128, nch, 64])
                    TT(BKc.ap[:, jj, :, 0:64], c3(tb.ap), pcb, ALU.mult, r=[tb, PCs], w=[BKc])
                    TT(kh.ap, kh.ap, pinv.ap, ALU.mult, r=[kh, pinv], w=[kh])
                    CP(BKs.ap[:, jj, :, 64:128], c3(kh.ap), r=[kh], w=[BKs], q='act')
                    TT(BKc.ap[:, jj, :, 64:128], c3(kh.ap), pcb, ALU.mult, r=[kh, PCs], w=[BKc])
                    for (src, dst) in ((ARs, ARo), (BKs, BKo)):
                        p2 = bpair()
                        n = nch * 128
                        for h0 in range(0, n, 512):
                            hn = min(512, n - h0)
                            MM(PQ[p2][:, h0:h0 + hn], PERMb, src.ap[:, jj, :, :].rearrange("p c t -> p (c t)")[:, h0:h0 + hn], True, True,
                               r=['cb', src], w=[('ps', 2 * p2), ('ps', 2 * p2 + 1)])
                        CP(dst.ap[0:64, jj, :, :].rearrange("p c t -> p (c t)"), PQ[p2][0:64, 0:n], r=[('ps', 2 * p2), ('ps', 2 * p2 + 1)], w=[dst], q='act')
                    b = bank()
                    MM(bank_ap(b)[:, 0:8], PERMf, PCs.ap[:, jj, :], True, True, r=['cst', PCs], w=[('ps', b)])
                    CP(PCH.ap[:, :, 2 * jj], PCs.ap[0:64, jj, 0:nch], r=[PCs], w=[PCH])
                    CP(PCH.ap[:, :, 2 * jj + 1], bank_ap(b)[0:64, 0:nch], r=[('ps', b)], w=[PCH])
                srelease(mP)
                NSL = 3
                TOKs = [salloc([2, 3, 128], BF16, parts=64) for _ in range(NSL)]
                GTs = [salloc([4, 4, 64], BF16, parts=64) for _ in range(NSL)]
                Xss = [[salloc([4, 64], BF16, parts=64) for _ in range(2)] for _ in range(NSL)]
                MTss = [[salloc([4, 2, 64], BF16, parts=64) for _ in range(2)] for _ in range(NSL)]
                Wb = salloc([4, 64], BF16, parts=64); Ub = salloc([4, 64], BF16, parts=64); STb = salloc([4, 64], BF16, parts=64)
                ysq = salloc([4, 64], F32, parts=64); yn = salloc([4, 64], F32, parts=64); bon = salloc([4, 64], F32, parts=64)
                stm = salloc([4, 64], F32, parts=64)
                yout = salloc([256], BF16)
                MSET(yout.ap[64:128, :], 0.0, w=[yout])
                ysb = salloc([4, 64], F32, parts=64)
                sm = salloc([8, 4], F32, parts=64)
                STq = ST[:, qd * 4:(qd + 1) * 4, :]
                CP(STb.ap[0:64], STq, r=['st'], w=[STb])
                A_O1 = [0, 2]
                A_O2 = [(1, 0), (3, 0)]

                def ARh(hq, c):
                    jj, hp = hq // 2, hq % 2
                    return (ARs if hp == 0 else ARo).ap[0:64, jj, c, :], (ARs if hp == 0 else ARo)

                def BKh(hq, c):
                    jj, hp = hq // 2, hq % 2
                    return (BKs if hp == 0 else BKo).ap[0:64, jj, c, :], (BKs if hp == 0 else BKo)

                a_done = [False] * nch; b1_done = [False] * nch; b2_done = [False] * nch; ycp = [False] * nch

                def partA(c, pset):
                    if F_NOA:
                        a_done[c] = True
                        return
                    sl = c % NSL
                    TOKc, GT, Xp, MTp = TOKs[sl], GTs[sl], Xss[sl], MTss[sl]
                    cs = slice(c * 64, (c + 1) * 64)
                    b1 = A_O1[pset]; k1 = ('ps', b1)
                    b2, c2 = A_O2[pset]; k2 = ('ps', b2)
                    btv = 4
                    ktv = ('ps', btv)
                    tv = bank_apb(btv)[0:64, 0:768].rearrange("p (j k c) -> p j k c", k=3, c=128)
                    for jj in range(2):
                        TR(tv[:, jj, 0, :], VB.ap[:, jj, cs], IDb, r=[VB, 'cb'], w=[ktv])
                        TR(tv[:, jj, 1, :], BKc.ap[:, jj, c, 0:64], IDb, r=[BKc, 'cb'], w=[ktv])
                        TR(tv[:, jj, 2, :], BKc.ap[:, jj, c, 64:128], IDb, r=[BKc, 'cb'], w=[ktv])
                    xv = bank_ap(b2)[0:64, c2:c2 + 256].rearrange("p (h s) -> p h s", s=64)
                    for hq in range(4):
                        ar, arb = ARh(hq, c)
                        bk, bkb = BKh(hq, c)
                        MM(xv[:, hq, :], ar[:, 0:64], bk[:, 0:64], True, True, r=[arb, bkb], w=[k2])
                    CP(TOKc.ap[0:64], tv, r=[ktv], w=[TOKc], q='act')
                    yield
                    Xc, MTc = Xp[0], MTp[0]
                    TT(Xc.ap[0:64], xv, MASKX.unsqueeze(1).to_broadcast([64, 4, 64]), ALU.mult, r=[k2, 'cst'], w=[Xc])
                    gv = bank_ap(b1)[0:64, 0:512].rearrange("p (h x) -> p h x", x=256)
                    for half in range(2):
                        for h2 in range(2):
                            hq = half * 2 + h2
                            ar, arb = ARh(hq, c)
                            bk, bkb = BKh(hq, c)
                            MM(gv[:, h2, 0:128], bk[:, 0:64], ar, True, True, r=[arb, bkb], w=[k1])
                            MM(gv[:, h2, 128:256], bk[:, 64:128], ar, True, True, r=[arb, bkb], w=[k1])
                        yield
                        TT(GT.ap[0:64, half * 2:half * 2 + 2].rearrange("p h b t -> p h (b t)"), gv, MASKG.unsqueeze(1).to_broadcast([64, 2, 256]), ALU.mult,
                           r=[k1, 'cst'], w=[GT])
                    CP(MTc.ap[0:64, :, 0, :], GT.ap[0:64, :, 0, :], r=[GT], w=[MTc], q='act')
                    CP(MTc.ap[0:64, :, 1, :], I64f.unsqueeze(1).to_broadcast([64, 4, 64]), r=['cst'], w=[MTc])
                    yield
                    o1 = bank_ap(b1)[0:64, 0:512].rearrange("p (h x) -> p h x", x=128)
                    o2 = bank_ap(b2)[0:64, c2:c2 + 256].rearrange("p (h s) -> p h s", s=64)
                    for lev in range(6):
                        Xn, MTn = Xp[(lev + 1) % 2], MTp[(lev + 1) % 2]
                        last = lev == 5
                        for hq in range(4):
                            if last:
                                MM(o1[:, hq, 64:128], Xc.ap[0:64, hq, :], MTc.ap[0:64, hq, 1, :], True, True, r=[Xc, MTc], w=[k1])
                            else:
                                MM(o1[:, hq, :], Xc.ap[0:64, hq, :], MTc.ap[0:64, hq, :, :].rearrange("p a t -> p (a t)"), True, True, r=[Xc, MTc], w=[k1])
                                MM(o2[:, hq, :], MTc.ap[0:64, hq, 0, :], Xc.ap[0:64, hq, :], True, True, r=[Xc, MTc], w=[k2])
                        yield
                        TT(MTn.ap[0:64, :, 1, :], o1[:, :, 64:128], MTc.ap[0:64, :, 1, :], ALU.add, r=[k1, MTc], w=[MTn])
                        if not last:
                            CP(MTn.ap[0:64, :, 0, :], o1[:, :, 0:64], r=[k1], w=[MTn], q='act')
                            CP(Xn.ap[0:64], o2, r=[k2], w=[Xn], q='act')
                        Xc, MTc = Xn, MTn
                        yield
                    assert MTc is MTp[0]
                    a_done[c] = True

                def chainB1():
                    for c in range(nch):
                        while not (a_done[c] and (c < 1 or ycp[c - 1])):
                            yield
                        if F_NOB1:
                            b1_done[c] = True
                            yield
                            continue
                        sl = c % NSL
                        TOKc, GT, TTf = TOKs[sl], GTs[sl], MTss[sl][0]
                        vtok = lambda hq: TOKc.ap[0:64, hq // 2, 0, (hq % 2) * 64:(hq % 2) * 64 + 64]
                        bctok = lambda hq: TOKc.ap[0:64, hq // 2, 1, (hq % 2) * 64:(hq % 2) * 64 + 64]
                        kctok = lambda hq: TOKc.ap[0:64, hq // 2, 2, (hq % 2) * 64:(hq % 2) * 64 + 64]
                        wk_ = ('ps', 5)
                        wv = bank_ap(5)[0:64, 0:256].rearrange("p (h v) -> p h v", v=64)
                        for hq in range(4):
                            ar, arb = ARh(hq, c)
                            MM(wv[:, hq, :], ar[:, 0:64], STb.ap[0:64, hq, :], True, False, r=[arb, STb], w=[wk_])
                            MM(wv[:, hq, :], GT.ap[0:64, hq, 2, :], vtok(hq), False, True, r=[GT, TOKc], w=[wk_])
                        TT(stm.ap[0:64], STq, PCH.ap[0:64, c, :].unsqueeze(2).to_broadcast([64, 4, 64]), ALU.mult, r=['st', PCH], w=[stm])
                        yield
                        CP(Wb.ap[0:64], wv, r=[wk_], w=[Wb], q='act')
                        yield
                        uk = ('ps', 5)
                        uv = bank_ap(5)[0:64, 0:256].rearrange("p (h v) -> p h v", v=64)
                        for hq in range(4):
                            MM(uv[:, hq, :], TTf.ap[0:64, hq, 1, :], Wb.ap[0:64, hq, :], True, True, r=[TTf, Wb], w=[uk])
                        yield
                        CP(Ub.ap[0:64], uv, r=[uk], w=[Ub])
                        yield
                        sk = ('ps', 5)
                        sv = bank_ap(5)[0:64, 0:256].rearrange("p (h v) -> p h v", v=64)
                        for hq in range(4):
                            MM(sv[:, hq, :], bctok(hq), Ub.ap[0:64, hq, :], True, False, r=[TOKc, Ub], w=[sk])
                            MM(sv[:, hq, :], kctok(hq), vtok(hq), False, True, r=[TOKc], w=[sk])
                        yk = ('ps', 6)
                        yv = bank_ap(6)[0:64, 0:256].rearrange("p (h v) -> p h v", v=64)
                        for hq in range(4):
                            ar, arb = ARh(hq, c)
                            MM(yv[:, hq, :], ar[:, 64:128], STb.ap[0:64, hq, :], True, False, r=[arb, STb], w=[yk])
                            MM(yv[:, hq, :], GT.ap[0:64, hq, 1, :], Ub.ap[0:64, hq, :], False, False, r=[GT, Ub], w=[yk])
                            MM(yv[:, hq, :], GT.ap[0:64, hq, 3, :], vtok(hq), False, True, r=[GT, TOKc], w=[yk])
                        yield
                        TT(STq, stm.ap[0:64], sv, ALU.add, r=[stm, sk], w=['st'])
                        yield
                        CP(STb.ap[0:64], STq, r=['st'], w=[STb], q='act')
                        b1_done[c] = True
                        yield

                def chainB2():
                    for c in range(nch):
                        while not b1_done[c]:
                            yield
                        if F_NOB2:
                            b2_done[c] = True
                            yield
                            continue
                        sl = c % NSL
                        TOKc = TOKs[sl]
                        cs = slice(c * 64, (c + 1) * 64)
                        yk = ('ps', 6)
                        yv = bank_ap(6)[0:64, 0:256].rearrange("p (h v) -> p h v", v=64)
                        gk_ = ('ps', 7)
                        CP(ysb.ap[0:64], yv, r=[yk], w=[ysb], q='act')
                        ycp[c] = True
                        yv = ysb.ap[0:64]
                        yk = ysb
                        for jj in range(2):
                            MM(bank_ap(7)[0:64, 256 + 2 * jj:256 + 2 * jj + 2], RK.ap[:, jj, cs], BO2b, True, True, r=[RK, 'cb'], w=[gk_])
                        MM(bank_ap(7)[0:64, 0:256], sg1.ap[:, cs], LB[:, qd * 256:(qd + 1) * 256], True, False, r=[sg1, 'lb'], w=[gk_])
                        MM(bank_ap(7)[0:64, 0:256], sg2.ap[:, cs], LB[:, 1024 + qd * 256:1024 + (qd + 1) * 256], False, True, r=[sg2, 'lb'], w=[gk_])
                        s1 = sm.ap[0:64, 0, :]; s2 = sm.ap[0:64, 1, :]; mean = sm.ap[0:64, 2, :]; msq = sm.ap[0:64, 3, :]
                        var = sm.ap[0:64, 4, :]; rs = sm.ap[0:64, 5, :]; bsc = sm.ap[0:64, 6, :]
                        OP('dve', lambda e, s1=s1, yv=yv: e.tensor_reduce(out=s1, in_=yv, axis=AX.X, op=ALU.add), r=_keys([yk]), w=_keys([sm]))
                        ACT(ysq.ap[0:64], yv, AF.Square, r=[yk], w=[ysq])
                        yield
                        OP('dve', lambda e, s2=s2, ysq=ysq: e.tensor_reduce(out=s2, in_=ysq.ap[0:64], axis=AX.X, op=ALU.add), r=_keys([ysq]), w=_keys([sm]))
                        TS_(mean, s1, 1.0 / 64, None, ALU.mult, ALU.bypass, r=[sm], w=[sm])
                        yield
                        TT(msq, mean, mean, ALU.mult, r=[sm], w=[sm])
                        yield
                        STT(var, s2, 1.0 / 64, msq, ALU.mult, ALU.subtract, r=[sm], w=[sm])
                        yield
                        ACT(var, var, AF.Sqrt, r=[sm], w=[sm], bias=GN_EPS)
                        CP(bsc, bank_ap(7)[0:64, 256:260], r=[gk_], w=[sm], q='act')
                        b3 = lambda ap: ap.unsqueeze(2).to_broadcast([64, 4, 64])
                        TT(yn.ap[0:64], yv, b3(mean), ALU.subtract, r=[yk, sm], w=[yn])
                        yield
                        OP('dve', lambda e, rs=rs, var=var: e.reciprocal(out=rs, in_=var), r=_keys([sm]), w=_keys([sm]))
                        yield
                        TT(yn.ap[0:64], yn.ap[0:64], b3(rs), ALU.mult, r=[yn, sm], w=[yn])
                        gq = gnt.ap[0:64, 0, :].rearrange("p (h v) -> p h v", v=64)
                        bq = gnt.ap[0:64, 1, :].rearrange("p (h v) -> p h v", v=64)
                        vt4 = TOKc.ap[0:64, :, 0, :].rearrange("p j (h v) -> p j h v", v=64)
                        bsc4 = bsc.rearrange("p (j h) -> p j h", h=2).unsqueeze(3).to_broadcast([64, 2, 2, 64])
                        TT(bon.ap[0:64].rearrange("p (j h) v -> p j h v", h=2), vt4, bsc4, ALU.mult, r=[TOKc, sm], w=[bon])
                        yield
                        TT(yn.ap[0:64], yn.ap[0:64], gq, ALU.mult, r=[yn, gnt], w=[yn])
                        yield
                        TT(yn.ap[0:64], yn.ap[0:64], bq, ALU.add, r=[yn, gnt], w=[yn])
                        yield
                        TT(yn.ap[0:64], yn.ap[0:64], bon.ap[0:64], ALU.add, r=[yn, bon], w=[yn])
                        yield
                        TT(yout.ap[0:64], yn.ap[0:64].rearrange("p h v -> p (h v)"), bank_ap(7)[0:64, 0:256], ALU.mult, r=[yn, gk_], w=[yout])
                        yield
                        tk = gk_
                        for jj in range(2):
                            TR(bank_apb(7)[:, 528 + jj * 128:528 + (jj + 1) * 128], yout.ap[:, jj * 128:(jj + 1) * 128], IDb, r=[yout, 'cb'], w=[tk])
                        yield
                        CP(YB[:, qd * 2:qd * 2 + 2, cs], bank_apb(7)[:, 528:784].rearrange("p (j t) -> p j t", t=128)[:, :, 0:64], r=[tk], w=[('yb', qd)], q='act')
                        b2_done[c] = True
                        yield

                gB1 = chainB1(); gB2 = chainB2()
                act_a = {}
                next_a = 0
                b1_alive = b2_alive = True
                guard = 0
                while b1_alive or b2_alive or act_a or next_a < nch:
                    guard += 1
                    assert guard < 100000, "scheduler stuck"
                    while next_a < nch and len(act_a) < F_THREADS and (next_a < NSL or b2_done[next_a - NSL]):
                        ps_free = [k for k in range(2) if k not in act_a][0]
                        act_a[ps_free] = partA(next_a, ps_free)
                        next_a += 1
                    for k in list(act_a.keys()):
                        try:
                            next(act_a[k])
                        except StopIteration:
                            del act_a[k]
                    for _ in range(2):
                        if b1_alive:
                            try:
                                next(gB1)
                            except StopIteration:
                                b1_alive = False
                    for _ in range(3):
                        if b2_alive:
                            try:
                                next(gB2)
                            except StopIteration:
                                b2_alive = False
                srelease(mQ)
            srelease(mB)
            spdma(wk_dst[l], ST[:].rearrange("p h v -> p (h v)"), r=['st'], w=[('wk', l)])

            mA = smark()
            Z = salloc([8, W], F32); DP = salloc([8, T], BF16)
            SA = salloc([W], F32); SB_ = salloc([W], F32); PW = salloc([2048], BF16); tfix = salloc([16], F32)
            pooldma(PW.ap, pw_d[l], w=[PW])
            for ch in range(8):
                b = proj_block(T, win_d[l, ch], KC, hr, hk)
                ACT(Z.ap[:, ch, 15:W], bank_ap(b)[:, 0:T], AF.Copy, r=[('ps', b)], w=[Z])
                CP(Z.ap[:, ch, 0:15], POOLST[:, l, ch, :], r=['poolst'], w=[Z])
                gi = ch // 2
                win = 2 << gi
                zc = Z.ap[:, ch, :]
                TT(SA.ap[:, 1:W], zc[:, 1:W], zc[:, 0:W - 1], ALU.add, r=[Z], w=[SA])
                fin = SA
                if win >= 4:
                    TT(SB_.ap[:, 3:W], SA.ap[:, 3:W], SA.ap[:, 1:W - 2], ALU.add, r=[SA], w=[SB_]); fin = SB_
                if win >= 8:
                    TT(SA.ap[:, 7:W], SB_.ap[:, 7:W], SB_.ap[:, 3:W - 4], ALU.add, r=[SB_], w=[SA]); fin = SA
                if win >= 16:
                    TT(SB_.ap[:, 15:W], SA.ap[:, 15:W], SA.ap[:, 7:W - 8], ALU.add, r=[SA], w=[SB_]); fin = SB_
                STT(DP.ap[:, ch, :], fin.ap[:, 15:W], 1.0 / win, zc[:, 15:W], ALU.mult, ALU.subtract, r=[fin, Z], w=[DP])
                if is_prompt_first:
                    TT(tfix.ap, fin.ap[:, 15:31], RC[:, gi, :], ALU.mult, r=[fin, 'cst'], w=[tfix])
                    TT(DP.ap[:, ch, 0:16], tfix.ap, zc[:, 15:31], ALU.subtract, r=[tfix, Z], w=[DP])
                CP(POOLST[:, l, ch, :], zc[:, T:T + 15], r=[Z], w=['poolst'])
            pwv = PW.ap.rearrange("p (g k o) -> p g k o", k=2, o=256)
            for g in range(4):
                for o2 in range(2):
                    b = bank()
                    for k2 in range(2):
                        MM(bank_ap(b)[:, 0:T], pwv[:, g, k2, o2 * 128:(o2 + 1) * 128], DP.ap[:, 2 * g + k2, :], k2 == 0, k2 == 1, r=[PW, DP], w=[('ps', b)])
                    ACT(YA[:, 2 * g + o2, 0:T], bank_ap(b)[:, 0:T], AF.Copy, r=[('ps', b), 'prm'], w=[('ya', 2 * g + o2)], scale=pcol(C_PSC + 2 * g + o2))
            srelease(mA)

            mC = smark()
            MG = salloc([KC, T], BF16)
            s1b = salloc([T], F32); s2b = salloc([T], F32); m1 = salloc([T], F32); m2 = salloc([T], F32)
            for oc in range(KC):
                half, sub = oc // 2, oc % 2
                if sub == 0:
                    spool = wload(ppj_d[l, half])
                    srw = wload(prw_d[l, half])
                bp1 = bank()
                for kc in range(8):
                    MM(bank_ap(bp1)[:, 0:T], WR[:, spool, sub * 1024 + kc * 128: sub * 1024 + (kc + 1) * 128], YA[:, kc, 0:T], kc == 0, kc == 7,
                       r=[('w', spool), ('ya', kc)], w=[('ps', bp1)])
                bp2 = bank()
                for kc in range(8):
                    MM(bank_ap(bp2)[:, 0:T], WR[:, srw, sub * 1024 + kc * 128: sub * 1024 + (kc + 1) * 128], YB[:, kc, 0:T], kc == 0, kc == 7,
                       r=[('w', srw), ('yb', kc // 2)], w=[('ps', bp2)])
                bg1 = proj_block(T, win_d[l, 36 + oc], KC, hr, hk)
                bg2 = proj_block(T, win_d[l, 52 + oc], KC, hr, hk)
                if sub == 1:
                    wfree(spool); wfree(srw)
                ACT(s1b.ap, bank_ap(bg1)[:, 0:T], AF.Sigmoid, r=[('ps', bg1)], w=[s1b])
                ACT(s2b.ap, bank_ap(bg2)[:, 0:T], AF.Sigmoid, r=[('ps', bg2)], w=[s2b])
                TT(m1.ap, s1b.ap, bank_ap(bp1)[:, 0:T], ALU.mult, r=[s1b, ('ps', bp1)], w=[m1])
                TT(m2.ap, s2b.ap, bank_ap(bp2)[:, 0:T], ALU.mult, r=[s2b, ('ps', bp2)], w=[m2])
                TT(MG.ap[:, oc, :], m1.ap, m2.ap, ALU.add, r=[m1, m2], w=[MG])
            for oc in range(KC):
                b = proj_block(T, wo_d[l, oc], KC, lambda kc: MG.ap[:, kc, :], lambda kc: [MG])
                TT(X[:, oc, 0:T], X[:, oc, 0:T], bank_ap(b)[:, 0:T], ALU.add, r=[('x', oc), ('ps', b)], w=[('x', oc)])
            srelease(mC)

            rmsnorm(T, PRM[:, C_NF:C_NF + KC], 'prm', lambda kc: H[:, kc, 0:T], lambda kc: [('h', kc)])
            mD = smark()
            ACTB = [salloc([FGC, T], BF16) for _ in range(2)]
            sgb = [salloc([T], F32) for _ in range(2)]
            for fg in range(FG):
                ab = ACTB[fg % 2]
                for fi in range(FGC):
                    fc = fg * FGC + fi
                    bg_ = proj_block(T, wg_d[l, fc], KC, hr, hk)
                    bu_ = proj_block(T, wu_d[l, fc], KC, hr, hk)
                    sg = sgb[fi % 2]
                    ACT(sg.ap, bank_ap(bg_)[:, 0:T], AF.Silu, r=[('ps', bg_)], w=[sg])
                    TT(ab.ap[:, fi, :], sg.ap, bank_ap(bu_)[:, 0:T], ALU.mult, r=[sg, ('ps', bu_)], w=[ab])
                for oc in range(KC):
                    b = proj_block(T, wd_d[l, fg, oc], FGC, lambda kc: ab.ap[:, kc, :], lambda kc: [ab])
                    TT(X[:, oc, 0:T], X[:, oc, 0:T], bank_ap(b)[:, 0:T], ALU.add, r=[('x', oc), ('ps', b)], w=[('x', oc)])
            srelease(mD)

            rmsnorm(T, PRM[:, C_NP:C_NP + KC], 'prm', lambda kc: H[:, kc, 0:T], lambda kc: [('h', kc)])
            mE = smark()
            PT = salloc([2, T], BF16); sp_ = salloc([T], F32); tp_ = salloc([T], F32)
            PJ = [salloc([2048], BF16) for _ in range(2)]
            pooldma(PT.ap, p_src(l), w=[PT])
            pooldma(PJ[0].ap, pj_d[l, 0], w=[PJ[0]])
            pooldma(PJ[1].ap, pj_d[l, 1], w=[PJ[1]])
            for oc in range(KC):
                pjb = PJ[oc // 8]
                bgp = proj_block(T, pg_d[l, oc], KC, hr, hk)
                bpp = bank()
                for k2 in range(2):
                    MM(bank_ap(bpp)[:, 0:T], pjb.ap[:, (oc % 8) * 256 + k2 * 128:(oc % 8) * 256 + (k2 + 1) * 128], PT.ap[:, k2, :], k2 == 0, k2 == 1,
                       r=[pjb, PT], w=[('ps', bpp)])
                ACT(sp_.ap, bank_ap(bgp)[:, 0:T], AF.Sigmoid, r=[('ps', bgp)], w=[sp_])
                TT(tp_.ap, sp_.ap, bank_ap(bpp)[:, 0:T], ALU.mult, r=[sp_, ('ps', bpp)], w=[tp_])
                TT(X[:, oc, 0:T], X[:, oc, 0:T], tp_.ap, ALU.add, r=[('x', oc), tp_], w=[('x', oc)])
            srelease(mE)

        def run_segment(kind, t0, T, first, last):
            xsrc = xp_d if kind == 'p' else xs_d
            ydst = yp_d if kind == 'p' else ys_d
            spdma(X[:, :, 0:T], xsrc[:, :, t0:t0 + T].rearrange("c p t -> p c t"), w=[('x', kc) for kc in range(KC)])
            if first:
                if kind == 'p':
                    MSET(POOLST[:], 0.0, w=['poolst'])
                    MSET(SHIFT[:], 0.0, w=['shift'])
                else:
                    spdma(POOLST[:].rearrange("p l c t -> p (l c t)"), stp_d, w=['poolst'])
                    spdma(SHIFT[:].rearrange("p l q -> p (l q)"), sts_d, w=['shift'])
            for l in range(L):
                if kind == 'p':
                    psrc = lambda l, t0=t0, T=T: pp_d[l, :, :, t0:t0 + T].rearrange("k p t -> p k t")
                    layer_tile(l, T, first, first, wkp_d, wkp_d, psrc)
                else:
                    psrc = lambda l: psm_d[l].rearrange("k p t -> p k t")
                    layer_tile(l, T, first, False, stw_d, wks_d, psrc)
            m = smark()
            YO = salloc([KC, T], F32)
            rmsnorm(T, NFIN[:], 'nfin', lambda kc: YO.ap[:, kc, :], lambda kc: list(YO.keys))
            spdma(ydst[:, :, t0:t0 + T].rearrange("c p t -> p c t"), YO.ap, r=[YO])
            srelease(m)
            if last:
                spdma(shp_d if kind == 'p' else shs_d, SHIFT[:].rearrange("p l q -> p (l q)"), r=['shift'])
                spdma(pop_d if kind == 'p' else pos_d, POOLST[:].rearrange("p l c t -> p (l c t)"), r=['poolst'])

        ntile = TP // TILE
        for ti in range(ntile):
            run_segment('p', ti * TILE, TILE, ti == 0, ti == ntile - 1)
        run_segment('s', 0, TS, True, True)

        cnt = S.analyze()
        sems = {k: es.enter_context(nc.semaphore("sem_%s_%s" % k)) for k in cnt}
        block = es.enter_context(nc.Block())
        S.emit(sems, block)
    return nc, len(S.ops)


def rw_cols():
    cols = []
    cols.append(list(range(3072, 3136)) + [-1] * 64)
    cols.append(list(range(3136, 3200)) + [-1] * 64)
    cols.append(list(range(3200, 3328)))
    cols.append(list(range(3328, 3360)) + [-1] * 96)
    for j in range(8):
        for base in (0, 1024, 2048):
            cols.append(list(range(base + 128 * j, base + 128 * (j + 1))))
    return np.array(cols, dtype=np.int64)


def make_consts():
    c = np.zeros((128, NCST), np.float32)
    c[:, K_ID:K_ID + 128] = np.eye(128)
    for m in range(128):
        c[(m + 64) % 128, K_PERM + m] = 1.0
    c[:, K_ONES:K_ONES + 128] = 1.0
    p = np.arange(128)
    c[:, K_BLK2:K_BLK2 + 128] = (p[:, None] // 64 == p[None, :] // 64)
    c[:, K_BO2:K_BO2 + 2] = (p[:, None] // 64 == np.arange(2)[None, :])
    s = np.arange(64)[:, None]; t = np.arange(64)[None, :]
    strict = (t > s).astype(np.float32); incl = (t >= s).astype(np.float32)
    c[0:64, K_MG:K_MG + 256] = np.concatenate([strict, incl, strict, incl], axis=1)
    c[0:64, K_MX:K_MX + 64] = (s > t)
    rst = np.ones(512, np.float32); rst[::64] = 0.0
    c[:, K_RST:K_RST + 512] = rst[None, :]
    for gi, win in enumerate((2, 4, 8, 16)):
        c[:, K_RC + gi * 16:K_RC + (gi + 1) * 16] = (1.0 / np.minimum(np.arange(16) + 1, win))[None, :]
    return c


def kblocks(w, kc_n):
    K, N = w.shape
    return np.ascontiguousarray(w.reshape(kc_n, 128, N // 128, 128).transpose(2, 1, 0, 3).reshape(N // 128, 128, kc_n * 128))


def layout_weights(inp, L):
    f = lambda k: np.asarray(inp[k], dtype=np.float32)
    cols = rw_cols()
    out = {}
    w_in = f('w_in')
    win_b = np.zeros((L, NWIN, 128, 2048), np.float32)
    lora_b = np.zeros((L, 2, 128, 2048), np.float32)
    prm = np.zeros((L, 128, NPRM), np.float32)
    mu = f('mu_shift')
    for l in range(L):
        wl = w_in[l]
        win_b[l, 0:8] = kblocks(wl[:, 0:1024], KC)
        rwm = np.zeros((D, NQ * 128), np.float32)
        cc = cols.reshape(-1)
        ok = cc >= 0
        rwm[:, ok] = wl[:, 1024 + cc[ok]]
        win_b[l, 8:36] = kblocks(rwm, KC)
        win_b[l, 36:52] = kblocks(wl[:, 1024 + 3360:1024 + 3360 + 2048], KC)
        win_b[l, 52:68] = kblocks(wl[:, 1024 + 3360 + 2048:], KC)
        lora_b[l, 0, 0:64, 0:1024] = f('w_decay_up')[l]
        lora_b[l, 0, 0:64, 1024:2048] = f('w_aaa_up')[l]
        lora_b[l, 1, :, 0:1024] = f('w_gate_up')[l][0:128]
        lora_b[l, 1, 0:32, 1024:2048] = f('w_gate_up')[l][128:160]
        pc = lambda v: v.reshape(-1, 128).T
        prm[l, :, C_NM:C_NM + 16] = pc(f('norm_mix')[l]); prm[l, :, C_NF:C_NF + 16] = pc(f('norm_ffn')[l]); prm[l, :, C_NP:C_NP + 16] = pc(f('norm_ple')[l])
        mum = np.zeros(NQ * 128, np.float32); mum[ok] = mu[l][cc[ok]]
        prm[l, :, C_MU:C_MU + NQ] = mum.reshape(NQ, 128).T
        prm[l, :, C_PSC:C_PSC + 8] = pc(f('pool_scale')[l]); prm[l, :, C_W0:C_W0 + 8] = pc(f('w0')[l]); prm[l, :, C_A0:C_A0 + 8] = pc(f('a0')[l])
        prm[l, :, C_KK:C_KK + 8] = pc(f('k_k')[l]); prm[l, :, C_KA:C_KA + 8] = pc(f('k_a')[l]); prm[l, :, C_RK:C_RK + 8] = pc(f('r_k')[l].reshape(-1))
    out['w_in_b'] = win_b; out['lora_b'] = lora_b; out['prm'] = prm
    pw = f('pool_w').reshape(L, 4, 2, 128, 256).transpose(0, 3, 1, 2, 4).reshape(L, 128, 2048)
    out['pw_b'] = np.ascontiguousarray(pw)

    def two_oc(w):
        r = np.stack([kblocks(w[l], 8) for l in range(L)])
        return np.ascontiguousarray(r.reshape(L, 8, 2, 128, 1024).transpose(0, 1, 3, 2, 4).reshape(L, 8, 128, 2048))
    out['ppool_b'] = two_oc(f('proj_pool')); out['prwkv_b'] = two_oc(f('proj_rwkv'))
    out['wo_b'] = np.stack([kblocks(f('w_out')[l], KC) for l in range(L)])
    out['wg_b'] = np.stack([kblocks(f('w_ffn_gate')[l], KC) for l in range(L)])
    out['wu_b'] = np.stack([kblocks(f('w_ffn_up')[l], KC) for l in range(L)])
    wd = f('w_ffn_down')
    out['wd_b'] = np.stack([np.stack([kblocks(wd[l][fg * FGC * 128:(fg + 1) * FGC * 128], FGC) for fg in range(FG)]) for l in range(L)])
    out['pg_b'] = np.stack([kblocks(f('w_ple_gate')[l], KC) for l in range(L)])
    pj = np.stack([kblocks(f('w_ple_proj')[l], 2) for l in range(L)])
    out['pj_b'] = np.ascontiguousarray(pj.reshape(L, 2, 8, 128, 256).transpose(0, 1, 3, 2, 4).reshape(L, 2, 128, 2048))
    out['nfin'] = np.ascontiguousarray(f('norm_final').reshape(16, 128).T)
    out['gn'] = np.ascontiguousarray(np.concatenate([f('gn_gain'), f('gn_bias')], axis=1))
    out['cst'] = make_consts()
    return out, cols


_CACHE = {}


def run(inputs, L, TP, TS, TILE, ncores):
    key = (L, TP, TS, TILE)
    if key not in _CACHE:
        _CACHE[key] = build_program(L, TP, TS, TILE)[0]
    nc = _CACHE[key]
    f = lambda k: np.asarray(inputs[k], dtype=np.float32)
    shared, cols = layout_weights(inputs, L)
    cc = cols.reshape(-1); ok = cc >= 0
    in_maps = []
    for b in range(ncores):
        m = dict(shared)
        m['xp'] = np.ascontiguousarray(f('x_prompt')[b].T.reshape(KC, 128, TP))
        m['xs'] = np.ascontiguousarray(f('x_sample')[b].T.reshape(KC, 128, TS))
        m['pp'] = np.ascontiguousarray(f('p_prompt')[:, b].transpose(0, 2, 1).reshape(L, 2, 128, TP))
        m['psm'] = np.ascontiguousarray(f('p_sample')[:, b].transpose(0, 2, 1).reshape(L, 2, 128, TS))
        ss = np.zeros((L, NQ * 128), np.float32); ss[:, ok] = f('state_shift')[:, b][:, cc[ok]]
        m['st_shift'] = np.ascontiguousarray(ss.reshape(L, NQ, 128).transpose(2, 0, 1).reshape(128, L * NQ))
        sp = f('state_pool')[:, b]
        m['st_pool'] = np.ascontiguousarray(sp.reshape(L, 15, 8, 128).transpose(3, 0, 2, 1).reshape(128, L * 120))
        sw = f('state_wkv')[:, b]
        m['st_wkv'] = np.ascontiguousarray(sw.transpose(0, 3, 1, 2).reshape(L, 64, 1024))
        in_maps.append(m)
    res = run_bass_kernel_spmd(nc, in_maps, core_ids=list(range(ncores)))
    R = res.results
    inv = np.full(3360, -1, np.int64); inv[cc[ok]] = np.nonzero(ok)[0]

    def unshift(a):
        v = a.reshape(128, L, NQ).transpose(1, 2, 0).reshape(L, NQ * 128)
        return v[:, inv]
    y_p = np.stack([R[b]['yp'].reshape(D, TP).T for b in range(ncores)])
    y_s = np.stack([R[b]['ys'].reshape(D, TS).T for b in range(ncores)])
    outs = [y_p, y_s]
    for sfx in ('p', 's'):
        sh = np.stack([unshift(R[b]['sh_' + sfx]) for b in range(ncores)], axis=1)
        po = np.stack([R[b]['po_' + sfx].reshape(128, L, 8, 15).transpose(1, 3, 2, 0).reshape(L, 15, 1024) for b in range(ncores)], axis=1)
        wk = np.stack([R[b]['wk_' + sfx].reshape(L, 64, 16, 64).transpose(0, 2, 3, 1) for b in range(ncores)], axis=1)
        outs += [sh, po, wk]
    return tuple(np.ascontiguousarray(o.astype(np.float32)) for o in outs)


def kernel(**inputs):
    return run(inputs, 4, 2048, 64, 512, 8)
```

```python
import contextlib
import os
F_THREADS = int(os.environ.get('F_THREADS', '2'))
F_YV1 = int(os.environ.get('F_YV1', '0'))
F_NOB2 = int(os.environ.get('F_NOB2', '0'))
F_NOB1 = int(os.environ.get('F_NOB1', '0'))
F_NOA = int(os.environ.get('F_NOA', '0'))
F_TVB = int(os.environ.get('F_TVB', '0'))
import numpy as np
import concourse.bass as bass
import concourse.mybir as mybir
from concourse.bass_utils import run_bass_kernel_spmd

F32 = mybir.dt.float32
BF16 = mybir.dt.bfloat16
AF = mybir.ActivationFunctionType
ALU = mybir.AluOpType
AX = mybir.AxisListType

D = 2048
KC = 16
POOLW = 1024
RWW = 1024
NHEAD = 16
DFF = 5632
FC = 44
FG = 4
FGC = 11
PLE = 256
NQ = 28
NWIN = 68
NPRM = 124
C_NM, C_NF, C_NP, C_MU, C_PSC, C_W0, C_A0, C_KK, C_KA, C_RK = 0, 16, 32, 48, 76, 84, 92, 100, 108, 116
K_ID, K_PERM, K_ONES, K_BLK2, K_BO2, K_MG, K_MX, K_RST, K_RC = 0, 128, 256, 384, 512, 514, 770, 834, 1346
NCST = 1410
NORM_EPS = 1e-6
GN_EPS = 64e-5
C0 = float(np.exp(-0.5))
NSLOT = 6
SCR_KB = 72


class Sched:
    QUEUES = ['pe', 'act', 'dve', 'pool', 'sp']

    def __init__(self):
        self.ops = []
        self.cap = None

    def op(self, q, fn, reads=(), writes=(), stream=None, cost=300):
        o = dict(q=q, fn=fn, reads=reads, writes=writes, stream=stream, cost=cost)
        if self.cap is not None:
            assert stream is None
            self.cap.append(o)
        else:
            self.ops.append(o)

    def begin_capture(self):
        self.cap = []

    def end_capture(self, lat=180):
        ops = self.cap
        self.cap = None
        n = len(ops)
        last_w = {}
        readers = {}
        preds = [set() for _ in range(n)]
        for i, o in enumerate(ops):
            for r in o['reads']:
                j = last_w.get(r)
                if j is not None:
                    preds[i].add(j)
            for w in o['writes']:
                j = last_w.get(w)
                if j is not None:
                    preds[i].add(j)
                for j in readers.get(w, ()):
                    preds[i].add(j)
            preds[i].discard(i)
            for r in o['reads']:
                readers.setdefault(r, []).append(i)
            for w in o['writes']:
                last_w[w] = i
                readers[w] = []
        succs = [[] for _ in range(n)]
        indeg = [0] * n
        for i in range(n):
            indeg[i] = len(preds[i])
            for j in preds[i]:
                succs[j].append(i)
        finish = [0.0] * n
        free = {}
        ready = [i for i in range(n) if indeg[i] == 0]
        ready_t = {i: 0.0 for i in ready}
        order = []
        while ready:
            best = None
            for i in ready:
                st = max(free.get(ops[i]['q'], 0.0), ready_t[i])
                key = (st, i)
                if best is None or key < best[0]:
                    best = (key, i)
            (st, _), i = best
            ready.remove(i)
            q = ops[i]['q']
            fin = st + ops[i]['cost']
            finish[i] = fin
            free[q] = fin
            order.append(i)
            for k in succs[i]:
                indeg[k] -= 1
                extra = 0.0 if (q == 'pe' and ops[k]['q'] == 'pe') else lat
                ready_t[k] = max(ready_t.get(k, 0.0), fin + extra)
                if indeg[k] == 0:
                    ready.append(k)
        assert len(order) == n
        for i in order:
            self.ops.append(ops[i])
        return max(finish) if n else 0.0

    def analyze(self):
        ops = self.ops
        last_w = {}
        readers = {}
        last_on_stream = {}
        for i, o in enumerate(ops):
            deps = set()
            for r in o['reads']:
                j = last_w.get(r)
                if j is not None:
                    deps.add(j)
            for w in o['writes']:
                j = last_w.get(w)
                if j is not None:
                    deps.add(j)
                rd = readers.get(w)
                if rd:
                    deps.update(rd.values())
            if o['stream'] is not None:
                j = last_on_stream.get(o['stream'])
                if j is not None:
                    deps.add(j)
                last_on_stream[o['stream']] = i
            deps.discard(i)
            ck = ('s', o['stream']) if o['stream'] is not None else ('q', o['q'])
            o['ck'] = ck
            for r in o['reads']:
                readers.setdefault(r, {})[ck] = i
            for w in o['writes']:
                last_w[w] = i
                readers[w] = {}
            o['deps'] = deps
            o['sig'] = o['stream'] is not None
        for o in ops:
            nd = set()
            for d in o['deps']:
                p = ops[d]
                if p['ck'] == ('q', 'pe') and o['ck'] == ('q', 'pe'):
                    continue
                nd.add(d)
                p['sig'] = True
            o['deps'] = nd
        cnt = {}
        for o in ops:
            if o['sig']:
                k = o['ck']
                cnt[k] = cnt.get(k, 0) + (16 if o['stream'] is not None else 1)
                o['semval'] = cnt[k]
        self.final_counts = cnt
        known = {q: {} for q in self.QUEUES}
        for o in ops:
            need = {}
            for d in o['deps']:
                p = ops[d]
                k = p['ck']
                v = p['semval']
                if v > need.get(k, 0):
                    need[k] = v
            waits = []
            kq = known[o['q']]
            for k, v in need.items():
                if kq.get(k, 0) >= v:
                    continue
                kq[k] = v
                waits.append((k, v))
            o['waits'] = waits
        return cnt

    def emit(self, sems, block):
        ops = self.ops
        byq = {q: [o for o in ops if o['q'] == q] for q in self.QUEUES}
        dec = {'pe': block.tensor, 'act': block.scalar, 'dve': block.vector, 'pool': block.gpsimd, 'sp': block.sync}
        final = list(self.final_counts.items())
        for q in self.QUEUES:
            lst = byq[q]

            def body(eng, lst=lst, q=q):
                for o in lst:
                    for (k, v) in o['waits']:
                        eng.wait_ge(sems[k], v)
                    ins = o['fn'](eng)
                    if o['sig']:
                        ins.then_inc(sems[o['ck']], 16 if o['stream'] is not None else 1)
                if q == 'sp':
                    for k, v in final:
                        eng.wait_ge(sems[k], v)
            dec[q](body)


class Buf:
    def __init__(self, ap, keys):
        self.ap = ap
        self.keys = tuple(keys)


def _keys(items):
    out = []
    for it in items:
        if isinstance(it, Buf):
            out.extend(it.keys)
        elif isinstance(it, list):
            out.extend(_keys(it))
        elif isinstance(it, tuple) and len(it) == 2 and it[0] == 'ps':
            out.append(('psh', it[1], 0))
            out.append(('psh', it[1], 256))
        else:
            out.append(it)
    return tuple(out)


def build_program(L, TP, TS, TILE):
    nc = bass.Bass("TRN2", target_bir_lowering=False)
    S = Sched()

    def OP(q, fn, r=(), w=(), stream=None, cost=300):
        S.op(q, fn, _keys(r), _keys(w), stream, cost)

    def nel(ap):
        n = 1
        for d in ap.shape[1:]:
            n *= int(d)
        return n

    def din(name, shape):
        return nc.dram_tensor(name, list(shape), F32, kind="ExternalInput").ap()

    def dout(name, shape):
        return nc.dram_tensor(name, list(shape), F32, kind="ExternalOutput").ap()

    xp_d = din("xp", [KC, 128, TP]); xs_d = din("xs", [KC, 128, TS])
    pp_d = din("pp", [L, 2, 128, TP]); psm_d = din("psm", [L, 2, 128, TS])
    sts_d = din("st_shift", [128, L * NQ]); stp_d = din("st_pool", [128, L * 120]); stw_d = din("st_wkv", [L, 64, 1024])
    prm_d = din("prm", [L, 128, NPRM]); nfin_d = din("nfin", [128, KC]); gn_d = din("gn", [L, 2048]); cst_d = din("cst", [128, NCST])
    win_d = din("w_in_b", [L, NWIN, 128, 2048]); pw_d = din("pw_b", [L, 128, 2048]); lora_d = din("lora_b", [L, 2, 128, 2048])
    ppj_d = din("ppool_b", [L, 8, 128, 2048]); prw_d = din("prwkv_b", [L, 8, 128, 2048]); wo_d = din("wo_b", [L, 16, 128, 2048])
    wg_d = din("wg_b", [L, FC, 128, 2048]); wu_d = din("wu_b", [L, FC, 128, 2048]); wd_d = din("wd_b", [L, FG, 16, 128, FGC * 128])
    pg_d = din("pg_b", [L, 16, 128, 2048]); pj_d = din("pj_b", [L, 2, 128, 2048])
    yp_d = dout("yp", [KC, 128, TP]); ys_d = dout("ys", [KC, 128, TS])
    shp_d = dout("sh_p", [128, L * NQ]); shs_d = dout("sh_s", [128, L * NQ])
    pop_d = dout("po_p", [128, L * 120]); pos_d = dout("po_s", [128, L * 120])
    wkp_d = dout("wk_p", [L, 64, 1024]); wks_d = dout("wk_s", [L, 64, 1024])

    es = contextlib.ExitStack()
    with es:
        def sbt(name, shape, dt):
            return es.enter_context(nc.sbuf_tensor(name, list(shape), dt))

        TM = TILE
        X = sbt("X", [128, KC, TM], F32)
        H = sbt("H", [128, KC, TM], BF16)
        YA = sbt("YA", [128, 8, TM], BF16)
        YB = sbt("YB", [128, 8, TM], BF16)
        ST = sbt("ST", [64, NHEAD, 64], F32)
        POOLST = sbt("POOLST", [128, L, 8, 15], F32)
        SHIFT = sbt("SHIFT", [128, L, NQ], F32)
        WR = sbt("WR", [128, NSLOT, 2048], BF16)
        LA = sbt("LA", [128, 2048], BF16)
        LB = sbt("LB", [128, 2048], BF16)
        CST = sbt("CST", [128, NCST], F32)
        CB = sbt("CB", [128, 514], BF16)
        PRM = sbt("PRM", [128, NPRM], F32)
        NFIN = sbt("NFIN", [128, KC], F32)
        SCR = sbt("SCR", [128, SCR_KB * 256], F32)
        SCRf = SCR[:]
        SCRb = SCR[:].bitcast(BF16)
        PQ = [es.enter_context(nc.psum_tensor("PQ%d" % i, [128, 1024], F32)) for i in range(4)]
        PQb = [p[:].bitcast(BF16) for p in PQ]

        IDb = CB[:, 0:128]; PERMb = CB[:, 128:256]; ONESb = CB[:, 256:384]; BLK2b = CB[:, 384:512]; BO2b = CB[:, 512:514]
        PERMf = CST[:, K_PERM:K_PERM + 128]

        scr_top = [0]

        def salloc(free_shape, dt, parts=128):
            n = int(np.prod(free_shape))
            nbytes = n * (4 if dt == F32 else 2)
            nslab = (nbytes + 1023) // 1024
            off = scr_top[0]
            scr_top[0] += nslab
            assert scr_top[0] <= SCR_KB, "scratch overflow %d" % scr_top[0]
            if dt == F32:
                ap = SCRf[0:parts, off * 256: off * 256 + n]
            else:
                ap = SCRb[0:parts, off * 512: off * 512 + n]
            if len(free_shape) == 2:
                ap = ap.rearrange("p (a b) -> p a b", b=free_shape[1])
            elif len(free_shape) == 3:
                ap = ap.rearrange("p (a b c) -> p a b c", b=free_shape[1], c=free_shape[2])
            elif len(free_shape) == 4:
                ap = ap.rearrange("p (a b c d) -> p a b c d", b=free_shape[1], c=free_shape[2], d=free_shape[3])
            return Buf(ap, [('scr', off + i) for i in range(nslab)])

        def smark():
            return scr_top[0]

        def srelease(m):
            scr_top[0] = m

        bank_rr = [0]

        def bank():
            b = bank_rr[0] % 8
            bank_rr[0] += 1
            return b

        def bank_ap(b):
            return PQ[b // 2][:, (b % 2) * 512:(b % 2) * 512 + 512]

        def bank_apb(b):
            return PQb[b // 2][:, (b % 2) * 1024:(b % 2) * 1024 + 1024]

        pair_rr = [0]

        def bpair():
            p = pair_rr[0] % 4
            pair_rr[0] += 1
            return p

        wr_ctr = [0]

        wr_busy = [False] * NSLOT

        def wfree(s):
            wr_busy[s] = False

        def wload(src_ap):
            n = src_ap.shape[-1]
            s = wr_ctr[0] % NSLOT
            wr_ctr[0] += 1
            assert not wr_busy[s], "weight ring slot still pinned"
            wr_busy[s] = True
            OP('pool', lambda e, s=s, src_ap=src_ap, n=n: e.dma_start(out=WR[:, s, 0:n], in_=src_ap), w=[('w', s)], stream='w%d' % s)
            return s

        sp_ctr = [0]

        def spdma(out, in_, r=(), w=()):
            st = 'sp%d' % (sp_ctr[0] % 4)
            sp_ctr[0] += 1
            OP('sp', lambda e, out=out, in_=in_: e.dma_start(out=out, in_=in_), r=r, w=w, stream=st)

        def pooldma(out, in_, r=(), w=()):
            OP('pool', lambda e, out=out, in_=in_: e.dma_start(out=out, in_=in_), r=r, w=w, stream='pm')

        def MM(out, lhsT, rhs, start, stop, r, w):
            OP('pe', lambda e, out=out, lhsT=lhsT, rhs=rhs, start=start, stop=stop: e.matmul(out, lhsT=lhsT, rhs=rhs, start=start, stop=stop), r=r, w=w,
               cost=40 + 0.45 * nel(rhs))

        def TR(out, in_, ident, r, w):
            OP('pe', lambda e, out=out, in_=in_, ident=ident: e.transpose(out, in_, ident), r=r, w=w, cost=75)

        def ACT(out, in_, func, r, w, bias=0.0, scale=1.0):
            OP('act', lambda e, out=out, in_=in_, func=func, bias=bias, scale=scale: e.activation(out=out, in_=in_, func=func, bias=bias, scale=scale), r=r, w=w,
               cost=230 + 0.85 * nel(out))

        def TT(out, in0, in1, op, r, w, q='dve'):
            OP(q, lambda e, out=out, in0=in0, in1=in1, op=op: e.tensor_tensor(out=out, in0=in0, in1=in1, op=op), r=r, w=w, cost=130 + 1.05 * nel(out))

        def TS_(out, in0, s1, s2, op0, op1, r, w):
            OP('dve', lambda e, out=out, in0=in0, s1=s1, s2=s2, op0=op0, op1=op1: e.tensor_scalar(out=out, in0=in0, scalar1=s1, scalar2=s2, op0=op0, op1=op1), r=r, w=w,
               cost=130 + 1.05 * nel(out))

        def STT(out, in0, sc, in1, op0, op1, r, w):
            OP('dve', lambda e, out=out, in0=in0, sc=sc, in1=in1, op0=op0, op1=op1: e.scalar_tensor_tensor(out=out, in0=in0, scalar=sc, in1=in1, op0=op0, op1=op1), r=r, w=w,
               cost=130 + 1.05 * nel(out))

        def CP(out, in_, r, w, q='dve'):
            if q == 'act':
                ACT(out, in_, AF.Copy, r, w)
            else:
                OP('dve', lambda e, out=out, in_=in_: e.tensor_copy(out=out, in_=in_), r=r, w=w, cost=130 + 1.05 * nel(out))

        def MSET(ap, val, w):
            OP('dve', lambda e, ap=ap, val=val: e.memset(ap, val), w=w)

        spdma(CST[:], cst_d, w=['cst'])
        spdma(NFIN[:], nfin_d, w=['nfin'])
        CP(CB[:, 0:514], CST[:, 0:514], r=['cst'], w=['cb'])
        MASKG = CST[0:64, K_MG:K_MG + 256]
        MASKX = CST[0:64, K_MX:K_MX + 64]
        I64f = CST[0:64, K_ID:K_ID + 64]
        RC = CST[:, K_RC:K_RC + 64].rearrange("p (g t) -> p g t", t=16)

        def rmsnorm(T, gsrc, gkey, dst_fn, dst_key_fn):
            m = smark()
            sqb = [salloc([T], BF16) for _ in range(4)]
            rt = salloc([T], F32)
            rstd = salloc([T], F32)
            b = bank()
            pk = ('ps', b)
            for kc in range(KC):
                sq = sqb[kc % 4]
                ACT(sq.ap, X[:, kc, 0:T], AF.Square, r=[('x', kc)], w=[sq])
                MM(bank_ap(b)[:, 0:T], ONESb, sq.ap, kc == 0, kc == KC - 1, r=[sq, 'cb'], w=[pk])
            ACT(rt.ap, bank_ap(b)[:, 0:T], AF.Sqrt, r=[pk], w=[rt], bias=NORM_EPS, scale=1.0 / D)
            OP('dve', lambda e: e.reciprocal(out=rstd.ap, in_=rt.ap), r=_keys([rt]), w=_keys([rstd]))
            for kc in range(KC):
                STT(dst_fn(kc), X[:, kc, 0:T], gsrc[:, kc:kc + 1], rstd.ap, ALU.mult, ALU.mult,
                    r=[('x', kc), gkey, rstd], w=list(dst_key_fn(kc)))
            srelease(m)

        def proj_block(T, blk_ap, n_k, rhs_fn, rhs_keys_fn, lhs_off=0, m=128):
            s = wload(blk_ap)
            b = bank()
            for kc in range(n_k):
                MM(bank_ap(b)[0:m, 0:T], WR[:, s, lhs_off + kc * 128: lhs_off + kc * 128 + m], rhs_fn(kc), kc == 0, kc == n_k - 1,
                   r=[('w', s)] + list(rhs_keys_fn(kc)), w=[('ps', b)])
            wfree(s)
            return b

        def layer_tile(l, T, first, is_prompt_first, wk_src, wk_dst, p_src):
            nch = T // 64
            W = 15 + T
            spdma(PRM[:], prm_d[l], w=['prm'])
            pooldma(LA[:], lora_d[l, 0], w=['la'])
            pooldma(LB[:], lora_d[l, 1], w=['lb'])
            if is_prompt_first:
                MSET(ST[:], 0.0, w=['st'])
            else:
                spdma(ST[:].rearrange("p h v -> p (h v)"), wk_src[l], r=[('wk', l)] if wk_src is wkp_d else [], w=['st'])

            def pcol(c):
                return PRM[:, c:c + 1]

            rmsnorm(T, PRM[:, C_NM:C_NM + KC], 'prm', lambda kc: H[:, kc, 0:T], lambda kc: [('h', kc)])
            hr = lambda kc: H[:, kc, 0:T]
            hk = lambda kc: [('h', kc)]

            mB = smark()
            tw = salloc([T], BF16); ad = salloc([T], BF16); sg1 = salloc([T], BF16); sg2 = salloc([T], BF16)
            gnt = salloc([2, 256], F32, parts=64)
            RWT = {}

            def rw_alloc():
                rwb_all = salloc([3, T + 1], F32)
                RWT['rwb'] = [Buf(rwb_all.ap[:, i, :], rwb_all.keys) for i in range(3)]
                RWT['xm'] = [salloc([T], F32) for _ in range(3)]
                RWT['tmpA'] = salloc([T], F32)

            def rw_chunk(q, slot):
                rwb, xm, tmpA = RWT['rwb'], RWT['xm'], RWT['tmpA']
                b = proj_block(T, win_d[l, 8 + q], KC, hr, hk)
                rb = rwb[slot]
                ACT(rb.ap[:, 1:T + 1], bank_ap(b)[:, 0:T], AF.Copy, r=[('ps', b)], w=[rb])
                CP(rb.ap[:, 0:1], SHIFT[:, l, q:q + 1], r=['shift'], w=[rb])
                TT(tmpA.ap, rb.ap[:, 0:T], rb.ap[:, 1:T + 1], ALU.subtract, r=[rb], w=[tmpA])
                STT(xm[slot].ap, tmpA.ap, pcol(C_MU + q), rb.ap[:, 1:T + 1], ALU.mult, ALU.add, r=[tmpA, rb, 'prm'], w=[xm[slot]])
                CP(SHIFT[:, l, q:q + 1], rb.ap[:, T:T + 1], r=[rb], w=['shift'])

            mL = smark()
            rw_alloc()
            xm = RWT['xm']
            rw_chunk(0, 0)
            ACT(tw.ap[0:64, :], xm[0].ap[0:64, :], AF.Tanh, r=[xm[0]], w=[tw])
            rw_chunk(1, 1)
            CP(ad.ap[0:64, :], xm[1].ap[0:64, :], r=[xm[1]], w=[ad], q='act')
            rw_chunk(2, 2)
            ACT(sg1.ap, xm[2].ap, AF.Sigmoid, r=[xm[2]], w=[sg1])
            rw_chunk(3, 0)
            ACT(sg2.ap, xm[0].ap, AF.Sigmoid, r=[xm[0]], w=[sg2])
            srelease(mL)

            for qd in range(4):
                mQ = smark()
                spdma(gnt.ap[:, 0, :], gn_d[l:l + 1, qd * 256:(qd + 1) * 256].partition_broadcast(64), w=[gnt])
                spdma(gnt.ap[:, 1, :], gn_d[l:l + 1, 1024 + qd * 256:1024 + (qd + 1) * 256].partition_broadcast(64), w=[gnt])
                ARs = salloc([2, nch, 128], BF16); ARo = salloc([2, nch, 128], BF16)
                BKs = salloc([2, nch, 128], BF16); BKo = salloc([2, nch, 128], BF16)
                BKc = salloc([2, nch, 128], BF16)
                VB = salloc([2, T], BF16); RK = salloc([2, T], BF16)
                PCs = salloc([2, 8], F32); PCH = salloc([nch, 4], F32, parts=64)
                MSET(PCs.ap, 0.0, w=[PCs])
                mP = smark()
                rw_alloc()
                xm = RWT['xm']
                t = [salloc([T], F32) for _ in range(11)]
                sqk = salloc([T], BF16)
                for jj in range(2):
                    j = qd * 2 + jj
                    for i3 in range(3):
                        rw_chunk(4 + 3 * j + i3, i3)
                    XR, XK, XV = xm[0], xm[1], xm[2]
                    sigz, cum, exc, pinc, pinv, pexc, av, kkr, kk, kh, tb = t
                    b = bank()
                    MM(bank_ap(b)[:, 0:T], LA[0:64, j * 128:(j + 1) * 128], tw.ap[0:64, :], True, True, r=['la', tw], w=[('ps', b)])
                    ACT(sigz.ap, bank_ap(b)[:, 0:T], AF.Sigmoid, r=[('ps', b), 'prm'], w=[sigz], bias=pcol(C_W0 + j))
                    OP('dve', lambda e, cum=cum, sigz=sigz: e.tensor_tensor_scan(out=cum.ap, data0=CST[:, K_RST:K_RST + T], data1=sigz.ap,
                                                                                 initial=0.0, op0=ALU.mult, op1=ALU.add),
                       r=_keys([sigz, 'cst']), w=_keys([cum]))
                    TT(exc.ap, cum.ap, sigz.ap, ALU.subtract, r=[cum, sigz], w=[exc])
                    ACT(pinc.ap, cum.ap, AF.Exp, r=[cum], w=[pinc], scale=-C0)
                    ACT(pinv.ap, cum.ap, AF.Exp, r=[cum], w=[pinv], scale=C0)
                    ACT(pexc.ap, exc.ap, AF.Exp, r=[exc], w=[pexc], scale=-C0)
                    CP(PCs.ap[:, jj, 0:nch], pinc.ap.rearrange("p (c t) -> p c t", t=64)[:, :, 63], r=[pinc], w=[PCs])
                    b = bank()
                    MM(bank_ap(b)[:, 0:T], LA[0:64, 1024 + j * 128:1024 + (j + 1) * 128], ad.ap[0:64, :], True, True, r=['la', ad], w=[('ps', b)])
                    ACT(av.ap, bank_ap(b)[:, 0:T], AF.Sigmoid, r=[('ps', b), 'prm'], w=[av], bias=pcol(C_A0 + j))
                    ACT(sqk.ap, XK.ap, AF.Square, r=[XK, 'prm'], w=[sqk], scale=pcol(C_KK + j))
                    b = bank()
                    MM(bank_ap(b)[:, 0:T], BLK2b, sqk.ap, True, True, r=['cb', sqk], w=[('ps', b)])
                    ACT(exc.ap, bank_ap(b)[:, 0:T], AF.Sqrt, r=[('ps', b)], w=[exc])
                    TS_(exc.ap, exc.ap, 1e-12, None, ALU.max, ALU.bypass, r=[exc], w=[exc])
                    OP('dve', lambda e, exc=exc: e.reciprocal(out=exc.ap, in_=exc.ap), r=_keys([exc]), w=_keys([exc]))
                    STT(kk.ap, XK.ap, pcol(C_KK + j), exc.ap, ALU.mult, ALU.mult, r=[XK, exc, 'prm'], w=[kk])
                    TS_(tb.ap, av.ap, 1.0, pcol(C_KA + j), ALU.subtract, ALU.mult, r=[av, 'prm'], w=[tb])
                    STT(kh.ap, tb.ap, 1.0, XK.ap, ALU.add, ALU.mult, r=[tb, XK], w=[kh])
                    c3 = lambda ap: ap.rearrange("p (c t) -> p c t", t=64)
                    STT(ARs.ap[:, jj, :, 0:64], c3(kk.ap), -1.0, c3(pexc.ap), ALU.mult, ALU.mult, r=[kk, pexc], w=[ARs])
                    TT(ARs.ap[:, jj, :, 64:128], c3(XR.ap), c3(pinc.ap), ALU.mult, r=[XR, pinc], w=[ARs])
                    STT(RK.ap[:, jj, :], XR.ap, pcol(C_RK + j), kh.ap, ALU.mult, ALU.mult, r=[XR, kh, 'prm'], w=[RK])
                    CP(VB.ap[:, jj, :], XV.ap, r=[XV], w=[VB], q='act')
                    TT(tb.ap, kk.ap, av.ap, ALU.mult, r=[kk, av], w=[tb])
                    TT(tb.ap, tb.ap, pinv.ap, ALU.mult, r=[tb, pinv], w=[tb])
                    CP(BKs.ap[:, jj, :, 0:64], c3(tb.ap), r=[tb], w=[BKs], q='act')
                    pcb = PCs.ap[:, jj, 0:nch].unsqueeze(2).to_broadcast([128, nch, 64])
                    TT(BKc.ap[:, jj, :, 0:64], c3(tb.ap), pcb, ALU.mult, r=[tb, PCs], w=[BKc])
                    TT(kh.ap, kh.ap, pinv.ap, ALU.mult, r=[kh, pinv], w=[kh])
                    CP(BKs.ap[:, jj, :, 64:128], c3(kh.ap), r=[kh], w=[BKs], q='act')
                    TT(BKc.ap[:, jj, :, 64:128], c3(kh.ap), pcb, ALU.mult, r=[kh, PCs], w=[BKc])
                    for (src, dst) in ((ARs, ARo), (BKs, BKo)):
                        p2 = bpair()
                        n = nch * 128
                        for h0 in range(0, n, 512):
                            hn = min(512, n - h0)
                            MM(PQ[p2][:, h0:h0 + hn], PERMb, src.ap[:, jj, :, :].rearrange("p c t -> p (c t)")[:, h0:h0 + hn], True, True,
                               r=['cb', src], w=[('ps', 2 * p2), ('ps', 2 * p2 + 1)])
                        CP(dst.ap[0:64, jj, :, :].rearrange("p c t -> p (c t)"), PQ[p2][0:64, 0:n], r=[('ps', 2 * p2), ('ps', 2 * p2 + 1)], w=[dst], q='act')
                    b = bank()
                    MM(bank_ap(b)[:, 0:8], PERMf, PCs.ap[:, jj, :], True, True, r=['cst', PCs], w=[('ps', b)])
                    CP(PCH.ap[:, :, 2 * jj], PCs.ap[0:64, jj, 0:nch], r=[PCs], w=[PCH])
                    CP(PCH.ap[:, :, 2 * jj + 1], bank_ap(b)[0:64, 0:nch], r=[('ps', b)], w=[PCH])
                srelease(mP)
                NSL = 3
                TOKs = [salloc([2, 3, 128], BF16, parts=64) for _ in range(NSL)]
                GTs = [salloc([4, 4, 64], BF16, parts=64) for _ in range(NSL)]
                XMT = [[salloc([4, 3, 64], BF16, parts=64) for _ in range(2)] for _ in range(NSL)]
                Wb = salloc([4, 64], BF16, parts=64); Ub = salloc([4, 64], BF16, parts=64); STb = salloc([4, 64], BF16, parts=64)
                ysq = salloc([4, 64], F32, parts=64); yn = salloc([4, 64], F32, parts=64); bon = salloc([4, 64], F32, parts=64)
                stm = salloc([4, 64], F32, parts=64)
                yout = salloc([256], BF16)
                MSET(yout.ap[64:128, :], 0.0, w=[yout])
                ysb = salloc([4, 64], F32, parts=64)
                sm = salloc([8, 4], F32, parts=64)
                STq = ST[:, qd * 4:(qd + 1) * 4, :]
                CP(STb.ap[0:64], STq, r=['st'], w=[STb])
                A_O1 = [0, 2]
                A_O2 = [(1, 0), (3, 0)]

                def ARh(hq, c):
                    jj, hp = hq // 2, hq % 2
                    return (ARs if hp == 0 else ARo).ap[0:64, jj, c, :], (ARs if hp == 0 else ARo)

                def BKh(hq, c):
                    jj, hp = hq // 2, hq % 2
                    return (BKs if hp == 0 else BKo).ap[0:64, jj, c, :], (BKs if hp == 0 else BKo)

                a_done = [False] * nch; b1_done = [False] * nch; b2_done = [False] * nch; ycp = [False] * nch

                def partA(c, pset):
                    if F_NOA:
                        a_done[c] = True
                        return
                    sl = c % NSL
                    TOKc, GT, XM = TOKs[sl], GTs[sl], XMT[sl]
                    cs = slice(c * 64, (c + 1) * 64)
                    b1 = A_O1[pset]; k1 = ('ps', b1)
                    b2, c2 = A_O2[pset]; k2 = ('ps', b2)
                    btv = 4
                    ktv = ('ps', btv)
                    tv = bank_apb(btv)[0:64, 0:768].rearrange("p (j k c) -> p j k c", k=3, c=128)
                    for jj in range(2):
                        TR(tv[:, jj, 0, :], VB.ap[:, jj, cs], IDb, r=[VB, 'cb'], w=[ktv])
                        TR(tv[:, jj, 1, :], BKc.ap[:, jj, c, 0:64], IDb, r=[BKc, 'cb'], w=[ktv])
                        TR(tv[:, jj, 2, :], BKc.ap[:, jj, c, 64:128], IDb, r=[BKc, 'cb'], w=[ktv])
                    xv = bank_ap(b2)[0:64, c2:c2 + 256].rearrange("p (h s) -> p h s", s=64)
                    for hq in range(4):
                        ar, arb = ARh(hq, c)
                        bk, bkb = BKh(hq, c)
                        MM(xv[:, hq, :], ar[:, 0:64], bk[:, 0:64], True, True, r=[arb, bkb], w=[k2])
                    CP(TOKc.ap[0:64], tv, r=[ktv], w=[TOKc], q='act')
                    yield
                    Cc = XM[0]
                    TT(Cc.ap[0:64, :, 0, :], xv, MASKX.unsqueeze(1).to_broadcast([64, 4, 64]), ALU.mult, r=[k2, 'cst'], w=[Cc])
                    gv = bank_ap(b1)[0:64, 0:512].rearrange("p (h x) -> p h x", x=256)
                    for half in range(2):
                        for h2 in range(2):
                            hq = half * 2 + h2
                            ar, arb = ARh(hq, c)
                            bk, bkb = BKh(hq, c)
                            MM(gv[:, h2, 0:128], bk[:, 0:64], ar, True, True, r=[arb, bkb], w=[k1])
                            MM(gv[:, h2, 128:256], bk[:, 64:128], ar, True, True, r=[arb, bkb], w=[k1])
                        yield
                        TT(GT.ap[0:64, half * 2:half * 2 + 2].rearrange("p h b t -> p h (b t)"), gv, MASKG.unsqueeze(1).to_broadcast([64, 2, 256]), ALU.mult,
                           r=[k1, 'cst'], w=[GT])
                    CP(Cc.ap[0:64, :, 1, :], GT.ap[0:64, :, 0, :], r=[GT], w=[Cc], q='act')
                    CP(Cc.ap[0:64, :, 2, :], I64f.unsqueeze(1).to_broadcast([64, 4, 64]), r=['cst'], w=[Cc])
                    yield
                    pv = PQ[b1 // 2][0:64, :].rearrange("p (h x) -> p h x", x=256)
                    kk2 = [k1, k2]
                    for lev in range(6):
                        Cn = XM[(lev + 1) % 2]
                        last = lev == 5
                        for hq in range(4):
                            if last:
                                MM(pv[:, hq, 128:192], Cc.ap[0:64, hq, 0, :], Cc.ap[0:64, hq, 2, :], True, True, r=[Cc], w=kk2)
                            else:
                                MM(pv[:, hq, 64:192], Cc.ap[0:64, hq, 0, :], Cc.ap[0:64, hq, 1:3, :].rearrange("p a t -> p (a t)"), True, True, r=[Cc], w=kk2)
                                MM(pv[:, hq, 0:64], Cc.ap[0:64, hq, 1, :], Cc.ap[0:64, hq, 0, :], True, True, r=[Cc], w=kk2)
                        yield
                        TT(Cn.ap[0:64, :, 2, :], pv[:, :, 128:192], Cc.ap[0:64, :, 2, :], ALU.add, r=kk2 + [Cc], w=[Cn])
                        if not last:
                            CP(Cn.ap[0:64, :, 0:2, :].rearrange("p h a t -> p h (a t)"), pv[:, :, 0:128], r=kk2, w=[Cn], q='act')
                        Cc = Cn
                        yield
                    assert Cc is XM[0]
                    a_done[c] = True

                def chainB1():
                    for c in range(nch):
                        while not (a_done[c] and (c < 1 or ycp[c - 1])):
                            yield
                        if F_NOB1:
                            b1_done[c] = True
                            yield
                            continue
                        sl = c % NSL
                        TOKc, GT, TTf = TOKs[sl], GTs[sl], XMT[sl][0]
                        vtok = lambda hq: TOKc.ap[0:64, hq // 2, 0, (hq % 2) * 64:(hq % 2) * 64 + 64]
                        bctok = lambda hq: TOKc.ap[0:64, hq // 2, 1, (hq % 2) * 64:(hq % 2) * 64 + 64]
                        kctok = lambda hq: TOKc.ap[0:64, hq // 2, 2, (hq % 2) * 64:(hq % 2) * 64 + 64]
                        wk_ = ('ps', 5)
                        wv = bank_ap(5)[0:64, 0:256].rearrange("p (h v) -> p h v", v=64)
                        for hq in range(4):
                            ar, arb = ARh(hq, c)
                            MM(wv[:, hq, :], ar[:, 0:64], STb.ap[0:64, hq, :], True, False, r=[arb, STb], w=[wk_])
                            MM(wv[:, hq, :], GT.ap[0:64, hq, 2, :], vtok(hq), False, True, r=[GT, TOKc], w=[wk_])
                        TT(stm.ap[0:64], STq, PCH.ap[0:64, c, :].unsqueeze(2).to_broadcast([64, 4, 64]), ALU.mult, r=['st', PCH], w=[stm])
                        yield
                        CP(Wb.ap[0:64], wv, r=[wk_], w=[Wb], q='act')
                        yield
                        uk = ('ps', 5)
                        uv = bank_ap(5)[0:64, 0:256].rearrange("p (h v) -> p h v", v=64)
                        for hq in range(4):
                            MM(uv[:, hq, :], TTf.ap[0:64, hq, 2, :], Wb.ap[0:64, hq, :], True, True, r=[TTf, Wb], w=[uk])
                        yield
                        CP(Ub.ap[0:64], uv, r=[uk], w=[Ub])
                        yield
                        sk = ('ps', 5)
                        sv = bank_ap(5)[0:64, 0:256].rearrange("p (h v) -> p h v", v=64)
                        for hq in range(4):
                            MM(sv[:, hq, :], bctok(hq), Ub.ap[0:64, hq, :], True, False, r=[TOKc, Ub], w=[sk])
                            MM(sv[:, hq, :], kctok(hq), vtok(hq), False, True, r=[TOKc], w=[sk])
                        yk = ('ps', 6)
                        yv = bank_ap(6)[0:64, 0:256].rearrange("p (h v) -> p h v", v=64)
                        for hq in range(4):
                            ar, arb = ARh(hq, c)
                            MM(yv[:, hq, :], ar[:, 64:128], STb.ap[0:64, hq, :], True, False, r=[arb, STb], w=[yk])
                            MM(yv[:, hq, :], GT.ap[0:64, hq, 1, :], Ub.ap[0:64, hq, :], False, False, r=[GT, Ub], w=[yk])
                            MM(yv[:, hq, :], GT.ap[0:64, hq, 3, :], vtok(hq), False, True, r=[GT, TOKc], w=[yk])
                        yield
                        TT(STq, stm.ap[0:64], sv, ALU.add, r=[stm, sk], w=['st'])
                        yield
                        CP(STb.ap[0:64], STq, r=['st'], w=[STb], q='act')
                        b1_done[c] = True
                        yield

                def chainB2():
                    for c in range(nch):
                        while not b1_done[c]:
                            yield
                        if F_NOB2:
                            b2_done[c] = True
                            yield
                            continue
                        sl = c % NSL
                        TOKc = TOKs[sl]
                        cs = slice(c * 64, (c + 1) * 64)
                        yk = ('ps', 6)
                        yv = bank_ap(6)[0:64, 0:256].rearrange("p (h v) -> p h v", v=64)
                        gk_ = ('ps', 7)
                        CP(ysb.ap[0:64], yv, r=[yk], w=[ysb], q='act')
                        ycp[c] = True
                        yv = ysb.ap[0:64]
                        yk = ysb
                        for jj in range(2):
                            MM(bank_ap(7)[0:64, 256 + 2 * jj:256 + 2 * jj + 2], RK.ap[:, jj, cs], BO2b, True, True, r=[RK, 'cb'], w=[gk_])
                        MM(bank_ap(7)[0:64, 0:256], sg1.ap[:, cs], LB[:, qd * 256:(qd + 1) * 256], True, False, r=[sg1, 'lb'], w=[gk_])
                        MM(bank_ap(7)[0:64, 0:256], sg2.ap[:, cs], LB[:, 1024 + qd * 256:1024 + (qd + 1) * 256], False, True, r=[sg2, 'lb'], w=[gk_])
                        s1 = sm.ap[0:64, 0, :]; s2 = sm.ap[0:64, 1, :]; mean = sm.ap[0:64, 2, :]; msq = sm.ap[0:64, 3, :]
                        var = sm.ap[0:64, 4, :]; rs = sm.ap[0:64, 5, :]; bsc = sm.ap[0:64, 6, :]
                        OP('dve', lambda e, s1=s1, yv=yv: e.tensor_reduce(out=s1, in_=yv, axis=AX.X, op=ALU.add), r=_keys([yk]), w=_keys([sm]))
                        ACT(ysq.ap[0:64], yv, AF.Square, r=[yk], w=[ysq])
                        yield
                        OP('dve', lambda e, s2=s2, ysq=ysq: e.tensor_reduce(out=s2, in_=ysq.ap[0:64], axis=AX.X, op=ALU.add), r=_keys([ysq]), w=_keys([sm]))
                        TS_(mean, s1, 1.0 / 64, None, ALU.mult, ALU.bypass, r=[sm], w=[sm])
                        yield
                        TT(msq, mean, mean, ALU.mult, r=[sm], w=[sm])
                        yield
                        STT(var, s2, 1.0 / 64, msq, ALU.mult, ALU.subtract, r=[sm], w=[sm])
                        yield
                        ACT(var, var, AF.Sqrt, r=[sm], w=[sm], bias=GN_EPS)
                        CP(bsc, bank_ap(7)[0:64, 256:260], r=[gk_], w=[sm], q='act')
                        b3 = lambda ap: ap.unsqueeze(2).to_broadcast([64, 4, 64])
                        TT(yn.ap[0:64], yv, b3(mean), ALU.subtract, r=[yk, sm], w=[yn])
                        yield
                        OP('dve', lambda e, rs=rs, var=var: e.reciprocal(out=rs, in_=var), r=_keys([sm]), w=_keys([sm]))
                        yield
                        TT(yn.ap[0:64], yn.ap[0:64], b3(rs), ALU.mult, r=[yn, sm], w=[yn])
                        gq = gnt.ap[0:64, 0, :].rearrange("p (h v) -> p h v", v=64)
                        bq = gnt.ap[0:64, 1, :].rearrange("p (h v) -> p h v", v=64)
                        vt4 = TOKc.ap[0:64, :, 0, :].rearrange("p j (h v) -> p j h v", v=64)
                        bsc4 = bsc.rearrange("p (j h) -> p j h", h=2).unsqueeze(3).to_broadcast([64, 2, 2, 64])
                        TT(bon.ap[0:64].rearrange("p (j h) v -> p j h v", h=2), vt4, bsc4, ALU.mult, r=[TOKc, sm], w=[bon])
                        yield
                        TT(yn.ap[0:64], yn.ap[0:64], gq, ALU.mult, r=[yn, gnt], w=[yn])
                        yield
                        TT(yn.ap[0:64], yn.ap[0:64], bq, ALU.add, r=[yn, gnt], w=[yn])
                        yield
                        TT(yn.ap[0:64], yn.ap[0:64], bon.ap[0:64], ALU.add, r=[yn, bon], w=[yn])
                        yield
                        TT(yout.ap[0:64], yn.ap[0:64].rearrange("p h v -> p (h v)"), bank_ap(7)[0:64, 0:256], ALU.mult, r=[yn, gk_], w=[yout])
                        yield
                        tk = gk_
                        for jj in range(2):
                            TR(bank_apb(7)[:, 528 + jj * 128:528 + (jj + 1) * 128], yout.ap[:, jj * 128:(jj + 1) * 128], IDb, r=[yout, 'cb'], w=[tk])
                        yield
                        CP(YB[:, qd * 2:qd * 2 + 2, cs], bank_apb(7)[:, 528:784].rearrange("p (j t) -> p j t", t=128)[:, :, 0:64], r=[tk], w=[('yb', qd)], q='act')
                        b2_done[c] = True
                        yield

                S.begin_capture()
                gB1 = chainB1(); gB2 = chainB2()
                for c in range(nch):
                    for _ in partA(c, c % 2):
                        pass
                    while not b1_done[c]:
                        next(gB1)
                    while not b2_done[c]:
                        next(gB2)
                S.end_capture()
                srelease(mQ)
            srelease(mB)
            spdma(wk_dst[l], ST[:].rearrange("p h v -> p (h v)"), r=['st'], w=[('wk', l)])

            mA = smark()
            Z = salloc([8, W], F32); DP = salloc([8, T], BF16)
            SA = salloc([W], F32); SB_ = salloc([W], F32); PW = salloc([2048], BF16); tfix = salloc([16], F32)
            pooldma(PW.ap, pw_d[l], w=[PW])
            for ch in range(8):
                b = proj_block(T, win_d[l, ch], KC, hr, hk)
                ACT(Z.ap[:, ch, 15:W], bank_ap(b)[:, 0:T], AF.Copy, r=[('ps', b)], w=[Z])
                CP(Z.ap[:, ch, 0:15], POOLST[:, l, ch, :], r=['poolst'], w=[Z])
                gi = ch // 2
                win = 2 << gi
                zc = Z.ap[:, ch, :]
                TT(SA.ap[:, 1:W], zc[:, 1:W], zc[:, 0:W - 1], ALU.add, r=[Z], w=[SA])
                fin = SA
                if win >= 4:
                    TT(SB_.ap[:, 3:W], SA.ap[:, 3:W], SA.ap[:, 1:W - 2], ALU.add, r=[SA], w=[SB_]); fin = SB_
                if win >= 8:
                    TT(SA.ap[:, 7:W], SB_.ap[:, 7:W], SB_.ap[:, 3:W - 4], ALU.add, r=[SB_], w=[SA]); fin = SA
                if win >= 16:
                    TT(SB_.ap[:, 15:W], SA.ap[:, 15:W], SA.ap[:, 7:W - 8], ALU.add, r=[SA], w=[SB_]); fin = SB_
                STT(DP.ap[:, ch, :], fin.ap[:, 15:W], 1.0 / win, zc[:, 15:W], ALU.mult, ALU.subtract, r=[fin, Z], w=[DP])
                if is_prompt_first:
                    TT(tfix.ap, fin.ap[:, 15:31], RC[:, gi, :], ALU.mult, r=[fin, 'cst'], w=[tfix])
                    TT(DP.ap[:, ch, 0:16], tfix.ap, zc[:, 15:31], ALU.subtract, r=[tfix, Z], w=[DP])
                CP(POOLST[:, l, ch, :], zc[:, T:T + 15], r=[Z], w=['poolst'])
            pwv = PW.ap.rearrange("p (g k o) -> p g k o", k=2, o=256)
            for g in range(4):
                for o2 in range(2):
                    b = bank()
                    for k2 in range(2):
                        MM(bank_ap(b)[:, 0:T], pwv[:, g, k2, o2 * 128:(o2 + 1) * 128], DP.ap[:, 2 * g + k2, :], k2 == 0, k2 == 1, r=[PW, DP], w=[('ps', b)])
                    ACT(YA[:, 2 * g + o2, 0:T], bank_ap(b)[:, 0:T], AF.Copy, r=[('ps', b), 'prm'], w=[('ya', 2 * g + o2)], scale=pcol(C_PSC + 2 * g + o2))
            srelease(mA)

            mC = smark()
            MG = salloc([KC, T], BF16)
            s1b = salloc([T], F32); s2b = salloc([T], F32); m1 = salloc([T], F32); m2 = salloc([T], F32)
            for oc in range(KC):
                half, sub = oc // 2, oc % 2
                if sub == 0:
                    spool = wload(ppj_d[l, half])
                    srw = wload(prw_d[l, half])
                bp1 = bank()
                for kc in range(8):
                    MM(bank_ap(bp1)[:, 0:T], WR[:, spool, sub * 1024 + kc * 128: sub * 1024 + (kc + 1) * 128], YA[:, kc, 0:T], kc == 0, kc == 7,
                       r=[('w', spool), ('ya', kc)], w=[('ps', bp1)])
                bp2 = bank()
                for kc in range(8):
                    MM(bank_ap(bp2)[:, 0:T], WR[:, srw, sub * 1024 + kc * 128: sub * 1024 + (kc + 1) * 128], YB[:, kc, 0:T], kc == 0, kc == 7,
                       r=[('w', srw), ('yb', kc // 2)], w=[('ps', bp2)])
                bg1 = proj_block(T, win_d[l, 36 + oc], KC, hr, hk)
                bg2 = proj_block(T, win_d[l, 52 + oc], KC, hr, hk)
                if sub == 1:
                    wfree(spool); wfree(srw)
                ACT(s1b.ap, bank_ap(bg1)[:, 0:T], AF.Sigmoid, r=[('ps', bg1)], w=[s1b])
                ACT(s2b.ap, bank_ap(bg2)[:, 0:T], AF.Sigmoid, r=[('ps', bg2)], w=[s2b])
                TT(m1.ap, s1b.ap, bank_ap(bp1)[:, 0:T], ALU.mult, r=[s1b, ('ps', bp1)], w=[m1])
                TT(m2.ap, s2b.ap, bank_ap(bp2)[:, 0:T], ALU.mult, r=[s2b, ('ps', bp2)], w=[m2])
                TT(MG.ap[:, oc, :], m1.ap, m2.ap, ALU.add, r=[m1, m2], w=[MG])
            for oc in range(KC):
                b = proj_block(T, wo_d[l, oc], KC, lambda kc: MG.ap[:, kc, :], lambda kc: [MG])
                TT(X[:, oc, 0:T], X[:, oc, 0:T], bank_ap(b)[:, 0:T], ALU.add, r=[('x', oc), ('ps', b)], w=[('x', oc)])
            srelease(mC)

            rmsnorm(T, PRM[:, C_NF:C_NF + KC], 'prm', lambda kc: H[:, kc, 0:T], lambda kc: [('h', kc)])
            mD = smark()
            ACTB = [salloc([FGC, T], BF16) for _ in range(2)]
            sgb = [salloc([T], F32) for _ in range(2)]
            for fg in range(FG):
                ab = ACTB[fg % 2]
                for fi in range(FGC):
                    fc = fg * FGC + fi
                    bg_ = proj_block(T, wg_d[l, fc], KC, hr, hk)
                    bu_ = proj_block(T, wu_d[l, fc], KC, hr, hk)
                    sg = sgb[fi % 2]
                    ACT(sg.ap, bank_ap(bg_)[:, 0:T], AF.Silu, r=[('ps', bg_)], w=[sg])
                    TT(ab.ap[:, fi, :], sg.ap, bank_ap(bu_)[:, 0:T], ALU.mult, r=[sg, ('ps', bu_)], w=[ab])
                for oc in range(KC):
                    b = proj_block(T, wd_d[l, fg, oc], FGC, lambda kc: ab.ap[:, kc, :], lambda kc: [ab])
                    TT(X[:, oc, 0:T], X[:, oc, 0:T], bank_ap(b)[:, 0:T], ALU.add, r=[('x', oc), ('ps', b)], w=[('x', oc)])
            srelease(mD)

            rmsnorm(T, PRM[:, C_NP:C_NP + KC], 'prm', lambda kc: H[:, kc, 0:T], lambda kc: [('h', kc)])
            mE = smark()
            PT = salloc([2, T], BF16); sp_ = salloc([T], F32); tp_ = salloc([T], F32)
            PJ = [salloc([2048], BF16) for _ in range(2)]
            pooldma(PT.ap, p_src(l), w=[PT])
            pooldma(PJ[0].ap, pj_d[l, 0], w=[PJ[0]])
            pooldma(PJ[1].ap, pj_d[l, 1], w=[PJ[1]])
            for oc in range(KC):
                pjb = PJ[oc // 8]
                bgp = proj_block(T, pg_d[l, oc], KC, hr, hk)
                bpp = bank()
                for k2 in range(2):
                    MM(bank_ap(bpp)[:, 0:T], pjb.ap[:, (oc % 8) * 256 + k2 * 128:(oc % 8) * 256 + (k2 + 1) * 128], PT.ap[:, k2, :], k2 == 0, k2 == 1,
                       r=[pjb, PT], w=[('ps', bpp)])
                ACT(sp_.ap, bank_ap(bgp)[:, 0:T], AF.Sigmoid, r=[('ps', bgp)], w=[sp_])
                TT(tp_.ap, sp_.ap, bank_ap(bpp)[:, 0:T], ALU.mult, r=[sp_, ('ps', bpp)], w=[tp_])
                TT(X[:, oc, 0:T], X[:, oc, 0:T], tp_.ap, ALU.add, r=[('x', oc), tp_], w=[('x', oc)])
            srelease(mE)

        def run_segment(kind, t0, T, first, last):
            xsrc = xp_d if kind == 'p' else xs_d
            ydst = yp_d if kind == 'p' else ys_d
            spdma(X[:, :, 0:T], xsrc[:, :, t0:t0 + T].rearrange("c p t -> p c t"), w=[('x', kc) for kc in range(KC)])
            if first:
                if kind == 'p':
                    MSET(POOLST[:], 0.0, w=['poolst'])
                    MSET(SHIFT[:], 0.0, w=['shift'])
                else:
                    spdma(POOLST[:].rearrange("p l c t -> p (l c t)"), stp_d, w=['poolst'])
                    spdma(SHIFT[:].rearrange("p l q -> p (l q)"), sts_d, w=['shift'])
            for l in range(L):
                if kind == 'p':
                    psrc = lambda l, t0=t0, T=T: pp_d[l, :, :, t0:t0 + T].rearrange("k p t -> p k t")
                    layer_tile(l, T, first, first, wkp_d, wkp_d, psrc)
                else:
                    psrc = lambda l: psm_d[l].rearrange("k p t -> p k t")
                    layer_tile(l, T, first, False, stw_d, wks_d, psrc)
            m = smark()
            YO = salloc([KC, T], F32)
            rmsnorm(T, NFIN[:], 'nfin', lambda kc: YO.ap[:, kc, :], lambda kc: list(YO.keys))
            spdma(ydst[:, :, t0:t0 + T].rearrange("c p t -> p c t"), YO.ap, r=[YO])
            srelease(m)
            if last:
                spdma(shp_d if kind == 'p' else shs_d, SHIFT[:].rearrange("p l q -> p (l q)"), r=['shift'])
                spdma(pop_d if kind == 'p' else pos_d, POOLST[:].rearrange("p l c t -> p (l c t)"), r=['poolst'])

        ntile = TP // TILE
        for ti in range(ntile):
            run_segment('p', ti * TILE, TILE, ti == 0, ti == ntile - 1)
        run_segment('s', 0, TS, True, True)

        cnt = S.analyze()
        sems = {k: es.enter_context(nc.semaphore("sem_%s_%s" % k)) for k in cnt}
        block = es.enter_context(nc.Block())
        S.emit(sems, block)
    return nc, len(S.ops)


def rw_cols():
    cols = []
    cols.append(list(range(3072, 3136)) + [-1] * 64)
    cols.append(list(range(3136, 3200)) + [-1] * 64)
    cols.append(list(range(3200, 3328)))
    cols.append(list(range(3328, 3360)) + [-1] * 96)
    for j in range(8):
        for base in (0, 1024, 2048):
            cols.append(list(range(base + 128 * j, base + 128 * (j + 1))))
    return np.array(cols, dtype=np.int64)


def make_consts():
    c = np.zeros((128, NCST), np.float32)
    c[:, K_ID:K_ID + 128] = np.eye(128)
    for m in range(128):
        c[(m + 64) % 128, K_PERM + m] = 1.0
    c[:, K_ONES:K_ONES + 128] = 1.0
    p = np.arange(128)
    c[:, K_BLK2:K_BLK2 + 128] = (p[:, None] // 64 == p[None, :] // 64)
    c[:, K_BO2:K_BO2 + 2] = (p[:, None] // 64 == np.arange(2)[None, :])
    s = np.arange(64)[:, None]; t = np.arange(64)[None, :]
    strict = (t > s).astype(np.float32); incl = (t >= s).astype(np.float32)
    c[0:64, K_MG:K_MG + 256] = np.concatenate([strict, incl, strict, incl], axis=1)
    c[0:64, K_MX:K_MX + 64] = (s > t)
    rst = np.ones(512, np.float32); rst[::64] = 0.0
    c[:, K_RST:K_RST + 512] = rst[None, :]
    for gi, win in enumerate((2, 4, 8, 16)):
        c[:, K_RC + gi * 16:K_RC + (gi + 1) * 16] = (1.0 / np.minimum(np.arange(16) + 1, win))[None, :]
    return c


def kblocks(w, kc_n):
    K, N = w.shape
    return np.ascontiguousarray(w.reshape(kc_n, 128, N // 128, 128).transpose(2, 1, 0, 3).reshape(N // 128, 128, kc_n * 128))


def layout_weights(inp, L):
    f = lambda k: np.asarray(inp[k], dtype=np.float32)
    cols = rw_cols()
    out = {}
    w_in = f('w_in')
    win_b = np.zeros((L, NWIN, 128, 2048), np.float32)
    lora_b = np.zeros((L, 2, 128, 2048), np.float32)
    prm = np.zeros((L, 128, NPRM), np.float32)
    mu = f('mu_shift')
    for l in range(L):
        wl = w_in[l]
        win_b[l, 0:8] = kblocks(wl[:, 0:1024], KC)
        rwm = np.zeros((D, NQ * 128), np.float32)
        cc = cols.reshape(-1)
        ok = cc >= 0
        rwm[:, ok] = wl[:, 1024 + cc[ok]]
        win_b[l, 8:36] = kblocks(rwm, KC)
        win_b[l, 36:52] = kblocks(wl[:, 1024 + 3360:1024 + 3360 + 2048], KC)
        win_b[l, 52:68] = kblocks(wl[:, 1024 + 3360 + 2048:], KC)
        lora_b[l, 0, 0:64, 0:1024] = f('w_decay_up')[l]
        lora_b[l, 0, 0:64, 1024:2048] = f('w_aaa_up')[l]
        lora_b[l, 1, :, 0:1024] = f('w_gate_up')[l][0:128]
        lora_b[l, 1, 0:32, 1024:2048] = f('w_gate_up')[l][128:160]
        pc = lambda v: v.reshape(-1, 128).T
        prm[l, :, C_NM:C_NM + 16] = pc(f('norm_mix')[l]); prm[l, :, C_NF:C_NF + 16] = pc(f('norm_ffn')[l]); prm[l, :, C_NP:C_NP + 16] = pc(f('norm_ple')[l])
        mum = np.zeros(NQ * 128, np.float32); mum[ok] = mu[l][cc[ok]]
        prm[l, :, C_MU:C_MU + NQ] = mum.reshape(NQ, 128).T
        prm[l, :, C_PSC:C_PSC + 8] = pc(f('pool_scale')[l]); prm[l, :, C_W0:C_W0 + 8] = pc(f('w0')[l]); prm[l, :, C_A0:C_A0 + 8] = pc(f('a0')[l])
        prm[l, :, C_KK:C_KK + 8] = pc(f('k_k')[l]); prm[l, :, C_KA:C_KA + 8] = pc(f('k_a')[l]); prm[l, :, C_RK:C_RK + 8] = pc(f('r_k')[l].reshape(-1))
    out['w_in_b'] = win_b; out['lora_b'] = lora_b; out['prm'] = prm
    pw = f('pool_w').reshape(L, 4, 2, 128, 256).transpose(0, 3, 1, 2, 4).reshape(L, 128, 2048)
    out['pw_b'] = np.ascontiguousarray(pw)

    def two_oc(w):
        r = np.stack([kblocks(w[l], 8) for l in range(L)])
        return np.ascontiguousarray(r.reshape(L, 8, 2, 128, 1024).transpose(0, 1, 3, 2, 4).reshape(L, 8, 128, 2048))
    out['ppool_b'] = two_oc(f('proj_pool')); out['prwkv_b'] = two_oc(f('proj_rwkv'))
    out['wo_b'] = np.stack([kblocks(f('w_out')[l], KC) for l in range(L)])
    out['wg_b'] = np.stack([kblocks(f('w_ffn_gate')[l], KC) for l in range(L)])
    out['wu_b'] = np.stack([kblocks(f('w_ffn_up')[l], KC) for l in range(L)])
    wd = f('w_ffn_down')
    out['wd_b'] = np.stack([np.stack([kblocks(wd[l][fg * FGC * 128:(fg + 1) * FGC * 128], FGC) for fg in range(FG)]) for l in range(L)])
    out['pg_b'] = np.stack([kblocks(f('w_ple_gate')[l], KC) for l in range(L)])
    pj = np.stack([kblocks(f('w_ple_proj')[l], 2) for l in range(L)])
    out['pj_b'] = np.ascontiguousarray(pj.reshape(L, 2, 8, 128, 256).transpose(0, 1, 3, 2, 4).reshape(L, 2, 128, 2048))
    out['nfin'] = np.ascontiguousarray(f('norm_final').reshape(16, 128).T)
    out['gn'] = np.ascontiguousarray(np.concatenate([f('gn_gain'), f('gn_bias')], axis=1))
    out['cst'] = make_consts()
    return out, cols


_CACHE = {}


def run(inputs, L, TP, TS, TILE, ncores):
    key = (L, TP, TS, TILE)
    if key not in _CACHE:
        _CACHE[key] = build_program(L, TP, TS, TILE)[0]
    nc = _CACHE[key]
    f = lambda k: np.asarray(inputs[k], dtype=np.float32)
    shared, cols = layout_weights(inputs, L)
    cc = cols.reshape(-1); ok = cc >= 0
    in_maps = []
    for b in range(ncores):
        m = dict(shared)
        m['xp'] = np.ascontiguousarray(f('x_prompt')[b].T.reshape(KC, 128, TP))
        m['xs'] = np.ascontiguousarray(f('x_sample')[b].T.reshape(KC, 128, TS))
        m['pp'] = np.ascontiguousarray(f('p_prompt')[:, b].transpose(0, 2, 1).reshape(L, 2, 128, TP))
        m['psm'] = np.ascontiguousarray(f('p_sample')[:, b].transpose(0, 2, 1).reshape(L, 2, 128, TS))
        ss = np.zeros((L, NQ * 128), np.float32); ss[:, ok] = f('state_shift')[:, b][:, cc[ok]]
        m['st_shift'] = np.ascontiguousarray(ss.reshape(L, NQ, 128).transpose(2, 0, 1).reshape(128, L * NQ))
        sp = f('state_pool')[:, b]
        m['st_pool'] = np.ascontiguousarray(sp.reshape(L, 15, 8, 128).transpose(3, 0, 2, 1).reshape(128, L * 120))
        sw = f('state_wkv')[:, b]
        m['st_wkv'] = np.ascontiguousarray(sw.transpose(0, 3, 1, 2).reshape(L, 64, 1024))
        in_maps.append(m)
    res = run_bass_kernel_spmd(nc, in_maps, core_ids=list(range(ncores)))
    R = res.results
    inv = np.full(3360, -1, np.int64); inv[cc[ok]] = np.nonzero(ok)[0]

    def unshift(a):
        v = a.reshape(128, L, NQ).transpose(1, 2, 0).reshape(L, NQ * 128)
        return v[:, inv]
    y_p = np.stack([R[b]['yp'].reshape(D, TP).T for b in range(ncores)])
    y_s = np.stack([R[b]['ys'].reshape(D, TS).T for b in range(ncores)])
    outs = [y_p, y_s]
    for sfx in ('p', 's'):
        sh = np.stack([unshift(R[b]['sh_' + sfx]) for b in range(ncores)], axis=1)
        po = np.stack([R[b]['po_' + sfx].reshape(128, L, 8, 15).transpose(1, 3, 2, 0).reshape(L, 15, 1024) for b in range(ncores)], axis=1)
        wk = np.stack([R[b]['wk_' + sfx].reshape(L, 64, 16, 64).transpose(0, 2, 3, 1) for b in range(ncores)], axis=1)
        outs += [sh, po, wk]
    return tuple(np.ascontiguousarray(o.astype(np.float32)) for o in outs)


def kernel(**inputs):
    return run(inputs, 4, 2048, 64, 512, 8)
```

```python
import contextlib
import os
F_THREADS = int(os.environ.get('F_THREADS', '2'))
F_YV1 = int(os.environ.get('F_YV1', '0'))
F_NOB2 = int(os.environ.get('F_NOB2', '0'))
F_NOB1 = int(os.environ.get('F_NOB1', '0'))
F_NOA = int(os.environ.get('F_NOA', '0'))
F_TVB = int(os.environ.get('F_TVB', '0'))
import numpy as np
import concourse.bass as bass
import concourse.mybir as mybir
from concourse.bass_utils import run_bass_kernel_spmd

F32 = mybir.dt.float32
BF16 = mybir.dt.bfloat16
AF = mybir.ActivationFunctionType
ALU = mybir.AluOpType
AX = mybir.AxisListType

D = 2048
KC = 16
POOLW = 1024
RWW = 1024
NHEAD = 16
DFF = 5632
FC = 44
FG = 4
FGC = 11
PLE = 256
NQ = 28
NWIN = 68
NPRM = 124
C_NM, C_NF, C_NP, C_MU, C_PSC, C_W0, C_A0, C_KK, C_KA, C_RK = 0, 16, 32, 48, 76, 84, 92, 100, 108, 116
K_ID, K_PERM, K_ONES, K_BLK2, K_BO2, K_MG, K_MX, K_RST, K_RC = 0, 128, 256, 384, 512, 514, 770, 834, 1346
NCST = 1410
NORM_EPS = 1e-6
GN_EPS = 64e-5
C0 = float(np.exp(-0.5))
NSLOT = 6
SCR_KB = 72


class Sched:
    QUEUES = ['pe', 'act', 'dve', 'pool', 'sp']

    def __init__(self):
        self.ops = []
        self.cap = None

    def op(self, q, fn, reads=(), writes=(), stream=None, cost=300):
        o = dict(q=q, fn=fn, reads=reads, writes=writes, stream=stream, cost=cost)
        if self.cap is not None:
            self.cap.append(o)
        else:
            self.ops.append(o)

    def begin_capture(self):
        self.cap = []

    def end_capture(self, lat=180):
        ops = self.cap
        self.cap = None
        n = len(ops)
        last_w = {}
        readers = {}
        preds = [set() for _ in range(n)]
        for i, o in enumerate(ops):
            for r in o['reads']:
                j = last_w.get(r)
                if j is not None:
                    preds[i].add(j)
            for w in o['writes']:
                j = last_w.get(w)
                if j is not None:
                    preds[i].add(j)
                for j in readers.get(w, ()):
                    preds[i].add(j)
            preds[i].discard(i)
            for r in o['reads']:
                readers.setdefault(r, []).append(i)
            for w in o['writes']:
                last_w[w] = i
                readers[w] = []
        succs = [[] for _ in range(n)]
        indeg = [0] * n
        for i in range(n):
            indeg[i] = len(preds[i])
            for j in preds[i]:
                succs[j].append(i)
        finish = [0.0] * n
        free = {}
        ready = [i for i in range(n) if indeg[i] == 0]
        ready_t = {i: 0.0 for i in ready}
        order = []
        while ready:
            best = None
            for i in ready:
                st = max(free.get(ops[i]['q'], 0.0), ready_t[i])
                key = (st, i)
                if best is None or key < best[0]:
                    best = (key, i)
            (st, _), i = best
            ready.remove(i)
            q = ops[i]['q']
            if ops[i]['stream'] is not None:
                fin = st + 2600.0
                free[q] = st + 60.0
            else:
                fin = st + ops[i]['cost']
                free[q] = fin
            finish[i] = fin
            order.append(i)
            for k in succs[i]:
                indeg[k] -= 1
                extra = 0.0 if (q == 'pe' and ops[k]['q'] == 'pe') else lat
                ready_t[k] = max(ready_t.get(k, 0.0), fin + extra)
                if indeg[k] == 0:
                    ready.append(k)
        assert len(order) == n
        for i in order:
            self.ops.append(ops[i])
        return max(finish) if n else 0.0

    def analyze(self):
        ops = self.ops
        last_w = {}
        readers = {}
        last_on_stream = {}
        for i, o in enumerate(ops):
            deps = set()
            for r in o['reads']:
                j = last_w.get(r)
                if j is not None:
                    deps.add(j)
            for w in o['writes']:
                j = last_w.get(w)
                if j is not None:
                    deps.add(j)
                rd = readers.get(w)
                if rd:
                    deps.update(rd.values())
            if o['stream'] is not None:
                j = last_on_stream.get(o['stream'])
                if j is not None:
                    deps.add(j)
                last_on_stream[o['stream']] = i
            deps.discard(i)
            ck = ('s', o['stream']) if o['stream'] is not None else ('q', o['q'])
            o['ck'] = ck
            for r in o['reads']:
                readers.setdefault(r, {})[ck] = i
            for w in o['writes']:
                last_w[w] = i
                readers[w] = {}
            o['deps'] = deps
            o['sig'] = o['stream'] is not None
        for o in ops:
            nd = set()
            for d in o['deps']:
                p = ops[d]
                if p['ck'] == ('q', 'pe') and o['ck'] == ('q', 'pe'):
                    continue
                nd.add(d)
                p['sig'] = True
            o['deps'] = nd
        cnt = {}
        for o in ops:
            if o['sig']:
                k = o['ck']
                cnt[k] = cnt.get(k, 0) + (16 if o['stream'] is not None else 1)
                o['semval'] = cnt[k]
        self.final_counts = cnt
        known = {q: {} for q in self.QUEUES}
        for o in ops:
            need = {}
            for d in o['deps']:
                p = ops[d]
                k = p['ck']
                v = p['semval']
                if v > need.get(k, 0):
                    need[k] = v
            waits = []
            kq = known[o['q']]
            for k, v in need.items():
                if kq.get(k, 0) >= v:
                    continue
                kq[k] = v
                waits.append((k, v))
            o['waits'] = waits
        return cnt

    def emit(self, sems, block):
        ops = self.ops
        byq = {q: [o for o in ops if o['q'] == q] for q in self.QUEUES}
        dec = {'pe': block.tensor, 'act': block.scalar, 'dve': block.vector, 'pool': block.gpsimd, 'sp': block.sync}
        final = list(self.final_counts.items())
        for q in self.QUEUES:
            lst = byq[q]

            def body(eng, lst=lst, q=q):
                for o in lst:
                    for (k, v) in o['waits']:
                        eng.wait_ge(sems[k], v)
                    ins = o['fn'](eng)
                    if o['sig']:
                        ins.then_inc(sems[o['ck']], 16 if o['stream'] is not None else 1)
                if q == 'sp':
                    for k, v in final:
                        eng.wait_ge(sems[k], v)
            dec[q](body)


class Buf:
    def __init__(self, ap, keys):
        self.ap = ap
        self.keys = tuple(keys)


def _keys(items):
    out = []
    for it in items:
        if isinstance(it, Buf):
            out.extend(it.keys)
        elif isinstance(it, list):
            out.extend(_keys(it))
        elif isinstance(it, tuple) and len(it) == 2 and it[0] == 'ps':
            out.append(('psh', it[1], 0))
            out.append(('psh', it[1], 256))
        else:
            out.append(it)
    return tuple(out)


def build_program(L, TP, TS, TILE):
    nc = bass.Bass("TRN2", target_bir_lowering=False)
    S = Sched()

    def OP(q, fn, r=(), w=(), stream=None, cost=300):
        S.op(q, fn, _keys(r), _keys(w), stream, cost)

    def nel(ap):
        n = 1
        for d in ap.shape[1:]:
            n *= int(d)
        return n

    def din(name, shape):
        return nc.dram_tensor(name, list(shape), F32, kind="ExternalInput").ap()

    def dout(name, shape):
        return nc.dram_tensor(name, list(shape), F32, kind="ExternalOutput").ap()

    xp_d = din("xp", [KC, 128, TP]); xs_d = din("xs", [KC, 128, TS])
    pp_d = din("pp", [L, 2, 128, TP]); psm_d = din("psm", [L, 2, 128, TS])
    sts_d = din("st_shift", [128, L * NQ]); stp_d = din("st_pool", [128, L * 120]); stw_d = din("st_wkv", [L, 64, 1024])
    prm_d = din("prm", [L, 128, NPRM]); nfin_d = din("nfin", [128, KC]); gn_d = din("gn", [L, 2048]); cst_d = din("cst", [128, NCST])
    win_d = din("w_in_b", [L, NWIN, 128, 2048]); pw_d = din("pw_b", [L, 128, 2048]); lora_d = din("lora_b", [L, 2, 128, 2048])
    ppj_d = din("ppool_b", [L, 8, 128, 2048]); prw_d = din("prwkv_b", [L, 8, 128, 2048]); wo_d = din("wo_b", [L, 16, 128, 2048])
    wg_d = din("wg_b", [L, FC, 128, 2048]); wu_d = din("wu_b", [L, FC, 128, 2048]); wd_d = din("wd_b", [L, FG, 16, 128, FGC * 128])
    pg_d = din("pg_b", [L, 16, 128, 2048]); pj_d = din("pj_b", [L, 2, 128, 2048])
    yp_d = dout("yp", [KC, 128, TP]); ys_d = dout("ys", [KC, 128, TS])
    shp_d = dout("sh_p", [128, L * NQ]); shs_d = dout("sh_s", [128, L * NQ])
    pop_d = dout("po_p", [128, L * 120]); pos_d = dout("po_s", [128, L * 120])
    wkp_d = dout("wk_p", [L, 64, 1024]); wks_d = dout("wk_s", [L, 64, 1024])

    es = contextlib.ExitStack()
    with es:
        def sbt(name, shape, dt):
            return es.enter_context(nc.sbuf_tensor(name, list(shape), dt))

        TM = TILE
        X = sbt("X", [128, KC, TM], F32)
        H = sbt("H", [128, KC, TM], BF16)
        YA = sbt("YA", [128, 8, TM], BF16)
        YB = sbt("YB", [128, 8, TM], BF16)
        ST = sbt("ST", [64, NHEAD, 64], F32)
        POOLST = sbt("POOLST", [128, L, 8, 15], F32)
        SHIFT = sbt("SHIFT", [128, L, NQ], F32)
        WR = sbt("WR", [128, NSLOT, 2048], BF16)
        LA = sbt("LA", [128, 2048], BF16)
        LB = sbt("LB", [128, 2048], BF16)
        CST = sbt("CST", [128, NCST], F32)
        CB = sbt("CB", [128, 514], BF16)
        PRM = sbt("PRM", [128, NPRM], F32)
        NFIN = sbt("NFIN", [128, KC], F32)
        SCR = sbt("SCR", [128, SCR_KB * 256], F32)
        SCRf = SCR[:]
        SCRb = SCR[:].bitcast(BF16)
        PQ = [es.enter_context(nc.psum_tensor("PQ%d" % i, [128, 1024], F32)) for i in range(4)]
        PQb = [p[:].bitcast(BF16) for p in PQ]

        IDb = CB[:, 0:128]; PERMb = CB[:, 128:256]; ONESb = CB[:, 256:384]; BLK2b = CB[:, 384:512]; BO2b = CB[:, 512:514]
        PERMf = CST[:, K_PERM:K_PERM + 128]

        scr_top = [0]

        def salloc(free_shape, dt, parts=128):
            n = int(np.prod(free_shape))
            nbytes = n * (4 if dt == F32 else 2)
            nslab = (nbytes + 1023) // 1024
            off = scr_top[0]
            scr_top[0] += nslab
            assert scr_top[0] <= SCR_KB, "scratch overflow %d" % scr_top[0]
            if dt == F32:
                ap = SCRf[0:parts, off * 256: off * 256 + n]
            else:
                ap = SCRb[0:parts, off * 512: off * 512 + n]
            if len(free_shape) == 2:
                ap = ap.rearrange("p (a b) -> p a b", b=free_shape[1])
            elif len(free_shape) == 3:
                ap = ap.rearrange("p (a b c) -> p a b c", b=free_shape[1], c=free_shape[2])
            elif len(free_shape) == 4:
                ap = ap.rearrange("p (a b c d) -> p a b c d", b=free_shape[1], c=free_shape[2], d=free_shape[3])
            return Buf(ap, [('scr', off + i) for i in range(nslab)])

        def smark():
            return scr_top[0]

        def srelease(m):
            scr_top[0] = m

        bank_rr = [0]

        def bank():
            b = bank_rr[0] % 8
            bank_rr[0] += 1
            return b

        def bank_ap(b):
            return PQ[b // 2][:, (b % 2) * 512:(b % 2) * 512 + 512]

        def bank_apb(b):
            return PQb[b // 2][:, (b % 2) * 1024:(b % 2) * 1024 + 1024]

        pair_rr = [0]

        def bpair():
            p = pair_rr[0] % 4
            pair_rr[0] += 1
            return p

        wr_ctr = [0]

        wr_busy = [False] * NSLOT

        def wfree(s):
            wr_busy[s] = False

        def wload(src_ap):
            n = src_ap.shape[-1]
            s = wr_ctr[0] % NSLOT
            wr_ctr[0] += 1
            assert not wr_busy[s], "weight ring slot still pinned"
            wr_busy[s] = True
            OP('pool', lambda e, s=s, src_ap=src_ap, n=n: e.dma_start(out=WR[:, s, 0:n], in_=src_ap), w=[('w', s)], stream='w%d' % s)
            return s

        sp_ctr = [0]

        def spdma(out, in_, r=(), w=()):
            st = 'sp%d' % (sp_ctr[0] % 4)
            sp_ctr[0] += 1
            OP('sp', lambda e, out=out, in_=in_: e.dma_start(out=out, in_=in_), r=r, w=w, stream=st)

        def pooldma(out, in_, r=(), w=()):
            OP('pool', lambda e, out=out, in_=in_: e.dma_start(out=out, in_=in_), r=r, w=w, stream='pm')

        def MM(out, lhsT, rhs, start, stop, r, w):
            OP('pe', lambda e, out=out, lhsT=lhsT, rhs=rhs, start=start, stop=stop: e.matmul(out, lhsT=lhsT, rhs=rhs, start=start, stop=stop), r=r, w=w,
               cost=40 + 0.45 * nel(rhs))

        def TR(out, in_, ident, r, w):
            OP('pe', lambda e, out=out, in_=in_, ident=ident: e.transpose(out, in_, ident), r=r, w=w, cost=75)

        def ACT(out, in_, func, r, w, bias=0.0, scale=1.0):
            OP('act', lambda e, out=out, in_=in_, func=func, bias=bias, scale=scale: e.activation(out=out, in_=in_, func=func, bias=bias, scale=scale), r=r, w=w,
               cost=230 + 0.85 * nel(out))

        def TT(out, in0, in1, op, r, w, q='dve'):
            OP(q, lambda e, out=out, in0=in0, in1=in1, op=op: e.tensor_tensor(out=out, in0=in0, in1=in1, op=op), r=r, w=w, cost=130 + 1.05 * nel(out))

        def TS_(out, in0, s1, s2, op0, op1, r, w):
            OP('dve', lambda e, out=out, in0=in0, s1=s1, s2=s2, op0=op0, op1=op1: e.tensor_scalar(out=out, in0=in0, scalar1=s1, scalar2=s2, op0=op0, op1=op1), r=r, w=w,
               cost=130 + 1.05 * nel(out))

        def STT(out, in0, sc, in1, op0, op1, r, w):
            OP('dve', lambda e, out=out, in0=in0, sc=sc, in1=in1, op0=op0, op1=op1: e.scalar_tensor_tensor(out=out, in0=in0, scalar=sc, in1=in1, op0=op0, op1=op1), r=r, w=w,
               cost=130 + 1.05 * nel(out))

        def CP(out, in_, r, w, q='dve'):
            if q == 'act':
                ACT(out, in_, AF.Copy, r, w)
            else:
                OP('dve', lambda e, out=out, in_=in_: e.tensor_copy(out=out, in_=in_), r=r, w=w, cost=130 + 1.05 * nel(out))

        def MSET(ap, val, w):
            OP('dve', lambda e, ap=ap, val=val: e.memset(ap, val), w=w)

        spdma(CST[:], cst_d, w=['cst'])
        spdma(NFIN[:], nfin_d, w=['nfin'])
        CP(CB[:, 0:514], CST[:, 0:514], r=['cst'], w=['cb'])
        MASKG = CST[0:64, K_MG:K_MG + 256]
        MASKX = CST[0:64, K_MX:K_MX + 64]
        I64f = CST[0:64, K_ID:K_ID + 64]
        RC = CST[:, K_RC:K_RC + 64].rearrange("p (g t) -> p g t", t=16)

        def rmsnorm(T, gsrc, gkey, dst_fn, dst_key_fn):
            m = smark()
            sqb = [salloc([T], BF16) for _ in range(4)]
            rt = salloc([T], F32)
            rstd = salloc([T], F32)
            b = bank()
            pk = ('ps', b)
            for kc in range(KC):
                sq = sqb[kc % 4]
                ACT(sq.ap, X[:, kc, 0:T], AF.Square, r=[('x', kc)], w=[sq])
                MM(bank_ap(b)[:, 0:T], ONESb, sq.ap, kc == 0, kc == KC - 1, r=[sq, 'cb'], w=[pk])
            ACT(rt.ap, bank_ap(b)[:, 0:T], AF.Sqrt, r=[pk], w=[rt], bias=NORM_EPS, scale=1.0 / D)
            OP('dve', lambda e: e.reciprocal(out=rstd.ap, in_=rt.ap), r=_keys([rt]), w=_keys([rstd]))
            for kc in range(KC):
                STT(dst_fn(kc), X[:, kc, 0:T], gsrc[:, kc:kc + 1], rstd.ap, ALU.mult, ALU.mult,
                    r=[('x', kc), gkey, rstd], w=list(dst_key_fn(kc)))
            srelease(m)

        def proj_block(T, blk_ap, n_k, rhs_fn, rhs_keys_fn, lhs_off=0, m=128):
            s = wload(blk_ap)
            b = bank()
            for kc in range(n_k):
                MM(bank_ap(b)[0:m, 0:T], WR[:, s, lhs_off + kc * 128: lhs_off + kc * 128 + m], rhs_fn(kc), kc == 0, kc == n_k - 1,
                   r=[('w', s)] + list(rhs_keys_fn(kc)), w=[('ps', b)])
            wfree(s)
            return b

        def layer_tile(l, T, first, is_prompt_first, wk_src, wk_dst, p_src):
            nch = T // 64
            W = 15 + T
            spdma(PRM[:], prm_d[l], w=['prm'])
            pooldma(LA[:], lora_d[l, 0], w=['la'])
            pooldma(LB[:], lora_d[l, 1], w=['lb'])
            if is_prompt_first:
                MSET(ST[:], 0.0, w=['st'])
            else:
                spdma(ST[:].rearrange("p h v -> p (h v)"), wk_src[l], r=[('wk', l)] if wk_src is wkp_d else [], w=['st'])

            def pcol(c):
                return PRM[:, c:c + 1]

            rmsnorm(T, PRM[:, C_NM:C_NM + KC], 'prm', lambda kc: H[:, kc, 0:T], lambda kc: [('h', kc)])
            hr = lambda kc: H[:, kc, 0:T]
            hk = lambda kc: [('h', kc)]

            mB = smark()
            tw = salloc([T], BF16); ad = salloc([T], BF16); sg1 = salloc([T], BF16); sg2 = salloc([T], BF16)
            gnt = salloc([2, 256], F32, parts=64)
            MSET(tw.ap[64:128, :], 0.0, w=[tw])
            MSET(ad.ap[64:128, :], 0.0, w=[ad])
            RWT = {}

            def rw_alloc():
                rwb_all = salloc([3, T + 1], F32)
                RWT['rwb'] = [Buf(rwb_all.ap[:, i, :], rwb_all.keys) for i in range(3)]
                RWT['xm'] = [salloc([T], F32) for _ in range(3)]
                RWT['tmpA'] = salloc([T], F32)

            def rw_chunk(q, slot):
                rwb, xm, tmpA = RWT['rwb'], RWT['xm'], RWT['tmpA']
                b = proj_block(T, win_d[l, 8 + q], KC, hr, hk)
                rb = rwb[slot]
                ACT(rb.ap[:, 1:T + 1], bank_ap(b)[:, 0:T], AF.Copy, r=[('ps', b)], w=[rb])
                CP(rb.ap[:, 0:1], SHIFT[:, l, q:q + 1], r=['shift'], w=[rb])
                TT(tmpA.ap, rb.ap[:, 0:T], rb.ap[:, 1:T + 1], ALU.subtract, r=[rb], w=[tmpA])
                STT(xm[slot].ap, tmpA.ap, pcol(C_MU + q), rb.ap[:, 1:T + 1], ALU.mult, ALU.add, r=[tmpA, rb, 'prm'], w=[xm[slot]])
                CP(SHIFT[:, l, q:q + 1], rb.ap[:, T:T + 1], r=[rb], w=['shift'])

            mL = smark()
            rw_alloc()
            xm = RWT['xm']
            rw_chunk(0, 0)
            ACT(tw.ap[0:64, :], xm[0].ap[0:64, :], AF.Tanh, r=[xm[0]], w=[tw])
            rw_chunk(1, 1)
            CP(ad.ap[0:64, :], xm[1].ap[0:64, :], r=[xm[1]], w=[ad], q='act')
            rw_chunk(2, 2)
            ACT(sg1.ap, xm[2].ap, AF.Sigmoid, r=[xm[2]], w=[sg1])
            rw_chunk(3, 0)
            ACT(sg2.ap, xm[0].ap, AF.Sigmoid, r=[xm[0]], w=[sg2])
            srelease(mL)

            for qd in range(4):
                mQ = smark()
                bank_rr[0] = 0
                pair_rr[0] = 0
                S.begin_capture()
                spdma(gnt.ap[:, 0, :], gn_d[l:l + 1, qd * 256:(qd + 1) * 256].partition_broadcast(64), w=[gnt])
                spdma(gnt.ap[:, 1, :], gn_d[l:l + 1, 1024 + qd * 256:1024 + (qd + 1) * 256].partition_broadcast(64), w=[gnt])
                ARs = salloc([2, nch, 128], BF16); ARo = salloc([2, nch, 128], BF16)
                BKs = salloc([2, nch, 128], BF16); BKo = salloc([2, nch, 128], BF16)
                BKc = salloc([2, nch, 128], BF16)
                VB = salloc([2, T], BF16); RK = salloc([2, T], BF16)
                PCs = salloc([2, 8], F32); PCH = salloc([nch, 4], F32, parts=64)
                MSET(PCs.ap, 0.0, w=[PCs])
                mP = smark()
                rw_alloc()
                xm = RWT['xm']
                t = [salloc([T], F32) for _ in range(11)]
                sqk = salloc([T], BF16)
                for jj in range(2):
                    j = qd * 2 + jj
                    for i3 in range(3):
                        rw_chunk(4 + 3 * j + i3, i3)
                    XR, XK, XV = xm[0], xm[1], xm[2]
                    sigz, cum, exc, pinc, pinv, pexc, av, kkr, kk, kh, tb = t
                    b = bank()
                    MM(bank_ap(b)[:, 0:T], LA[:, j * 128:(j + 1) * 128], tw.ap, True, True, r=['la', tw], w=[('ps', b)])
                    ACT(sigz.ap, bank_ap(b)[:, 0:T], AF.Sigmoid, r=[('ps', b), 'prm'], w=[sigz], bias=pcol(C_W0 + j))
                    OP('dve', lambda e, cum=cum, sigz=sigz: e.tensor_tensor_scan(out=cum.ap, data0=CST[:, K_RST:K_RST + T], data1=sigz.ap,
                                                                                 initial=0.0, op0=ALU.mult, op1=ALU.add),
                       r=_keys([sigz, 'cst']), w=_keys([cum]))
                    TT(exc.ap, cum.ap, sigz.ap, ALU.subtract, r=[cum, sigz], w=[exc])
                    ACT(pinc.ap, cum.ap, AF.Exp, r=[cum], w=[pinc], scale=-C0)
                    ACT(pinv.ap, cum.ap, AF.Exp, r=[cum], w=[pinv], scale=C0)
                    ACT(pexc.ap, exc.ap, AF.Exp, r=[exc], w=[pexc], scale=-C0)
                    CP(PCs.ap[:, jj, 0:nch], pinc.ap.rearrange("p (c t) -> p c t", t=64)[:, :, 63], r=[pinc], w=[PCs])
                    b = bank()
                    MM(bank_ap(b)[:, 0:T], LA[:, 1024 + j * 128:1024 + (j + 1) * 128], ad.ap, True, True, r=['la', ad], w=[('ps', b)])
                    ACT(av.ap, bank_ap(b)[:, 0:T], AF.Sigmoid, r=[('ps', b), 'prm'], w=[av], bias=pcol(C_A0 + j))
                    ACT(sqk.ap, XK.ap, AF.Square, r=[XK, 'prm'], w=[sqk], scale=pcol(C_KK + j))
                    b = bank()
                    MM(bank_ap(b)[:, 0:T], BLK2b, sqk.ap, True, True, r=['cb', sqk], w=[('ps', b)])
                    ACT(exc.ap, bank_ap(b)[:, 0:T], AF.Sqrt, r=[('ps', b)], w=[exc])
                    TS_(exc.ap, exc.ap, 1e-12, None, ALU.max, ALU.bypass, r=[exc], w=[exc])
                    OP('dve', lambda e, exc=exc: e.reciprocal(out=exc.ap, in_=exc.ap), r=_keys([exc]), w=_keys([exc]))
                    STT(kk.ap, XK.ap, pcol(C_KK + j), exc.ap, ALU.mult, ALU.mult, r=[XK, exc, 'prm'], w=[kk])
                    TS_(tb.ap, av.ap, 1.0, pcol(C_KA + j), ALU.subtract, ALU.mult, r=[av, 'prm'], w=[tb])
                    STT(kh.ap, tb.ap, 1.0, XK.ap, ALU.add, ALU.mult, r=[tb, XK], w=[kh])
                    c3 = lambda ap: ap.rearrange("p (c t) -> p c t", t=64)
                    STT(ARs.ap[:, jj, :, 0:64], c3(kk.ap), -1.0, c3(pexc.ap), ALU.mult, ALU.mult, r=[kk, pexc], w=[ARs])
                    TT(ARs.ap[:, jj, :, 64:128], c3(XR.ap), c3(pinc.ap), ALU.mult, r=[XR, pinc], w=[ARs])
                    STT(RK.ap[:, jj, :], XR.ap, pcol(C_RK + j), kh.ap, ALU.mult, ALU.mult, r=[XR, kh, 'prm'], w=[RK])
                    CP(VB.ap[:, jj, :], XV.ap, r=[XV], w=[VB], q='act')
                    TT(tb.ap, kk.ap, av.ap, ALU.mult, r=[kk, av], w=[tb])
                    TT(tb.ap, tb.ap, pinv.ap, ALU.mult, r=[tb, pinv], w=[tb])
                    CP(BKs.ap[:, jj, :, 0:64], c3(tb.ap), r=[tb], w=[BKs], q='act')
                    pcb = PCs.ap[:, jj, 0:nch].unsqueeze(2).to_broadcast([128, nch, 64])
                    TT(BKc.ap[:, jj, :, 0:64], c3(tb.ap), pcb, ALU.mult, r=[tb, PCs], w=[BKc])
                    TT(kh.ap, kh.ap, pinv.ap, ALU.mult, r=[kh, pinv], w=[kh])
                    CP(BKs.ap[:, jj, :, 64:128], c3(kh.ap), r=[kh], w=[BKs], q='act')
                    TT(BKc.ap[:, jj, :, 64:128], c3(kh.ap), pcb, ALU.mult, r=[kh, PCs], w=[BKc])
                    for (src, dst) in ((ARs, ARo), (BKs, BKo)):
                        p2 = bpair()
                        n = nch * 128
                        for h0 in range(0, n, 512):
                            hn = min(512, n - h0)
                            MM(PQ[p2][:, h0:h0 + hn], PERMb, src.ap[:, jj, :, :].rearrange("p c t -> p (c t)")[:, h0:h0 + hn], True, True,
                               r=['cb', src], w=[('ps', 2 * p2), ('ps', 2 * p2 + 1)])
                        CP(dst.ap[0:64, jj, :, :].rearrange("p c t -> p (c t)"), PQ[p2][0:64, 0:n], r=[('ps', 2 * p2), ('ps', 2 * p2 + 1)], w=[dst], q='act')
                    b = bank()
                    MM(bank_ap(b)[:, 0:8], PERMf, PCs.ap[:, jj, :], True, True, r=['cst', PCs], w=[('ps', b)])
                    CP(PCH.ap[:, :, 2 * jj], PCs.ap[0:64, jj, 0:nch], r=[PCs], w=[PCH])
                    CP(PCH.ap[:, :, 2 * jj + 1], bank_ap(b)[0:64, 0:nch], r=[('ps', b)], w=[PCH])
                srelease(mP)
                NSL = 3
                TOKs = [salloc([2, 3, 128], BF16, parts=64) for _ in range(NSL)]
                GTs = [salloc([4, 4, 64], BF16, parts=64) for _ in range(NSL)]
                XMT = [[salloc([4, 3, 64], BF16, parts=64) for _ in range(2)] for _ in range(NSL)]
                Wb = salloc([4, 64], BF16, parts=64); Ub = salloc([4, 64], BF16, parts=64); STb = salloc([4, 64], BF16, parts=64)
                ysq = salloc([4, 64], F32, parts=64); yn = salloc([4, 64], F32, parts=64); bon = salloc([4, 64], F32, parts=64)
                stm = salloc([4, 64], F32, parts=64)
                yout = salloc([256], BF16)
                MSET(yout.ap[64:128, :], 0.0, w=[yout])
                ysb = salloc([4, 64], F32, parts=64)
                sm = salloc([8, 4], F32, parts=64)
                STq = ST[:, qd * 4:(qd + 1) * 4, :]
                CP(STb.ap[0:64], STq, r=['st'], w=[STb])
                A_O1 = [0, 2]
                A_O2 = [(1, 0), (3, 0)]

                def ARh(hq, c):
                    jj, hp = hq // 2, hq % 2
                    return (ARs if hp == 0 else ARo).ap[0:64, jj, c, :], (ARs if hp == 0 else ARo)

                def BKh(hq, c):
                    jj, hp = hq // 2, hq % 2
                    return (BKs if hp == 0 else BKo).ap[0:64, jj, c, :], (BKs if hp == 0 else BKo)

                a_done = [False] * nch; b1_done = [False] * nch; b2_done = [False] * nch; ycp = [False] * nch

                def partA(c, pset):
                    if F_NOA:
                        a_done[c] = True
                        return
                    sl = c % NSL
                    TOKc, GT, XM = TOKs[sl], GTs[sl], XMT[sl]
                    cs = slice(c * 64, (c + 1) * 64)
                    b1 = A_O1[pset]; k1 = ('ps', b1)
                    b2, c2 = A_O2[pset]; k2 = ('ps', b2)
                    btv = 4
                    ktv = ('ps', btv)
                    tv = bank_apb(btv)[0:64, 0:768].rearrange("p (j k c) -> p j k c", k=3, c=128)
                    for jj in range(2):
                        TR(tv[:, jj, 0, :], VB.ap[:, jj, cs], IDb, r=[VB, 'cb'], w=[ktv])
                        TR(tv[:, jj, 1, :], BKc.ap[:, jj, c, 0:64], IDb, r=[BKc, 'cb'], w=[ktv])
                        TR(tv[:, jj, 2, :], BKc.ap[:, jj, c, 64:128], IDb, r=[BKc, 'cb'], w=[ktv])
                    xv = bank_ap(b2)[0:64, c2:c2 + 256].rearrange("p (h s) -> p h s", s=64)
                    for hq in range(4):
                        ar, arb = ARh(hq, c)
                        bk, bkb = BKh(hq, c)
                        MM(xv[:, hq, :], ar[:, 0:64], bk[:, 0:64], True, True, r=[arb, bkb], w=[k2])
                    CP(TOKc.ap[0:64], tv, r=[ktv], w=[TOKc], q='act')
                    yield
                    Cc = XM[0]
                    TT(Cc.ap[0:64, :, 0, :], xv, MASKX.unsqueeze(1).to_broadcast([64, 4, 64]), ALU.mult, r=[k2, 'cst'], w=[Cc])
                    gv = bank_ap(b1)[0:64, 0:512].rearrange("p (h x) -> p h x", x=256)
                    for half in range(2):
                        for h2 in range(2):
                            hq = half * 2 + h2
                            ar, arb = ARh(hq, c)
                            bk, bkb = BKh(hq, c)
                            MM(gv[:, h2, 0:128], bk[:, 0:64], ar, True, True, r=[arb, bkb], w=[k1])
                            MM(gv[:, h2, 128:256], bk[:, 64:128], ar, True, True, r=[arb, bkb], w=[k1])
                        yield
                        TT(GT.ap[0:64, half * 2:half * 2 + 2].rearrange("p h b t -> p h (b t)"), gv, MASKG.unsqueeze(1).to_broadcast([64, 2, 256]), ALU.mult,
                           r=[k1, 'cst'], w=[GT])
                    CP(Cc.ap[0:64, :, 1, :], GT.ap[0:64, :, 0, :], r=[GT], w=[Cc], q='act')
                    CP(Cc.ap[0:64, :, 2, :], I64f.unsqueeze(1).to_broadcast([64, 4, 64]), r=['cst'], w=[Cc])
                    yield
                    pv = PQ[b1 // 2][0:64, :].rearrange("p (h x) -> p h x", x=256)
                    kk2 = [k1, k2]
                    for lev in range(6):
                        Cn = XM[(lev + 1) % 2]
                        last = lev == 5
                        for hq in range(4):
                            if last:
                                MM(pv[:, hq, 128:192], Cc.ap[0:64, hq, 0, :], Cc.ap[0:64, hq, 2, :], True, True, r=[Cc], w=kk2)
                            else:
                                MM(pv[:, hq, 64:192], Cc.ap[0:64, hq, 0, :], Cc.ap[0:64, hq, 1:3, :].rearrange("p a t -> p (a t)"), True, True, r=[Cc], w=kk2)
                                MM(pv[:, hq, 0:64], Cc.ap[0:64, hq, 1, :], Cc.ap[0:64, hq, 0, :], True, True, r=[Cc], w=kk2)
                        yield
                        TT(Cn.ap[0:64, :, 2, :], pv[:, :, 128:192], Cc.ap[0:64, :, 2, :], ALU.add, r=kk2 + [Cc], w=[Cn])
                        if not last:
                            CP(Cn.ap[0:64, :, 0:2, :].rearrange("p h a t -> p h (a t)"), pv[:, :, 0:128], r=kk2, w=[Cn], q='act')
                        Cc = Cn
                        yield
                    assert Cc is XM[0]
                    a_done[c] = True

                def chainB1():
                    for c in range(nch):
                        while not (a_done[c] and (c < 1 or ycp[c - 1])):
                            yield
                        if F_NOB1:
                            b1_done[c] = True
                            yield
                            continue
                        sl = c % NSL
                        TOKc, GT, TTf = TOKs[sl], GTs[sl], XMT[sl][0]
                        vtok = lambda hq: TOKc.ap[0:64, hq // 2, 0, (hq % 2) * 64:(hq % 2) * 64 + 64]
                        bctok = lambda hq: TOKc.ap[0:64, hq // 2, 1, (hq % 2) * 64:(hq % 2) * 64 + 64]
                        kctok = lambda hq: TOKc.ap[0:64, hq // 2, 2, (hq % 2) * 64:(hq % 2) * 64 + 64]
                        wk_ = ('ps', 5)
                        wv = bank_ap(5)[0:64, 0:256].rearrange("p (h v) -> p h v", v=64)
                        for hq in range(4):
                            ar, arb = ARh(hq, c)
                            MM(wv[:, hq, :], ar[:, 0:64], STb.ap[0:64, hq, :], True, False, r=[arb, STb], w=[wk_])
                            MM(wv[:, hq, :], GT.ap[0:64, hq, 2, :], vtok(hq), False, True, r=[GT, TOKc], w=[wk_])
                        TT(stm.ap[0:64], STq, PCH.ap[0:64, c, :].unsqueeze(2).to_broadcast([64, 4, 64]), ALU.mult, r=['st', PCH], w=[stm])
                        yield
                        CP(Wb.ap[0:64], wv, r=[wk_], w=[Wb], q='act')
                        yield
                        uk = ('ps', 5)
                        uv = bank_ap(5)[0:64, 0:256].rearrange("p (h v) -> p h v", v=64)
                        for hq in range(4):
                            MM(uv[:, hq, :], TTf.ap[0:64, hq, 2, :], Wb.ap[0:64, hq, :], True, True, r=[TTf, Wb], w=[uk])
                        yield
                        CP(Ub.ap[0:64], uv, r=[uk], w=[Ub])
                        yield
                        sk = ('ps', 5)
                        sv = bank_ap(5)[0:64, 0:256].rearrange("p (h v) -> p h v", v=64)
                        for hq in range(4):
                            MM(sv[:, hq, :], bctok(hq), Ub.ap[0:64, hq, :], True, False, r=[TOKc, Ub], w=[sk])
                            MM(sv[:, hq, :], kctok(hq), vtok(hq), False, True, r=[TOKc], w=[sk])
                        yk = ('ps', 6)
                        yv = bank_ap(6)[0:64, 0:256].rearrange("p (h v) -> p h v", v=64)
                        for hq in range(4):
                            ar, arb = ARh(hq, c)
                            MM(yv[:, hq, :], ar[:, 64:128], STb.ap[0:64, hq, :], True, False, r=[arb, STb], w=[yk])
                            MM(yv[:, hq, :], GT.ap[0:64, hq, 1, :], Ub.ap[0:64, hq, :], False, False, r=[GT, Ub], w=[yk])
                            MM(yv[:, hq, :], GT.ap[0:64, hq, 3, :], vtok(hq), False, True, r=[GT, TOKc], w=[yk])
                        yield
                        TT(STq, stm.ap[0:64], sv, ALU.add, r=[stm, sk], w=['st'])
                        yield
                        CP(STb.ap[0:64], STq, r=['st'], w=[STb], q='act')
                        b1_done[c] = True
                        yield

                def chainB2():
                    for c in range(nch):
                        while not b1_done[c]:
                            yield
                        if F_NOB2:
                            b2_done[c] = True
                            yield
                            continue
                        sl = c % NSL
                        TOKc = TOKs[sl]
                        cs = slice(c * 64, (c + 1) * 64)
                        yk = ('ps', 6)
                        yv = bank_ap(6)[0:64, 0:256].rearrange("p (h v) -> p h v", v=64)
                        gk_ = ('ps', 7)
                        CP(ysb.ap[0:64], yv, r=[yk], w=[ysb], q='act')
                        ycp[c] = True
                        yv = ysb.ap[0:64]
                        yk = ysb
                        for jj in range(2):
                            MM(bank_ap(7)[0:64, 256 + 2 * jj:256 + 2 * jj + 2], RK.ap[:, jj, cs], BO2b, True, True, r=[RK, 'cb'], w=[gk_])
                        MM(bank_ap(7)[0:64, 0:256], sg1.ap[:, cs], LB[:, qd * 256:(qd + 1) * 256], True, False, r=[sg1, 'lb'], w=[gk_])
                        MM(bank_ap(7)[0:64, 0:256], sg2.ap[:, cs], LB[:, 1024 + qd * 256:1024 + (qd + 1) * 256], False, True, r=[sg2, 'lb'], w=[gk_])
                        s1 = sm.ap[0:64, 0, :]; s2 = sm.ap[0:64, 1, :]; mean = sm.ap[0:64, 2, :]; msq = sm.ap[0:64, 3, :]
                        var = sm.ap[0:64, 4, :]; rs = sm.ap[0:64, 5, :]; bsc = sm.ap[0:64, 6, :]
                        OP('dve', lambda e, s1=s1, yv=yv: e.tensor_reduce(out=s1, in_=yv, axis=AX.X, op=ALU.add), r=_keys([yk]), w=_keys([sm]))
                        ACT(ysq.ap[0:64], yv, AF.Square, r=[yk], w=[ysq])
                        yield
                        OP('dve', lambda e, s2=s2, ysq=ysq: e.tensor_reduce(out=s2, in_=ysq.ap[0:64], axis=AX.X, op=ALU.add), r=_keys([ysq]), w=_keys([sm]))
                        TS_(mean, s1, 1.0 / 64, None, ALU.mult, ALU.bypass, r=[sm], w=[sm])
                        yield
                        TT(msq, mean, mean, ALU.mult, r=[sm], w=[sm])
                        yield
                        STT(var, s2, 1.0 / 64, msq, ALU.mult, ALU.subtract, r=[sm], w=[sm])
                        yield
                        ACT(var, var, AF.Sqrt, r=[sm], w=[sm], bias=GN_EPS)
                        CP(bsc, bank_ap(7)[0:64, 256:260], r=[gk_], w=[sm], q='act')
                        b3 = lambda ap: ap.unsqueeze(2).to_broadcast([64, 4, 64])
                        TT(yn.ap[0:64], yv, b3(mean), ALU.subtract, r=[yk, sm], w=[yn])
                        yield
                        OP('dve', lambda e, rs=rs, var=var: e.reciprocal(out=rs, in_=var), r=_keys([sm]), w=_keys([sm]))
                        yield
                        TT(yn.ap[0:64], yn.ap[0:64], b3(rs), ALU.mult, r=[yn, sm], w=[yn])
                        gq = gnt.ap[0:64, 0, :].rearrange("p (h v) -> p h v", v=64)
                        bq = gnt.ap[0:64, 1, :].rearrange("p (h v) -> p h v", v=64)
                        vt4 = TOKc.ap[0:64, :, 0, :].rearrange("p j (h v) -> p j h v", v=64)
                        bsc4 = bsc.rearrange("p (j h) -> p j h", h=2).unsqueeze(3).to_broadcast([64, 2, 2, 64])
                        TT(bon.ap[0:64].rearrange("p (j h) v -> p j h v", h=2), vt4, bsc4, ALU.mult, r=[TOKc, sm], w=[bon])
                        yield
                        TT(yn.ap[0:64], yn.ap[0:64], gq, ALU.mult, r=[yn, gnt], w=[yn])
                        yield
                        TT(yn.ap[0:64], yn.ap[0:64], bq, ALU.add, r=[yn, gnt], w=[yn])
                        yield
                        TT(yn.ap[0:64], yn.ap[0:64], bon.ap[0:64], ALU.add, r=[yn, bon], w=[yn])
                        yield
                        TT(yout.ap[0:64], yn.ap[0:64].rearrange("p h v -> p (h v)"), bank_ap(7)[0:64, 0:256], ALU.mult, r=[yn, gk_], w=[yout])
                        yield
                        tk = gk_
                        for jj in range(2):
                            TR(bank_apb(7)[:, 528 + jj * 128:528 + (jj + 1) * 128], yout.ap[:, jj * 128:(jj + 1) * 128], IDb, r=[yout, 'cb'], w=[tk])
                        yield
                        CP(YB[:, qd * 2:qd * 2 + 2, cs], bank_apb(7)[:, 528:784].rearrange("p (j t) -> p j t", t=128)[:, :, 0:64], r=[tk], w=[('yb', qd)], q='act')
                        b2_done[c] = True
                        yield

                gB1 = chainB1(); gB2 = chainB2()
                for c in range(nch):
                    for _ in partA(c, c % 2):
                        pass
                    while not b1_done[c]:
                        next(gB1)
                    while not b2_done[c]:
                        next(gB2)
                S.end_capture()
                srelease(mQ)
            srelease(mB)
            spdma(wk_dst[l], ST[:].rearrange("p h v -> p (h v)"), r=['st'], w=[('wk', l)])

            mA = smark()
            Z = salloc([8, W], F32); DP = salloc([8, T], BF16)
            SA = salloc([W], F32); SB_ = salloc([W], F32); PW = salloc([2048], BF16); tfix = salloc([16], F32)
            pooldma(PW.ap, pw_d[l], w=[PW])
            for ch in range(8):
                b = proj_block(T, win_d[l, ch], KC, hr, hk)
                ACT(Z.ap[:, ch, 15:W], bank_ap(b)[:, 0:T], AF.Copy, r=[('ps', b)], w=[Z])
                CP(Z.ap[:, ch, 0:15], POOLST[:, l, ch, :], r=['poolst'], w=[Z])
                gi = ch // 2
                win = 2 << gi
                zc = Z.ap[:, ch, :]
                TT(SA.ap[:, 1:W], zc[:, 1:W], zc[:, 0:W - 1], ALU.add, r=[Z], w=[SA])
                fin = SA
                if win >= 4:
                    TT(SB_.ap[:, 3:W], SA.ap[:, 3:W], SA.ap[:, 1:W - 2], ALU.add, r=[SA], w=[SB_]); fin = SB_
                if win >= 8:
                    TT(SA.ap[:, 7:W], SB_.ap[:, 7:W], SB_.ap[:, 3:W - 4], ALU.add, r=[SB_], w=[SA]); fin = SA
                if win >= 16:
                    TT(SB_.ap[:, 15:W], SA.ap[:, 15:W], SA.ap[:, 7:W - 8], ALU.add, r=[SA], w=[SB_]); fin = SB_
                STT(DP.ap[:, ch, :], fin.ap[:, 15:W], 1.0 / win, zc[:, 15:W], ALU.mult, ALU.subtract, r=[fin, Z], w=[DP])
                if is_prompt_first:
                    TT(tfix.ap, fin.ap[:, 15:31], RC[:, gi, :], ALU.mult, r=[fin, 'cst'], w=[tfix])
                    TT(DP.ap[:, ch, 0:16], tfix.ap, zc[:, 15:31], ALU.subtract, r=[tfix, Z], w=[DP])
                CP(POOLST[:, l, ch, :], zc[:, T:T + 15], r=[Z], w=['poolst'])
            pwv = PW.ap.rearrange("p (g k o) -> p g k o", k=2, o=256)
            for g in range(4):
                for o2 in range(2):
                    b = bank()
                    for k2 in range(2):
                        MM(bank_ap(b)[:, 0:T], pwv[:, g, k2, o2 * 128:(o2 + 1) * 128], DP.ap[:, 2 * g + k2, :], k2 == 0, k2 == 1, r=[PW, DP], w=[('ps', b)])
                    ACT(YA[:, 2 * g + o2, 0:T], bank_ap(b)[:, 0:T], AF.Copy, r=[('ps', b), 'prm'], w=[('ya', 2 * g + o2)], scale=pcol(C_PSC + 2 * g + o2))
            srelease(mA)

            mC = smark()
            MG = salloc([KC, T], BF16)
            s1b = salloc([T], F32); s2b = salloc([T], F32); m1 = salloc([T], F32); m2 = salloc([T], F32)
            for oc in range(KC):
                half, sub = oc // 2, oc % 2
                if sub == 0:
                    spool = wload(ppj_d[l, half])
                    srw = wload(prw_d[l, half])
                bp1 = bank()
                for kc in range(8):
                    MM(bank_ap(bp1)[:, 0:T], WR[:, spool, sub * 1024 + kc * 128: sub * 1024 + (kc + 1) * 128], YA[:, kc, 0:T], kc == 0, kc == 7,
                       r=[('w', spool), ('ya', kc)], w=[('ps', bp1)])
                bp2 = bank()
                for kc in range(8):
                    MM(bank_ap(bp2)[:, 0:T], WR[:, srw, sub * 1024 + kc * 128: sub * 1024 + (kc + 1) * 128], YB[:, kc, 0:T], kc == 0, kc == 7,
                       r=[('w', srw), ('yb', kc // 2)], w=[('ps', bp2)])
                bg1 = proj_block(T, win_d[l, 36 + oc], KC, hr, hk)
                bg2 = proj_block(T, win_d[l, 52 + oc], KC, hr, hk)
                if sub == 1:
                    wfree(spool); wfree(srw)
                ACT(s1b.ap, bank_ap(bg1)[:, 0:T], AF.Sigmoid, r=[('ps', bg1)], w=[s1b])
                ACT(s2b.ap, bank_ap(bg2)[:, 0:T], AF.Sigmoid, r=[('ps', bg2)], w=[s2b])
                TT(m1.ap, s1b.ap, bank_ap(bp1)[:, 0:T], ALU.mult, r=[s1b, ('ps', bp1)], w=[m1])
                TT(m2.ap, s2b.ap, bank_ap(bp2)[:, 0:T], ALU.mult, r=[s2b, ('ps', bp2)], w=[m2])
                TT(MG.ap[:, oc, :], m1.ap, m2.ap, ALU.add, r=[m1, m2], w=[MG])
            for oc in range(KC):
                b = proj_block(T, wo_d[l, oc], KC, lambda kc: MG.ap[:, kc, :], lambda kc: [MG])
                TT(X[:, oc, 0:T], X[:, oc, 0:T], bank_ap(b)[:, 0:T], ALU.add, r=[('x', oc), ('ps', b)], w=[('x', oc)])
            srelease(mC)

            rmsnorm(T, PRM[:, C_NF:C_NF + KC], 'prm', lambda kc: H[:, kc, 0:T], lambda kc: [('h', kc)])
            mD = smark()
            ACTB = [salloc([FGC, T], BF16) for _ in range(2)]
            sgb = [salloc([T], F32) for _ in range(2)]
            for fg in range(FG):
                ab = ACTB[fg % 2]
                for fi in range(FGC):
                    fc = fg * FGC + fi
                    bg_ = proj_block(T, wg_d[l, fc], KC, hr, hk)
                    bu_ = proj_block(T, wu_d[l, fc], KC, hr, hk)
                    sg = sgb[fi % 2]
                    ACT(sg.ap, bank_ap(bg_)[:, 0:T], AF.Silu, r=[('ps', bg_)], w=[sg])
                    TT(ab.ap[:, fi, :], sg.ap, bank_ap(bu_)[:, 0:T], ALU.mult, r=[sg, ('ps', bu_)], w=[ab])
                for oc in range(KC):
                    b = proj_block(T, wd_d[l, fg, oc], FGC, lambda kc: ab.ap[:, kc, :], lambda kc: [ab])
                    TT(X[:, oc, 0:T], X[:, oc, 0:T], bank_ap(b)[:, 0:T], ALU.add, r=[('x', oc), ('ps', b)], w=[('x', oc)])
            srelease(mD)

            rmsnorm(T, PRM[:, C_NP:C_NP + KC], 'prm', lambda kc: H[:, kc, 0:T], lambda kc: [('h', kc)])
            mE = smark()
            PT = salloc([2, T], BF16); sp_ = salloc([T], F32); tp_ = salloc([T], F32)
            PJ = [salloc([2048], BF16) for _ in range(2)]
            pooldma(PT.ap, p_src(l), w=[PT])
            pooldma(PJ[0].ap, pj_d[l, 0], w=[PJ[0]])
            pooldma(PJ[1].ap, pj_d[l, 1], w=[PJ[1]])
            for oc in range(KC):
                pjb = PJ[oc // 8]
                bgp = proj_block(T, pg_d[l, oc], KC, hr, hk)
                bpp = bank()
                for k2 in range(2):
                    MM(bank_ap(bpp)[:, 0:T], pjb.ap[:, (oc % 8) * 256 + k2 * 128:(oc % 8) * 256 + (k2 + 1) * 128], PT.ap[:, k2, :], k2 == 0, k2 == 1,
                       r=[pjb, PT], w=[('ps', bpp)])
                ACT(sp_.ap, bank_ap(bgp)[:, 0:T], AF.Sigmoid, r=[('ps', bgp)], w=[sp_])
                TT(tp_.ap, sp_.ap, bank_ap(bpp)[:, 0:T], ALU.mult, r=[sp_, ('ps', bpp)], w=[tp_])
                TT(X[:, oc, 0:T], X[:, oc, 0:T], tp_.ap, ALU.add, r=[('x', oc), tp_], w=[('x', oc)])
            srelease(mE)

        def run_segment(kind, t0, T, first, last):
            xsrc = xp_d if kind == 'p' else xs_d
            ydst = yp_d if kind == 'p' else ys_d
            spdma(X[:, :, 0:T], xsrc[:, :, t0:t0 + T].rearrange("c p t -> p c t"), w=[('x', kc) for kc in range(KC)])
            if first:
                if kind == 'p':
                    MSET(POOLST[:], 0.0, w=['poolst'])
                    MSET(SHIFT[:], 0.0, w=['shift'])
                else:
                    spdma(POOLST[:].rearrange("p l c t -> p (l c t)"), stp_d, w=['poolst'])
                    spdma(SHIFT[:].rearrange("p l q -> p (l q)"), sts_d, w=['shift'])
            for l in range(L):
                if kind == 'p':
                    psrc = lambda l, t0=t0, T=T: pp_d[l, :, :, t0:t0 + T].rearrange("k p t -> p k t")
                    layer_tile(l, T, first, first, wkp_d, wkp_d, psrc)
                else:
                    psrc = lambda l: psm_d[l].rearrange("k p t -> p k t")
                    layer_tile(l, T, first, False, stw_d, wks_d, psrc)
            m = smark()
            YO = salloc([KC, T], F32)
            rmsnorm(T, NFIN[:], 'nfin', lambda kc: YO.ap[:, kc, :], lambda kc: list(YO.keys))
            spdma(ydst[:, :, t0:t0 + T].rearrange("c p t -> p c t"), YO.ap, r=[YO])
            srelease(m)
            if last:
                spdma(shp_d if kind == 'p' else shs_d, SHIFT[:].rearrange("p l q -> p (l q)"), r=['shift'])
                spdma(pop_d if kind == 'p' else pos_d, POOLST[:].rearrange("p l c t -> p (l c t)"), r=['poolst'])

        ntile = TP // TILE
        for ti in range(ntile):
            run_segment('p', ti * TILE, TILE, ti == 0, ti == ntile - 1)
        run_segment('s', 0, TS, True, True)

        cnt = S.analyze()
        sems = {k: es.enter_context(nc.semaphore("sem_%s_%s" % k)) for k in cnt}
        block = es.enter_context(nc.Block())
        S.emit(sems, block)
    return nc, len(S.ops)


def rw_cols():
    cols = []
    cols.append(list(range(3072, 3136)) + [-1] * 64)
    cols.append(list(range(3136, 3200)) + [-1] * 64)
    cols.append(list(range(3200, 3328)))
    cols.append(list(range(3328, 3360)) + [-1] * 96)
    for j in range(8):
        for base in (0, 1024, 2048):
            cols.append(list(range(base + 128 * j, base + 128 * (j + 1))))
    return np.array(cols, dtype=np.int64)


def make_consts():
    c = np.zeros((128, NCST), np.float32)
    c[:, K_ID:K_ID + 128] = np.eye(128)
    for m in range(128):
        c[(m + 64) % 128, K_PERM + m] = 1.0
    c[:, K_ONES:K_ONES + 128] = 1.0
    p = np.arange(128)
    c[:, K_BLK2:K_BLK2 + 128] = (p[:, None] // 64 == p[None, :] // 64)
    c[:, K_BO2:K_BO2 + 2] = (p[:, None] // 64 == np.arange(2)[None, :])
    s = np.arange(64)[:, None]; t = np.arange(64)[None, :]
    strict = (t > s).astype(np.float32); incl = (t >= s).astype(np.float32)
    c[0:64, K_MG:K_MG + 256] = np.concatenate([strict, incl, strict, incl], axis=1)
    c[0:64, K_MX:K_MX + 64] = (s > t)
    rst = np.ones(512, np.float32); rst[::64] = 0.0
    c[:, K_RST:K_RST + 512] = rst[None, :]
    for gi, win in enumerate((2, 4, 8, 16)):
        c[:, K_RC + gi * 16:K_RC + (gi + 1) * 16] = (1.0 / np.minimum(np.arange(16) + 1, win))[None, :]
    return c


def kblocks(w, kc_n):
    K, N = w.shape
    return np.ascontiguousarray(w.reshape(kc_n, 128, N // 128, 128).transpose(2, 1, 0, 3).reshape(N // 128, 128, kc_n * 128))


def layout_weights(inp, L):
    f = lambda k: np.asarray(inp[k], dtype=np.float32)
    cols = rw_cols()
    out = {}
    w_in = f('w_in')
    win_b = np.zeros((L, NWIN, 128, 2048), np.float32)
    lora_b = np.zeros((L, 2, 128, 2048), np.float32)
    prm = np.zeros((L, 128, NPRM), np.float32)
    mu = f('mu_shift')
    for l in range(L):
        wl = w_in[l]
        win_b[l, 0:8] = kblocks(wl[:, 0:1024], KC)
        rwm = np.zeros((D, NQ * 128), np.float32)
        cc = cols.reshape(-1)
        ok = cc >= 0
        rwm[:, ok] = wl[:, 1024 + cc[ok]]
        win_b[l, 8:36] = kblocks(rwm, KC)
        win_b[l, 36:52] = kblocks(wl[:, 1024 + 3360:1024 + 3360 + 2048], KC)
        win_b[l, 52:68] = kblocks(wl[:, 1024 + 3360 + 2048:], KC)
        lora_b[l, 0, 0:64, 0:1024] = f('w_decay_up')[l]
        lora_b[l, 0, 0:64, 1024:2048] = f('w_aaa_up')[l]
        lora_b[l, 1, :, 0:1024] = f('w_gate_up')[l][0:128]
        lora_b[l, 1, 0:32, 1024:2048] = f('w_gate_up')[l][128:160]
        pc = lambda v: v.reshape(-1, 128).T
        prm[l, :, C_NM:C_NM + 16] = pc(f('norm_mix')[l]); prm[l, :, C_NF:C_NF + 16] = pc(f('norm_ffn')[l]); prm[l, :, C_NP:C_NP + 16] = pc(f('norm_ple')[l])
        mum = np.zeros(NQ * 128, np.float32); mum[ok] = mu[l][cc[ok]]
        prm[l, :, C_MU:C_MU + NQ] = mum.reshape(NQ, 128).T
        prm[l, :, C_PSC:C_PSC + 8] = pc(f('pool_scale')[l]); prm[l, :, C_W0:C_W0 + 8] = pc(f('w0')[l]); prm[l, :, C_A0:C_A0 + 8] = pc(f('a0')[l])
        prm[l, :, C_KK:C_KK + 8] = pc(f('k_k')[l]); prm[l, :, C_KA:C_KA + 8] = pc(f('k_a')[l]); prm[l, :, C_RK:C_RK + 8] = pc(f('r_k')[l].reshape(-1))
    out['w_in_b'] = win_b; out['lora_b'] = lora_b; out['prm'] = prm
    pw = f('pool_w').reshape(L, 4, 2, 128, 256).transpose(0, 3, 1, 2, 4).reshape(L, 128, 2048)
    out['pw_b'] = np.ascontiguousarray(pw)

    def two_oc(w):
        r = np.stack([kblocks(w[l], 8) for l in range(L)])
        return np.ascontiguousarray(r.reshape(L, 8, 2, 128, 1024).transpose(0, 1, 3, 2, 4).reshape(L, 8, 128, 2048))
    out['ppool_b'] = two_oc(f('proj_pool')); out['prwkv_b'] = two_oc(f('proj_rwkv'))
    out['wo_b'] = np.stack([kblocks(f('w_out')[l], KC) for l in range(L)])
    out['wg_b'] = np.stack([kblocks(f('w_ffn_gate')[l], KC) for l in range(L)])
    out['wu_b'] = np.stack([kblocks(f('w_ffn_up')[l], KC) for l in range(L)])
    wd = f('w_ffn_down')
    out['wd_b'] = np.stack([np.stack([kblocks(wd[l][fg * FGC * 128:(fg + 1) * FGC * 128], FGC) for fg in range(FG)]) for l in range(L)])
    out['pg_b'] = np.stack([kblocks(f('w_ple_gate')[l], KC) for l in range(L)])
    pj = np.stack([kblocks(f('w_ple_proj')[l], 2) for l in range(L)])
    out['pj_b'] = np.ascontiguousarray(pj.reshape(L, 2, 8, 128, 256).transpose(0, 1, 3, 2, 4).reshape(L, 2, 128, 2048))
    out['nfin'] = np.ascontiguousarray(f('norm_final').reshape(16, 128).T)
    out['gn'] = np.ascontiguousarray(np.concatenate([f('gn_gain'), f('gn_bias')], axis=1))
    out['cst'] = make_consts()
    return out, cols


_CACHE = {}


def run(inputs, L, TP, TS, TILE, ncores):
    key = (L, TP, TS, TILE)
    if key not in _CACHE:
        _CACHE[key] = build_program(L, TP, TS, TILE)[0]
    nc = _CACHE[key]
    f = lambda k: np.asarray(inputs[k], dtype=np.float32)
    shared, cols = layout_weights(inputs, L)
    cc = cols.reshape(-1); ok = cc >= 0
    in_maps = []
    for b in range(ncores):
        m = dict(shared)
        m['xp'] = np.ascontiguousarray(f('x_prompt')[b].T.reshape(KC, 128, TP))
        m['xs'] = np.ascontiguousarray(f('x_sample')[b].T.reshape(KC, 128, TS))
        m['pp'] = np.ascontiguousarray(f('p_prompt')[:, b].transpose(0, 2, 1).reshape(L, 2, 128, TP))
        m['psm'] = np.ascontiguousarray(f('p_sample')[:, b].transpose(0, 2, 1).reshape(L, 2, 128, TS))
        ss = np.zeros((L, NQ * 128), np.float32); ss[:, ok] = f('state_shift')[:, b][:, cc[ok]]
        m['st_shift'] = np.ascontiguousarray(ss.reshape(L, NQ, 128).transpose(2, 0, 1).reshape(128, L * NQ))
        sp = f('state_pool')[:, b]
        m['st_pool'] = np.ascontiguousarray(sp.reshape(L, 15, 8, 128).transpose(3, 0, 2, 1).reshape(128, L * 120))
        sw = f('state_wkv')[:, b]
        m['st_wkv'] = np.ascontiguousarray(sw.transpose(0, 3, 1, 2).reshape(L, 64, 1024))
        in_maps.append(m)
    res = run_bass_kernel_spmd(nc, in_maps, core_ids=list(range(ncores)))
    R = res.results
    inv = np.full(3360, -1, np.int64); inv[cc[ok]] = np.nonzero(ok)[0]

    def unshift(a):
        v = a.reshape(128, L, NQ).transpose(1, 2, 0).reshape(L, NQ * 128)
        return v[:, inv]
    y_p = np.stack([R[b]['yp'].reshape(D, TP).T for b in range(ncores)])
    y_s = np.stack([R[b]['ys'].reshape(D, TS).T for b in range(ncores)])
    outs = [y_p, y_s]
    for sfx in ('p', 's'):
        sh = np.stack([unshift(R[b]['sh_' + sfx]) for b in range(ncores)], axis=1)
        po = np.stack([R[b]['po_' + sfx].reshape(128, L, 8, 15).transpose(1, 3, 2, 0).reshape(L, 15, 1024) for b in range(ncores)], axis=1)
        wk = np.stack([R[b]['wk_' + sfx].reshape(L, 64, 16, 64).transpose(0, 2, 3, 1) for b in range(ncores)], axis=1)
        outs += [sh, po, wk]
    return tuple(np.ascontiguousarray(o.astype(np.float32)) for o in outs)


def kernel(**inputs):
    return run(inputs, 4, 2048, 64, 512, 8)
```
